# Optimizing a Trainium2 kernel written in Bass

```python
import math
import jax, jax.numpy as jnp
from jax import lax
import numpy as np

D_MODEL = 1024
BATCH = 2
SEQ = 8192
DEPTH = 1

N_META = 16
BLOCK = 128
D_FF = 2816
EPS = 1e-6
NEG_INF = -1e30
MLA_HEADS = 8
MLA_Q_RANK = 256
MLA_KV_RANK = 128
MLA_NOPE_DIM = 64
MLA_ROPE_DIM = 32
MLA_QK_DIM = MLA_NOPE_DIM + MLA_ROPE_DIM
MLA_V_DIM = 64
ROPE_THETA = 10000.0
FOX_HEADS = 8
FOX_HEAD_DIM = 64
D_MIX = MLA_HEADS * MLA_V_DIM + FOX_HEADS * FOX_HEAD_DIM
IN_SPLITS = (MLA_Q_RANK, MLA_KV_RANK, MLA_ROPE_DIM, 3 * FOX_HEADS * FOX_HEAD_DIM, FOX_HEADS)
D_IN = sum(IN_SPLITS)

kernel_name = "hymba_mla_fox_macaron"


def rms_norm(x, g):
    xf = x.astype(jnp.float32)
    y = xf * lax.rsqrt(jnp.mean(xf * xf, axis=-1, keepdims=True) + EPS)
    return (y * g.astype(jnp.float32)).astype(x.dtype)


def swiglu(x, w_gate, w_up, w_down):
    return (jax.nn.silu(x @ w_gate) * (x @ w_up)) @ w_down


def rope(x, pos):
    half = x.shape[-1] // 2
    inv_freq = 1.0 / (ROPE_THETA ** (jnp.arange(half, dtype=jnp.float32) / half))
    ang = pos.astype(jnp.float32)[:, None] * inv_freq[None, :]
    cos = jnp.cos(ang)[None, :, None, :].astype(x.dtype)
    sin = jnp.sin(ang)[None, :, None, :].astype(x.dtype)
    x1, x2 = x[..., :half], x[..., half:]
    return jnp.concatenate([x1 * cos - x2 * sin, x2 * cos + x1 * sin], axis=-1)


def block_causal_attention(q, k, v, scale, key_valid, cum=None):
    L = q.shape[1]
    outs = []
    for i in range(L // BLOCK):
        q0, q1 = i * BLOCK, (i + 1) * BLOCK
        s = jnp.einsum('bqhd,bkhd->bhqk', q[:, q0:q1], k[:, :q1],
                       preferred_element_type=jnp.float32) * scale
        if cum is not None:
            cq = jnp.transpose(cum[:, q0:q1], (0, 2, 1))[:, :, :, None]
            ck = jnp.transpose(cum[:, :q1], (0, 2, 1))[:, :, None, :]
            s = s + (cq - ck)
        q_pos = q0 + jnp.arange(BLOCK)
        k_pos = jnp.arange(q1)
        mask = (k_pos[None, :] <= q_pos[:, None]) & key_valid[None, :q1]
        s = jnp.where(mask[None, None], s, NEG_INF)
        p = jax.nn.softmax(s, axis=-1).astype(v.dtype)
        outs.append(jnp.einsum('bhqk,bkhd->bqhd', p, v[:, :q1]))
    return jnp.concatenate(outs, axis=1)


def hybrid_mixer(u, pos, key_valid, w_in, g_cq, w_uq, g_ckv, w_ukv, g_q_mla, g_k_mla,
                 b_forget, g_q_fox, g_k_fox, w_out):
    B, L, _ = u.shape
    proj = u @ w_in
    offs = [int(o) for o in np.cumsum(IN_SPLITS)[:-1]]
    c_q, c_kv, k_pe, fox_qkv, f_logit = jnp.split(proj, offs, axis=-1)

    q = (rms_norm(c_q, g_cq) @ w_uq).reshape(B, L, MLA_HEADS, MLA_QK_DIM)
    q = rms_norm(q, g_q_mla)
    q = jnp.concatenate([q[..., :MLA_NOPE_DIM], rope(q[..., MLA_NOPE_DIM:], pos)], axis=-1)
    kv = (rms_norm(c_kv, g_ckv) @ w_ukv).reshape(B, L, MLA_HEADS, MLA_NOPE_DIM + MLA_V_DIM)
    k_nope, v = kv[..., :MLA_NOPE_DIM], kv[..., MLA_NOPE_DIM:]
    k_pe_b = jnp.broadcast_to(k_pe[:, :, None, :], (B, L, MLA_HEADS, MLA_ROPE_DIM))
    k = rms_norm(jnp.concatenate([k_nope, k_pe_b], axis=-1), g_k_mla)
    k = jnp.concatenate([k[..., :MLA_NOPE_DIM], rope(k[..., MLA_NOPE_DIM:], pos)], axis=-1)
    o_mla = block_causal_attention(q, k, v, 1.0 / math.sqrt(MLA_QK_DIM), key_valid)

    fq, fk, fv = jnp.split(fox_qkv.reshape(B, L, 3, FOX_HEADS, FOX_HEAD_DIM), 3, axis=2)
    fq = rms_norm(fq[:, :, 0], g_q_fox)
    fk = rms_norm(fk[:, :, 0], g_k_fox)
    fv = fv[:, :, 0]
    log_f = jax.nn.log_sigmoid(f_logit.astype(jnp.float32) + b_forget.astype(jnp.float32))
    cum = jnp.cumsum(log_f, axis=1)
    o_fox = block_causal_attention(fq, fk, fv, 1.0 / math.sqrt(FOX_HEAD_DIM), key_valid, cum)

    o = jnp.concatenate([o_mla.reshape(B, L, MLA_HEADS * MLA_V_DIM),
                         o_fox.reshape(B, L, FOX_HEADS * FOX_HEAD_DIM)], axis=-1)
    return o @ w_out


def setup_inputs(seed: int = 0) -> dict:
    key = jax.random.key(seed)
    ks = jax.random.split(key, 24)
    f32 = jnp.float32

    def nrm(k, shape, fan_in):
        return jax.random.normal(k, shape, f32) * (fan_in ** -0.5)

    def gain(k, n):
        return 1.0 + 0.02 * jax.random.normal(k, (DEPTH, n), f32)

    return {
        "x": jax.random.normal(ks[0], (BATCH, SEQ, D_MODEL), f32),
        "meta_tokens": jax.random.normal(ks[1], (N_META, D_MODEL), f32),
        "g_ffn1": gain(ks[2], D_MODEL),
        "w1_gate": nrm(ks[3], (DEPTH, D_MODEL, D_FF), D_MODEL),
        "w1_up": nrm(ks[4], (DEPTH, D_MODEL, D_FF), D_MODEL),
        "w1_down": nrm(ks[5], (DEPTH, D_FF, D_MODEL), D_FF),
        "g_mix": gain(ks[6], D_MODEL),
        "w_in": nrm(ks[7], (DEPTH, D_MODEL, D_IN), D_MODEL),
        "g_cq": gain(ks[8], MLA_Q_RANK),
        "w_uq": nrm(ks[9], (DEPTH, MLA_Q_RANK, MLA_HEADS * MLA_QK_DIM), MLA_Q_RANK),
        "g_ckv": gain(ks[10], MLA_KV_RANK),
        "w_ukv": nrm(ks[11], (DEPTH, MLA_KV_RANK, MLA_HEADS * (MLA_NOPE_DIM + MLA_V_DIM)), MLA_KV_RANK),
        "g_q_mla": gain(ks[12], MLA_QK_DIM),
        "g_k_mla": gain(ks[13], MLA_QK_DIM),
        "b_forget": jax.random.uniform(ks[14], (DEPTH, FOX_HEADS), f32, 1.0, 4.0),
        "g_q_fox": gain(ks[15], FOX_HEAD_DIM),
        "g_k_fox": gain(ks[16], FOX_HEAD_DIM),
        "w_out": nrm(ks[17], (DEPTH, D_MIX, D_MODEL), D_MIX),
        "g_ffn2": gain(ks[18], D_MODEL),
        "w2_gate": nrm(ks[19], (DEPTH, D_MODEL, D_FF), D_MODEL),
        "w2_up": nrm(ks[20], (DEPTH, D_MODEL, D_FF), D_MODEL),
        "w2_down": nrm(ks[21], (DEPTH, D_FF, D_MODEL), D_FF),
    }


def reference(x, meta_tokens, g_ffn1, w1_gate, w1_up, w1_down, g_mix, w_in, g_cq, w_uq,
              g_ckv, w_ukv, g_q_mla, g_k_mla, b_forget, g_q_fox, g_k_fox, w_out,
              g_ffn2, w2_gate, w2_up, w2_down):
    B = x.shape[0]
    pad = BLOCK - N_META
    meta = jnp.broadcast_to(meta_tokens.astype(x.dtype)[None], (B, N_META, D_MODEL))
    h = jnp.concatenate([jnp.zeros((B, pad, D_MODEL), x.dtype), meta, x], axis=1)
    L = h.shape[1]
    idx = jnp.arange(L)
    key_valid = idx >= pad
    pos = jnp.maximum(idx - pad, 0).astype(jnp.int32)
    for l in range(DEPTH):
        h = h + 0.5 * swiglu(rms_norm(h, g_ffn1[l]), w1_gate[l], w1_up[l], w1_down[l])
        h = h + hybrid_mixer(rms_norm(h, g_mix[l]), pos, key_valid, w_in[l], g_cq[l], w_uq[l],
                             g_ckv[l], w_ukv[l], g_q_mla[l], g_k_mla[l], b_forget[l],
                             g_q_fox[l], g_k_fox[l], w_out[l])
        h = h + 0.5 * swiglu(rms_norm(h, g_ffn2[l]), w2_gate[l], w2_up[l], w2_down[l])
    return h[:, BLOCK:]
```

```python
from contextlib import ExitStack
import os
import numpy as np
import concourse.bass as bass
import concourse.mybir as mybir
from concourse.bass_utils import run_bass_kernel_spmd

F32 = mybir.dt.float32
BF16 = mybir.dt.bfloat16
ALU = mybir.AluOpType
AF = mybir.ActivationFunctionType

D = 1024
DFF = 2816
NF = 22
DIN = 1960
NSLOT = 16
HS = 6
EPS = 1e-6
NEGM = 30000.0
G_FFN1, G_MIX, G_FFN2, G_CQ, G_CKV, G_QM, G_KM, G_QF, G_KF, G_B, G_SEL, G_END = (
    0, 1024, 2048, 3072, 3328, 3456, 3552, 3648, 3712, 3776, 3784, 3788)
R_KM, R_KF, R_V, R_END = 0, 768, 768 + 584, 768 + 584 + 1024


class Trk:
    __slots__ = ("ws", "rs", "grp")

    def __init__(self):
        self.ws = []
        self.rs = []
        self.grp = None


class Op:
    __slots__ = ("eng", "fn", "deps", "dma", "sig", "cnt", "sem", "val", "prev")


class Sched:
    def __init__(self, nc, es):
        self.nc = nc
        self.engs = {"pe": nc.tensor, "act": nc.scalar, "dve": nc.vector, "pool": nc.gpsimd, "sp": nc.sync}
        self.ops = {e: [] for e in self.engs}
        self.sems = {e: es.enter_context(nc.semaphore("s_" + e)) for e in self.engs}
        self.dsems = {q: [es.enter_context(nc.semaphore("d_%s%d" % (q, i))) for i in range(n)]
                      for q, n in (("sp", 24), ("pool", 12), ("act", 12))}
        self.dcount = {"sp": 0, "pool": 0, "act": 0}
        self.es = es
        self.ncc = 0
        self.last = {}

    def defer_begin(self):
        self._defer = []

    def defer_end(self):
        lst, self._defer = self._defer, None
        return lst

    def replay(self, lst):
        for a, k in lst:
            self.op(*a, **k)

    def op(self, eng, fn, r=(), w=(), dma=False, cc=False, join=None):
        if getattr(self, "_defer", None) is not None:
            self._defer.append(((eng, fn), dict(r=list(r), w=list(w), dma=dma, cc=cc, join=join)))
            return None
        o = Op()
        o.eng, o.fn, o.dma, o.sig, o.cnt, o.prev = eng, fn, dma, False, 0, None
        deps = []
        for t in r:
            for x in t.ws:
                deps.append((x, True))
        for t in w:
            if not (join is not None and t.grp == join):
                for x in t.ws:
                    deps.append((x, False))
            for x in t.rs:
                deps.append((x, False))
        o.deps = []
        for d, raw in deps:
            if d is o:
                continue
            if not d.dma and d.eng == eng:
                if eng in ("pe", "sp"):
                    continue
            if d not in o.deps:
                o.deps.append(d)
                d.sig = True
        for t in w:
            if join is not None and t.grp == join:
                t.ws.append(o)
            else:
                t.ws = [o]
            t.grp = join
            t.rs = []
        for t in r:
            if o not in t.rs:
                t.rs.append(o)
        if cc:
            o.sem = self.es.enter_context(self.nc.semaphore("cc%d" % self.ncc))
            self.ncc += 1
            o.val = 1
        elif dma:
            n = self.dcount[eng]
            self.dcount[eng] = n + 1
            pool = self.dsems[eng]
            o.sem = pool[n % len(pool)]
            o.val = 16 * (n // len(pool) + 1)
            key = (eng, n % len(pool))
            o.prev = self.last.get(key)
            self.last[key] = o
        self.ops[eng].append(o)
        return o

    def barrier(self):
        dm = {}
        for q in ("sp", "pool", "act"):
            for o in self.ops[q]:
                if o.dma:
                    dm[id(o.sem)] = o
        lasts = {}
        for e2 in self.engs:
            for p in reversed(self.ops[e2]):
                if not p.dma and p.fn is not None:
                    lasts[e2] = p
                    break
        for e in self.engs:
            o = Op()
            o.eng, o.fn, o.dma, o.sig, o.cnt, o.prev = e, None, False, False, 0, None
            o.deps = []
            for e2, p in lasts.items():
                if e2 != e:
                    p.sig = True
                    o.deps.append(p)
            o.deps.extend(dm.values())
            self.ops[e].append(o)

    def emit(self):
        for e, lst in self.ops.items():
            c = 0
            for o in lst:
                if o.sig and not o.dma and o.fn is not None:
                    c += 1
                    o.cnt = c
        for e, lst in self.ops.items():
            eng = self.engs[e]
            seen = {}
            dseen = {}

            def wait_dma(d):
                k = id(d.sem)
                if dseen.get(k, 0) < d.val:
                    eng.wait_ge(d.sem, d.val)
                    dseen[k] = d.val

            for o in lst:
                for d in o.deps:
                    if d.dma:
                        wait_dma(d)
                    elif d.fn is None:
                        continue
                    else:
                        if seen.get(d.eng, 0) < d.cnt:
                            eng.wait_ge(self.sems[d.eng], d.cnt)
                            seen[d.eng] = d.cnt
                if o.fn is None:
                    continue
                if o.dma and o.prev is not None:
                    wait_dma(o.prev)
                ins = o.fn()
                if o.dma and o.val == 1:
                    ins.then_inc(o.sem)
                elif o.dma:
                    ins.then_inc(o.sem, 16)
                elif o.sig:
                    ins.then_inc(self.sems[e], 1)
            if e in self.dsems:
                for o in lst:
                    if o.dma:
                        wait_dma(o)


def build():
    nc = bass.Bass("TRN2", target_bir_lowering=False)
    es = ExitStack()
    S = Sched(nc, es)

    def din(name, shape, dt=F32):
        return nc.dram_tensor(name, shape, dt, kind="ExternalInput")

    x_d = din("x", [NSLOT, 128, D])
    meta_d = din("meta", [128, D])
    gv_d = din("gvec", [128, G_END])
    cs_d = din("cossin", [128, 17, 32])
    mask_d = din("masks", [128, 8, 128])
    w1g_d, w1u_d, w1d_d = din("w1g", [D, DFF]), din("w1u", [D, DFF]), din("w1d", [DFF, D])
    w2g_d, w2u_d, w2d_d = din("w2g", [D, DFF]), din("w2u", [D, DFF]), din("w2d", [DFF, D])
    win_d, wuq_d, wukv_d, wout_d = din("w_in", [D, DIN]), din("w_uq", [256, 768]), din("w_ukv", [128, 1024]), din("w_out", [D, D])
    y_d = nc.dram_tensor("y", [NSLOT, 128, D], F32, kind="ExternalOutput")

    def scr(name, shape, dt=BF16):
        if os.environ.get("KDEBUG") and name in ("h_s", "qm_s", "qf_s", "mk_s", "mf_s", "mv_s", "mt_s", "ot_s"):
            return nc.dram_tensor(name, shape, dt, kind="ExternalOutput")
        return nc.dram_tensor(name, shape, dt)

    w1g_s, w1u_s, w1d_s = scr("w1g_s", [11, 128, 8, 256]), scr("w1u_s", [11, 128, 8, 256]), scr("w1d_s", [DFF, D])
    w2g_s, w2u_s, w2d_s = scr("w2g_s", [11, 128, 8, 256]), scr("w2u_s", [11, 128, 8, 256]), scr("w2d_s", [DFF, D])
    win_s, wuq_s, wukv_s, wout_s = scr("win_s", [D, DIN]), scr("wuq_s", [256, 768]), scr("wukv_s", [128, 1024]), scr("wout_s", [D, D])
    h_s = scr("h_s", [NSLOT, 128, D], F32)
    qm_s = scr("qm_s", [NSLOT, 96, 8, 128])
    qf_s = scr("qf_s", [NSLOT, 73, 8, 128])
    mk_s = scr("mk_s", [8, 96, 128])
    mf_s = scr("mf_s", [8, 73, 128])
    mv_s = scr("mv_s", [16, 128, 64])
    mt_s = scr("mt_s", [1, 8], F32)
    ot_s = scr("ot_s", [8, 128, 2048])
    NSH = [HS, NSLOT - HS]
    sendH = [scr("sendA", [16 * 160, NSH[0] * 128]), scr("sendB", [16 * 160, NSH[1] * 128])]
    gathH = [scr("gathA", [16 * 640, NSH[0] * 128]), scr("gathB", [16 * 640, NSH[1] * 128])]
    tsend = scr("tsend", [1, 128], F32)
    tgath = scr("tgath", [4, 128], F32)

    KD = bool(os.environ.get("KDEBUG"))
    if KD:
        dbg_osb = nc.dram_tensor("dbg_osb", [16, 128, 2048], F32, kind="ExternalOutput")
        dbg_g0 = nc.dram_tensor("dbg_g0", [2, 640, 2048], BF16, kind="ExternalOutput")
        dbg_g1 = nc.dram_tensor("dbg_g1", [2, 640, 2048], BF16, kind="ExternalOutput")

    def sb(name, shape, dt=F32, stack=es):
        return stack.enter_context(nc.sbuf_tensor(name, shape, dt))

    T_cast = {}

    gv = sb("gv", [128, G_END])
    T_gv = Trk()
    S.op("sp", lambda: nc.sync.dma_start(out=gv[:], in_=gv_d[:, :]), w=[T_gv], dma=True)
    cs = sb("cs", [128, 17, 32])
    T_cs = Trk()
    S.op("sp", lambda: nc.sync.dma_start(out=cs[:], in_=cs_d[:, :, :]), w=[T_cs], dma=True)
    ident = sb("ident", [128, 128], BF16)
    T_id = Trk()
    S.op("pool", lambda: nc.gpsimd.memset(ident[:], 0.0), w=[T_id])
    S.op("pool", lambda: nc.gpsimd.affine_select(out=ident[:], in_=ident[:], pattern=[[-1, 128]], compare_op=ALU.not_equal,
                                                 fill=1.0, base=0, channel_multiplier=1), r=[T_id], w=[T_id])
    tri = sb("tri", [128, 128], F32)
    T_tri = Trk()
    S.op("pool", lambda: nc.gpsimd.memset(tri[:], 1.0), w=[T_tri])
    S.op("pool", lambda: nc.gpsimd.affine_select(out=tri[:], in_=tri[:], pattern=[[1, 128]], compare_op=ALU.is_ge,
                                                 fill=0.0, base=0, channel_multiplier=-1), r=[T_tri], w=[T_tri])
    S.op("dve", lambda: nc.vector.tensor_scalar(out=gv[:, G_QM:G_QM + 96], in0=gv[:, G_QM:G_QM + 96], scalar1=float(96 ** -0.5),
                                                scalar2=None, op0=ALU.mult), r=[T_gv], w=[T_gv])
    S.op("dve", lambda: nc.vector.tensor_scalar(out=gv[:, G_QF:G_QF + 64], in0=gv[:, G_QF:G_QF + 64], scalar1=0.125,
                                                scalar2=None, op0=ALU.mult), r=[T_gv], w=[T_gv])

    cast_ops = {}

    def cast_all(dst, src, rows, key, gate=None):
        lst = []
        for i_, r0 in enumerate(range(0, rows, 256)):
            r1 = min(rows, r0 + 256)
            t = Trk()
            S.op("pool", lambda d=dst, s=src, a=r0, b=r1: nc.gpsimd.dma_start(out=d[a:b, :], in_=s[a:b, :]),
                 r=[gate[(2 * i_) % NF]] if gate else [], w=[t], dma=True)
            lst.append(t)
        cast_ops[key] = lst

    def cast_cols(dst, src, key, fp, gate=None):
        t = Trk()
        S.op("pool", lambda: nc.gpsimd.dma_start(out=dst[fp, :, :, :].rearrange("p k c -> k p c"),
                                                 in_=src[:, fp * 256:(fp + 1) * 256].rearrange("(k p) c -> k p c", p=128)),
             r=[gate[(2 * fp) % NF]] if gate else [], w=[t], dma=True)
        cast_ops.setdefault(key, []).append(t)

    for fp in range(11):
        cast_cols(w1g_s, w1g_d, "w1g", fp)
        cast_cols(w1u_s, w1u_d, "w1u", fp)
    cast_all(w1d_s, w1d_d, DFF, "w1d")
    cast_all(win_s, win_d, D, "win")
    cast_all(wuq_s, wuq_d, 256, "wuq")
    cast_all(wukv_s, wukv_d, 128, "wukv")
    def cast_phase_c(part, gate):
        if part == 0:
            cast_all(wout_s, wout_d, D, "wout", gate)
            for fp in range(5):
                cast_cols(w2g_s, w2g_d, "w2g", fp, gate)
                cast_cols(w2u_s, w2u_d, "w2u", fp, gate)
        elif part == 1:
            for fp in range(5, 11):
                cast_cols(w2g_s, w2g_d, "w2g", fp, gate)
                cast_cols(w2u_s, w2u_d, "w2u", fp, gate)
        else:
            cast_all(w2d_s, w2d_d, DFF, "w2d", gate)

    def make_ffn_bufs(stack, tg, ntmax=512):
        B = {}
        B["wd"] = sb("wd" + tg, [128, NF, D], BF16, stack)
        B["T_wd"] = Trk()
        B["slab"] = [(sb("sg%d" % i + tg, [128, 8, 256], BF16, stack), sb("su%d" % i + tg, [128, 8, 256], BF16, stack), Trk()) for i in range(2)]
        B["xn"] = [(sb("xn%d" % i + tg, [128, D], BF16, stack), Trk()) for i in range(2)]
        B["xT"] = sb("xT" + tg, [128, 8, ntmax], BF16, stack)
        B["T_xT"] = Trk()
        B["aT"] = sb("aT" + tg, [128, NF, ntmax], BF16, stack)
        B["T_aT"] = [Trk() for _ in range(NF)]
        B["sg"] = [(sb("sgl%d" % i + tg, [128, ntmax], F32, stack), Trk()) for i in range(2)]
        B["junk"] = sb("junk" + tg, [128, D], BF16, stack)
        B["T_junk"] = Trk()
        B["ss"] = sb("ss" + tg, [128, 8], F32, stack)
        B["T_ss"] = Trk()
        B["rs"] = sb("rs" + tg, [128, 8], F32, stack)
        B["T_rs"] = Trk()
        B["T_ss2"] = Trk()
        B["T_rs2"] = Trk()
        B["nslab"] = 0
        return B

    def load_wd(B, wd_s, key):
        for f0 in range(0, NF, 11):
            S.op("sp", lambda a=f0: nc.sync.dma_start(out=B["wd"][:, a:a + 11, :],
                                                      in_=wd_s[a * 128:(a + 11) * 128, :].rearrange("(f p) c -> p f c", p=128)),
                 r=cast_ops[key], w=[B["T_wd"]], dma=True)

    def rms_scale(B, PS, blocks, g_off, xT=None, T_xT=None, so=0):
        nb = len(blocks)
        if xT is None:
            xT, T_xT = B["xT"], B["T_xT"]
        ss_, rs_ = B["ss"][:, so:so + 4], B["rs"][:, so:so + 4]
        T_ss_, T_rs_ = (B["T_ss"], B["T_rs"]) if so == 0 else (B["T_ss2"], B["T_rs2"])
        for j, (h, th) in enumerate(blocks):
            S.op("act", lambda h=h, j=j: nc.scalar.activation(out=B["junk"][:], in_=h, func=AF.Square, accum_out=ss_[:, j:j + 1]),
                 r=[th], w=[B["T_junk"], T_ss_])
        S.op("act", lambda: nc.scalar.activation(out=rs_[:, 0:nb], in_=ss_[:, 0:nb], func=AF.Ln, scale=1.0 / D, bias=EPS),
             r=[T_ss_], w=[T_rs_])
        S.op("act", lambda: nc.scalar.activation(out=rs_[:, 0:nb], in_=rs_[:, 0:nb], func=AF.Exp, scale=-0.5),
             r=[T_rs_], w=[T_rs_])
        for j, (h, th) in enumerate(blocks):
            xn, txn = B["xn"][j % len(B["xn"])]
            S.op("dve", lambda h=h, j=j, xn=xn: nc.vector.scalar_tensor_tensor(out=xn[:], in0=h, scalar=rs_[:, j:j + 1],
                                                                                 in1=gv[:, g_off:g_off + D], op0=ALU.mult, op1=ALU.mult),
                 r=[th, T_rs_, T_gv], w=[txn])
            pt, tpt = PS["T"][j % 2]
            for k in range(8):
                S.op("pe", lambda k=k, xn=xn, pt=pt: nc.tensor.transpose(out=pt[:, k * 128:(k + 1) * 128], in_=xn[:, k * 128:(k + 1) * 128],
                                                                           identity=ident[:]),
                     r=[txn, T_id], w=[tpt])
            S.op("act", lambda j=j, pt=pt: nc.scalar.copy(out=xT[:, :, j * 128:(j + 1) * 128],
                                                          in_=pt[:, :].rearrange("p (k t) -> p k t", k=8)),
                 r=[tpt], w=[T_xT])

    def ffn(B, PS, blocks, g_off, wg_s, wu_s, kg, ku):
        for _ in ffn_gen(B, PS, blocks, g_off, wg_s, wu_s, kg, ku):
            pass

    def ffn_gen(B, PS, blocks, g_off, wg_s, wu_s, kg, ku):
        nb = len(blocks)
        NT = nb * 128
        rms_scale(B, PS, blocks, g_off)
        yield

        def load_slab(fp):
            sgt, sut, tsl = B["slab"][B["nslab"] % 2]
            B["nslab"] += 1
            S.op("sp", lambda: nc.sync.dma_start(out=sgt[:], in_=wg_s[fp, :, :, :]),
                 r=[cast_ops[kg][fp]], w=[tsl], dma=True)
            S.op("sp", lambda: nc.sync.dma_start(out=sut[:], in_=wu_s[fp, :, :, :]),
                 r=[cast_ops[ku][fp]], w=[tsl], dma=True)
            return sgt, sut, tsl

        nxt = load_slab(0)
        for fp in range(11):
            sgt, sut, tsl = nxt
            if fp + 1 < 11:
                nxt = load_slab(fp + 1)
            for j2 in range(2):
                f = 2 * fp + j2
                pg, tpg = PS["G"][f % len(PS["G"])]
                pu, tpu = PS["U"][f % len(PS["U"])]
                for k in range(8):
                    S.op("pe", lambda k=k, pg=pg, sgt=sgt, j2=j2: nc.tensor.matmul(pg[:, 0:NT], lhsT=sgt[:, k, j2 * 128:(j2 + 1) * 128],
                                                                                  rhs=B["xT"][:, k, 0:NT], start=(k == 0), stop=(k == 7)),
                         r=[tsl, B["T_xT"]], w=[tpg])
                for k in range(8):
                    S.op("pe", lambda k=k, pu=pu, sut=sut, j2=j2: nc.tensor.matmul(pu[:, 0:NT], lhsT=sut[:, k, j2 * 128:(j2 + 1) * 128],
                                                                                  rhs=B["xT"][:, k, 0:NT], start=(k == 0), stop=(k == 7)),
                         r=[tsl, B["T_xT"]], w=[tpu])
                sg, tsg = B["sg"][f % 2]
                S.op("act", lambda sg=sg, pg=pg: nc.scalar.activation(out=sg[:, 0:NT], in_=pg[:, 0:NT], func=AF.Silu), r=[tpg], w=[tsg])
                S.op("dve", lambda sg=sg, pu=pu, f=f: nc.vector.tensor_tensor(out=B["aT"][:, f, 0:NT], in0=sg[:, 0:NT], in1=pu[:, 0:NT], op=ALU.mult),
                     r=[tsg, tpu], w=[B["T_aT"][f]])
            yield
        if B.get("wd_pending"):
            load_wd(B, *B["wd_pending"])
            B["wd_pending"] = None
        for j, (h, th) in enumerate(blocks):
            for half in range(2):
                po, tpo = PS["O"][half]
                for f in range(NF):
                    S.op("pe", lambda f=f, j=j, half=half, po=po: nc.tensor.matmul(po[:, :], lhsT=B["aT"][:, f, j * 128:(j + 1) * 128],
                                                                                    rhs=B["wd"][:, f, half * 512:(half + 1) * 512],
                                                                                    start=(f == 0), stop=(f == NF - 1)),
                         r=[B["T_aT"][f], B["T_wd"]], w=[tpo])
                S.op("dve", lambda h=h, half=half, po=po: nc.vector.scalar_tensor_tensor(out=h[:, half * 512:(half + 1) * 512], in0=po[:, :], scalar=0.5,
                                                                                          in1=h[:, half * 512:(half + 1) * 512], op0=ALU.mult, op1=ALU.add),
                     r=[tpo, th], w=[th])
                yield

    def make_psum(stack, tg):
        psA = stack.enter_context(nc.psum_tensor("psA" + tg, [128, 6, 512], F32))
        psT = stack.enter_context(nc.psum_tensor("psT" + tg, [128, 2, 1024], BF16))
        PS = {"G": [(psA[:, 0, :], Trk()), (psA[:, 1, :], Trk())], "U": [(psA[:, 2, :], Trk()), (psA[:, 3, :], Trk())],
              "O": [(psA[:, 4, :], Trk()), (psA[:, 5, :], Trk())], "T": [(psT[:, 0, :], Trk()), (psT[:, 1, :], Trk())], "A": psA}
        return PS

    h1_s = scr("h1_s", [17, 128, D], F32)
    with ExitStack() as sa1:
        PSA1 = make_psum(sa1, "a")
        BA1 = make_ffn_bufs(sa1, "a", 512)
        BA1["wd_pending"] = (w1d_s, "w1d")
        hbA = [sb("hbufa%d" % i, [128, 4, D], F32, sa1) for i in range(2)]
        T_hbA = [[Trk() for _ in range(4)] for _ in range(2)]
        groupsA = [[0, 1, 2, 3], [4, 5, 6, 7], [8, 9, 10], [11, 12, 13], [14, 15, 16]]
        for gi, grp in enumerate(groupsA):
            blocks = []
            for j, lb in enumerate(grp):
                src = meta_d[:, :] if lb == 0 else x_d[lb - 1, :, :]
                S.op("sp", lambda j=j, src=src, gi=gi: nc.sync.dma_start(out=hbA[gi % 2][:, j, :], in_=src), w=[T_hbA[gi % 2][j]], dma=True)
                blocks.append((hbA[gi % 2][:, j, :], T_hbA[gi % 2][j]))
            ffn(BA1, PSA1, blocks, G_FFN1, w1g_s, w1u_s, "w1g", "w1u")
            for j, lb in enumerate(grp):
                S.op("pool", lambda j=j, lb=lb, gi=gi: nc.gpsimd.dma_start(out=h1_s[lb, :, :], in_=hbA[gi % 2][:, j, :]),
                     r=[T_hbA[gi % 2][j]], w=[Trk()], dma=True)
        S.barrier()

    with ExitStack() as sa:
        psA2 = sa.enter_context(nc.psum_tensor("psA2", [128, 6, 512], F32))
        psT2 = sa.enter_context(nc.psum_tensor("psT2", [128, 2, 1024], BF16))
        PST = {"T": [(psT2[:, 0, :], Trk()), (psT2[:, 1, :], Trk())]}
        win = sb("win", [128, 8, DIN], BF16, sa)
        T_win = Trk()
        wuq = sb("wuq", [128, 2, 768], BF16, sa)
        T_wuq = Trk()
        wukv = sb("wukv", [128, 1024], BF16, sa)
        T_wukv = Trk()
        for k0 in range(0, 8, 2):
            S.op("sp", lambda a=k0: nc.sync.dma_start(out=win[:, a:a + 2, :], in_=win_s[a * 128:(a + 2) * 128, :].rearrange("(k p) c -> p k c", p=128)),
                 r=cast_ops["win"], w=[T_win], dma=True, join="win")
        S.op("sp", lambda: nc.sync.dma_start(out=wuq[:], in_=wuq_s[:, :].rearrange("(k p) c -> p k c", p=128)), r=cast_ops["wuq"], w=[T_wuq], dma=True)
        S.op("sp", lambda: nc.sync.dma_start(out=wukv[:], in_=wukv_s[:, :]), r=cast_ops["wukv"], w=[T_wukv], dma=True)

        def make_mixer_set(i):
            M = {}

            def t(name, shape, dt=F32):
                M[name] = sb("%s_%d" % (name, i), shape, dt, sa)
                M["T_" + name] = Trk()
            t("h", [128, D])
            t("pj", [128, DIN])
            t("qkv", [128, 1792])
            t("cn", [128, 384], BF16)
            t("cT", [128, 3, 128], BF16)
            t("st", [128, 40])
            t("rr", [128, 40])
            t("tmp1", [128, 768])
            t("tmp2", [128, 576])
            t("rp", [128, 4, 8, 16])
            t("kr", [128, 64])
            t("Qm", [128, 8, 96], BF16)
            t("Km", [128, 8, 96], BF16)
            t("Qf", [128, 8, 80], BF16)
            t("Kf", [128, 8, 80], BF16)
            t("Vt", [128, 16, 64], BF16)
            t("flb", [128, 8])
            t("cA", [128, 8])
            t("c32", [128, 8])
            t("xTm", [128, 8, 128], BF16)
            M["stg"] = [(sb("stg%d_%d" % (k, i), [96, 8, 128], BF16, sa), Trk()) for k in range(4)]
            Bm = {"junk": sb("junk_%d" % i, [128, D], BF16, sa), "T_junk": Trk(),
                  "ss": sb("ss_%d" % i, [128, 8], F32, sa), "T_ss": Trk(), "rs": sb("rs_%d" % i, [128, 8], F32, sa), "T_rs": Trk(),
                  "T_ss2": Trk(), "T_rs2": Trk(), "xn": [(sb("xn_%d" % i, [128, D], BF16, sa), Trk())]}
            M["B"] = Bm
            M["PSM"] = [(psA2[:, 2 * i, :], Trk()), (psA2[:, 2 * i + 1, :], Trk())]
            Qf_, Kf_ = M["Qf"], M["Kf"]
            S.op("pool", lambda: nc.gpsimd.memset(Qf_[:, :, 64:67], 0.0), w=[M["T_Qf"]])
            S.op("pool", lambda: nc.gpsimd.memset(Qf_[:, :, 70:73], 1.0), w=[M["T_Qf"]])
            S.op("pool", lambda: nc.gpsimd.memset(Kf_[:, :, 64:70], 1.0), w=[M["T_Kf"]])
            return M

        send_trks = []

        def mixer_block(lb, M):
            slot = lb - 1
            BA, PSA, PSM = M["B"], PST, M["PSM"]
            h, th = M["h"][:, :], M["T_h"]
            pj, T_pj, qkv, T_qkv, cn, T_cn, cT, T_cT = M["pj"], M["T_pj"], M["qkv"], M["T_qkv"], M["cn"], M["T_cn"], M["cT"], M["T_cT"]
            st, T_st, rr, T_rr, tmp1, T_t1, tmp2, T_t2 = M["st"], M["T_st"], M["rr"], M["T_rr"], M["tmp1"], M["T_tmp1"], M["tmp2"], M["T_tmp2"]
            rp, T_rp, kr, T_kr = M["rp"], M["T_rp"], M["kr"], M["T_kr"]
            Qm, Km, Qf, Kf, Vt = M["Qm"], M["Km"], M["Qf"], M["Kf"], M["Vt"]
            T_Qm, T_Km, T_Qf, T_Kf, T_Vt = M["T_Qm"], M["T_Km"], M["T_Qf"], M["T_Kf"], M["T_Vt"]
            flb, T_flb, cA, T_cA, c32, T_c32 = M["flb"], M["T_flb"], M["cA"], M["T_cA"], M["c32"], M["T_c32"]
            xTm, T_xTm, stg = M["xTm"], M["T_xTm"], M["stg"]
            S.op("pool", lambda: nc.gpsimd.dma_start(out=h, in_=h1_s[lb, :, :]), w=[th], dma=True)
            rms_scale(BA, PSA, [(h, th)], G_MIX, xTm, T_xTm)
            yield
            banks = [PSM[0], PSM[1], PSM[0], PSM[1]]
            for cb in range(4):
                c0, c1 = cb * 512, min(DIN, (cb + 1) * 512)
                pb, tpb = banks[cb]
                for k in range(8):
                    S.op("pe", lambda k=k, pb=pb, c0=c0, c1=c1: nc.tensor.matmul(pb[:, 0:c1 - c0], lhsT=xTm[:, k, 0:128], rhs=win[:, k, c0:c1],
                                                                               start=(k == 0), stop=(k == 7)),
                         r=[T_xTm, T_win], w=[tpb])
                S.op("act", lambda pb=pb, c0=c0, c1=c1: nc.scalar.copy(out=pj[:, c0:c1], in_=pb[:, 0:c1 - c0]), r=[tpb], w=[T_pj])
                yield
            S.op("dve", lambda: nc.vector.tensor_tensor(out=flb[:], in0=pj[:, 1952:1960], in1=gv[:, G_B:G_B + 8], op=ALU.add), r=[T_pj, T_gv], w=[T_flb])
            S.op("act", lambda: nc.scalar.activation(out=flb[:], in_=flb[:], func=AF.Exp, scale=-1.0), r=[T_flb], w=[T_flb])
            S.op("act", lambda: nc.scalar.activation(out=flb[:], in_=flb[:], func=AF.Ln, bias=1.0), r=[T_flb], w=[T_flb])
            yield "S1END"
            S.op("act", lambda: nc.scalar.activation(out=BA["junk"][:, 0:256], in_=pj[:, 0:256], func=AF.Square, accum_out=st[:, 0:1]),
                 r=[T_pj], w=[BA["T_junk"], T_st])
            S.op("act", lambda: nc.scalar.activation(out=BA["junk"][:, 0:128], in_=pj[:, 256:384], func=AF.Square, accum_out=st[:, 1:2]),
                 r=[T_pj], w=[BA["T_junk"], T_st])
            S.op("act", lambda: nc.scalar.activation(out=BA["junk"][:, 0:32], in_=pj[:, 384:416], func=AF.Square, accum_out=st[:, 2:3]),
                 r=[T_pj], w=[BA["T_junk"], T_st])
            S.op("act", lambda: nc.scalar.activation(out=rr[:, 0:1], in_=st[:, 0:1], func=AF.Ln, scale=1.0 / 256, bias=EPS), r=[T_st], w=[T_rr])
            S.op("act", lambda: nc.scalar.activation(out=rr[:, 1:2], in_=st[:, 1:2], func=AF.Ln, scale=1.0 / 128, bias=EPS), r=[T_st], w=[T_rr])
            S.op("act", lambda: nc.scalar.activation(out=rr[:, 0:2], in_=rr[:, 0:2], func=AF.Exp, scale=-0.5), r=[T_rr], w=[T_rr])
            S.op("dve", lambda: nc.vector.scalar_tensor_tensor(out=cn[:, 0:256], in0=pj[:, 0:256], scalar=rr[:, 0:1], in1=gv[:, G_CQ:G_CQ + 256],
                                                               op0=ALU.mult, op1=ALU.mult), r=[T_pj, T_rr, T_gv], w=[T_cn])
            S.op("dve", lambda: nc.vector.scalar_tensor_tensor(out=cn[:, 256:384], in0=pj[:, 256:384], scalar=rr[:, 1:2], in1=gv[:, G_CKV:G_CKV + 128],
                                                               op0=ALU.mult, op1=ALU.mult), r=[T_pj, T_rr, T_gv], w=[T_cn])
            pt, tpt = PSA["T"][0]
            for k in range(3):
                S.op("pe", lambda k=k: nc.tensor.transpose(out=pt[:, k * 128:(k + 1) * 128], in_=cn[:, k * 128:(k + 1) * 128], identity=ident[:]),
                     r=[T_cn, T_id], w=[tpt])
            S.op("act", lambda: nc.scalar.copy(out=cT[:, :, :], in_=pt[:, 0:384].rearrange("p (k t) -> p k t", k=3)), r=[tpt], w=[T_cT])
            yield
            pq0, tq0 = PSM[0]
            pq1, tq1 = PSM[1]
            for k in range(2):
                S.op("pe", lambda k=k: nc.tensor.matmul(pq0[:, 0:512], lhsT=cT[:, k, :], rhs=wuq[:, k, 0:512], start=(k == 0), stop=(k == 1)),
                     r=[T_cT, T_wuq], w=[tq0])
            for k in range(2):
                S.op("pe", lambda k=k: nc.tensor.matmul(pq1[:, 0:256], lhsT=cT[:, k, :], rhs=wuq[:, k, 512:768], start=(k == 0), stop=(k == 1)),
                     r=[T_cT, T_wuq], w=[tq1])
            S.op("act", lambda: nc.scalar.copy(out=qkv[:, 0:512], in_=pq0[:, 0:512]), r=[tq0], w=[T_qkv])
            S.op("act", lambda: nc.scalar.copy(out=qkv[:, 512:768], in_=pq1[:, 0:256]), r=[tq1], w=[T_qkv])
            S.op("pe", lambda: nc.tensor.matmul(pq0[:, 0:512], lhsT=cT[:, 2, :], rhs=wukv[:, 0:512], start=True, stop=True), r=[T_cT, T_wukv], w=[tq0])
            S.op("pe", lambda: nc.tensor.matmul(pq1[:, 0:512], lhsT=cT[:, 2, :], rhs=wukv[:, 512:1024], start=True, stop=True), r=[T_cT, T_wukv], w=[tq1])
            S.op("act", lambda: nc.scalar.copy(out=qkv[:, 768:1280], in_=pq0[:, 0:512]), r=[tq0], w=[T_qkv])
            S.op("act", lambda: nc.scalar.copy(out=qkv[:, 1280:1792], in_=pq1[:, 0:512]), r=[tq1], w=[T_qkv])
            yield
            AX = mybir.AxisListType.X
            S.op("act", lambda: nc.scalar.activation(out=tmp1[:, 0:768], in_=qkv[:, 0:768], func=AF.Square), r=[T_qkv], w=[T_t1])
            S.op("dve", lambda: nc.vector.tensor_reduce(out=st[:, 8:16], in_=tmp1[:, 0:768].rearrange("p (h d) -> p h d", h=8), axis=AX, op=ALU.add),
                 r=[T_t1], w=[T_st])
            S.op("act", lambda: nc.scalar.activation(out=tmp2[:, 0:512].rearrange("p (h d) -> p h d", h=8),
                                                     in_=qkv[:, 768:1792].rearrange("p (h d) -> p h d", h=8)[:, :, 0:64], func=AF.Square), r=[T_qkv], w=[T_t2])
            S.op("dve", lambda: nc.vector.tensor_reduce(out=st[:, 16:24], in_=tmp2[:, 0:512].rearrange("p (h d) -> p h d", h=8), axis=AX, op=ALU.add),
                 r=[T_t2], w=[T_st])
            S.op("act", lambda: nc.scalar.activation(out=BA["junk"][:, 0:1024], in_=pj[:, 416:1440], func=AF.Square), r=[T_pj], w=[BA["T_junk"]])
            S.op("dve", lambda: nc.vector.tensor_reduce(out=st[:, 24:40], in_=BA["junk"][:, 0:1024].rearrange("p (h d) -> p h d", h=16), axis=AX, op=ALU.add),
                 r=[BA["T_junk"]], w=[T_st])
            S.op("dve", lambda: nc.vector.tensor_scalar(out=st[:, 16:24], in0=st[:, 16:24], scalar1=st[:, 2:3], scalar2=None, op0=ALU.add),
                 r=[T_st], w=[T_st])
            S.op("act", lambda: nc.scalar.activation(out=rr[:, 8:24], in_=st[:, 8:24], func=AF.Ln, scale=1.0 / 96, bias=EPS), r=[T_st], w=[T_rr])
            S.op("act", lambda: nc.scalar.activation(out=rr[:, 24:40], in_=st[:, 24:40], func=AF.Ln, scale=1.0 / 64, bias=EPS), r=[T_st], w=[T_rr])
            S.op("act", lambda: nc.scalar.activation(out=rr[:, 8:40], in_=rr[:, 8:40], func=AF.Exp, scale=-0.5), r=[T_rr], w=[T_rr])

            cosb = cs[:, lb, 0:16]
            sinb = cs[:, lb, 16:32]

            def bc_h(ap2, n):
                return ap2.unsqueeze(1).to_broadcast([128, 8, n])

            def bc_d(ap2, n):
                return ap2.unsqueeze(2).to_broadcast([128, 8, n])

            yield
            q3 = qkv[:, 0:768].rearrange("p (h d) -> p h d", h=8)
            t13 = tmp1[:, 0:768].rearrange("p (h d) -> p h d", h=8)
            S.op("dve", lambda: nc.vector.tensor_tensor(out=t13, in0=q3, in1=bc_d(rr[:, 8:16], 96), op=ALU.mult), r=[T_qkv, T_rr], w=[T_t1])
            S.op("dve", lambda: nc.vector.tensor_tensor(out=t13, in0=t13, in1=bc_h(gv[:, G_QM:G_QM + 96], 96), op=ALU.mult), r=[T_t1, T_gv], w=[T_t1])
            S.op("act", lambda: nc.scalar.copy(out=Qm[:, :, 0:64], in_=t13[:, :, 0:64]), r=[T_t1], w=[T_Qm])
            x1, x2 = t13[:, :, 64:80], t13[:, :, 80:96]
            S.op("dve", lambda: nc.vector.tensor_tensor(out=rp[:, 0], in0=x1, in1=bc_h(cosb, 16), op=ALU.mult), r=[T_t1, T_cs], w=[T_rp])
            S.op("dve", lambda: nc.vector.tensor_tensor(out=rp[:, 1], in0=x2, in1=bc_h(sinb, 16), op=ALU.mult), r=[T_t1, T_cs], w=[T_rp])
            S.op("dve", lambda: nc.vector.tensor_tensor(out=rp[:, 2], in0=x2, in1=bc_h(cosb, 16), op=ALU.mult), r=[T_t1, T_cs], w=[T_rp])
            S.op("dve", lambda: nc.vector.tensor_tensor(out=rp[:, 3], in0=x1, in1=bc_h(sinb, 16), op=ALU.mult), r=[T_t1, T_cs], w=[T_rp])
            S.op("dve", lambda: nc.vector.tensor_tensor(out=Qm[:, :, 64:80], in0=rp[:, 0], in1=rp[:, 1], op=ALU.subtract), r=[T_rp], w=[T_Qm])
            S.op("dve", lambda: nc.vector.tensor_tensor(out=Qm[:, :, 80:96], in0=rp[:, 2], in1=rp[:, 3], op=ALU.add), r=[T_rp], w=[T_Qm])
            yield
            kv3 = qkv[:, 768:1792].rearrange("p (h d) -> p h d", h=8)
            t23 = tmp2[:, 0:512].rearrange("p (h d) -> p h d", h=8)
            S.op("dve", lambda: nc.vector.tensor_tensor(out=t23, in0=kv3[:, :, 0:64], in1=bc_d(rr[:, 16:24], 64), op=ALU.mult), r=[T_qkv, T_rr], w=[T_t2])
            S.op("dve", lambda: nc.vector.tensor_tensor(out=Km[:, :, 0:64], in0=t23, in1=bc_h(gv[:, G_KM:G_KM + 64], 64), op=ALU.mult),
                 r=[T_t2, T_gv], w=[T_Km])
            S.op("dve", lambda: nc.vector.tensor_tensor(out=kr[:, 0:32], in0=pj[:, 384:416], in1=gv[:, G_KM + 64:G_KM + 96], op=ALU.mult),
                 r=[T_pj, T_gv], w=[T_kr])
            S.op("dve", lambda: nc.vector.tensor_tensor(out=tmp2[:, 512:528], in0=kr[:, 0:16], in1=cosb, op=ALU.mult), r=[T_kr, T_cs], w=[T_t2])
            S.op("dve", lambda: nc.vector.tensor_tensor(out=tmp2[:, 528:544], in0=kr[:, 16:32], in1=sinb, op=ALU.mult), r=[T_kr, T_cs], w=[T_t2])
            S.op("dve", lambda: nc.vector.tensor_tensor(out=tmp2[:, 544:560], in0=kr[:, 16:32], in1=cosb, op=ALU.mult), r=[T_kr, T_cs], w=[T_t2])
            S.op("dve", lambda: nc.vector.tensor_tensor(out=tmp2[:, 560:576], in0=kr[:, 0:16], in1=sinb, op=ALU.mult), r=[T_kr, T_cs], w=[T_t2])
            S.op("dve", lambda: nc.vector.tensor_tensor(out=kr[:, 32:48], in0=tmp2[:, 512:528], in1=tmp2[:, 528:544], op=ALU.subtract), r=[T_t2], w=[T_kr])
            S.op("dve", lambda: nc.vector.tensor_tensor(out=kr[:, 48:64], in0=tmp2[:, 544:560], in1=tmp2[:, 560:576], op=ALU.add), r=[T_t2], w=[T_kr])
            S.op("dve", lambda: nc.vector.tensor_tensor(out=Km[:, :, 64:96], in0=bc_h(kr[:, 32:64], 32), in1=bc_d(rr[:, 16:24], 32), op=ALU.mult),
                 r=[T_kr, T_rr], w=[T_Km])
            yield
            fq3 = pj[:, 416:928].rearrange("p (h d) -> p h d", h=8)
            fk3 = pj[:, 928:1440].rearrange("p (h d) -> p h d", h=8)
            t1f = tmp1[:, 0:512].rearrange("p (h d) -> p h d", h=8)
            S.op("dve", lambda: nc.vector.tensor_tensor(out=t1f, in0=fq3, in1=bc_d(rr[:, 24:32], 64), op=ALU.mult), r=[T_pj, T_rr], w=[T_t1])
            S.op("dve", lambda: nc.vector.tensor_tensor(out=Qf[:, :, 0:64], in0=t1f, in1=bc_h(gv[:, G_QF:G_QF + 64], 64), op=ALU.mult),
                 r=[T_t1, T_gv], w=[T_Qf])
            S.op("dve", lambda: nc.vector.tensor_tensor(out=t23, in0=fk3, in1=bc_d(rr[:, 32:40], 64), op=ALU.mult), r=[T_pj, T_rr], w=[T_t2])
            S.op("dve", lambda: nc.vector.tensor_tensor(out=Kf[:, :, 0:64], in0=t23, in1=bc_h(gv[:, G_KF:G_KF + 64], 64), op=ALU.mult),
                 r=[T_t2, T_gv], w=[T_Kf])
            yield
            S.op("act", lambda: nc.scalar.copy(out=Vt[:, 0:8, :], in_=kv3[:, :, 64:128]), r=[T_qkv], w=[T_Vt])
            S.op("act", lambda: nc.scalar.copy(out=Vt[:, 8:16, :], in_=pj[:, 1440:1952].rearrange("p (h d) -> p h d", h=8)), r=[T_pj], w=[T_Vt])
            yield
            pc, tpc = PSM[0]
            S.op("pe", lambda: nc.tensor.matmul(pc[:, 0:8], lhsT=tri[:], rhs=flb[:], start=True, stop=True), r=[T_tri, T_flb], w=[tpc])
            S.op("dve", lambda: nc.vector.tensor_copy(out=cA[:], in_=pc[:, 0:8]), r=[tpc], w=[T_cA])
            S.op("dve", lambda: nc.vector.tensor_copy(out=Kf[:, :, 70], in_=cA[:]), r=[T_cA], w=[T_Kf])
            S.op("dve", lambda: nc.vector.tensor_tensor(out=c32[:], in0=cA[:], in1=Kf[:, :, 70], op=ALU.subtract), r=[T_cA, T_Kf], w=[T_c32])
            S.op("dve", lambda: nc.vector.tensor_copy(out=Kf[:, :, 71], in_=c32[:]), r=[T_c32], w=[T_Kf])
            S.op("dve", lambda: nc.vector.tensor_tensor(out=c32[:], in0=c32[:], in1=Kf[:, :, 71], op=ALU.subtract), r=[T_c32, T_Kf], w=[T_c32])
            S.op("dve", lambda: nc.vector.tensor_copy(out=Kf[:, :, 72], in_=c32[:]), r=[T_c32], w=[T_Kf])
            S.op("dve", lambda: nc.vector.tensor_scalar(out=Qf[:, :, 67:70], in0=Kf[:, :, 70:73], scalar1=-1.0, scalar2=None, op0=ALU.mult),
                 r=[T_Kf], w=[T_Qf])
            yield
            def tr_store(src, tsrc, dk, si, dsts, q="sp"):
                pt, tpt = PSA["T"][si % 2]
                for hh in range(8):
                    S.op("pe", lambda hh=hh, pt=pt: nc.tensor.transpose(out=pt[0:dk, hh * 128:(hh + 1) * 128], in_=src[:, hh, 0:dk], identity=ident[:]),
                         r=[tsrc, T_id], w=[tpt])
                sg_, tsg_ = stg[si % 4]
                S.op("act", lambda pt=pt, sg_=sg_: nc.scalar.copy(out=sg_[0:dk, :, :], in_=pt[0:dk, :].rearrange("p (h t) -> p h t", h=8)),
                     r=[tpt], w=[tsg_])
                for dst in dsts:
                    tsv = Trk()
                    send_trks.append(tsv)
                    eng_ = {"sp": nc.sync, "act": nc.scalar, "pool": nc.gpsimd}[q]
                    S.op(q, lambda sg_=sg_, dst=dst, eng_=eng_: eng_.dma_start(out=dst, in_=sg_[0:dk, :, :]), r=[tsg_], w=[tsv], dma=True)

            hfx = 0 if slot < HS else 1
            sendX = sendH[hfx]
            nsx = NSH[hfx]
            s8 = slot - (0 if hfx == 0 else HS)

            def send_view(r0, dk):
                v = sendX[r0:r0 + 8 * 160, :].rearrange("(h r) (s t) -> r h s t", h=8, s=nsx)
                return v[0:dk, :, s8, :]

            if lb == 0:
                tr_store(Km, T_Km, 96, 0, [mk_s[:, :, :].rearrange("h d t -> d h t")])
                yield
                tr_store(Kf, T_Kf, 73, 1, [mf_s[:, :, :].rearrange("h d t -> d h t")])
                yield
                S.op("pool", lambda: nc.gpsimd.dma_start(out=mv_s[:, :, :].rearrange("h t d -> t h d"), in_=Vt[:, :, :]), r=[T_Vt], w=[Trk()], dma=True)
                S.op("pool", lambda: nc.gpsimd.dma_start(out=mt_s[0:1, :], in_=cA[127:128, :]), r=[T_cA], w=[Trk()], dma=True)
            else:
                tr_store(Qm, T_Qm, 96, 2, [qm_s[slot, :, :, :]])
                yield
                tr_store(Km, T_Km, 96, 0, [send_view(0, 96)])
                yield
                tr_store(Qf, T_Qf, 73, 3, [qf_s[slot, :, :, :]])
                yield
                tr_store(Kf, T_Kf, 73, 1, [send_view(8 * 160, 73)], q="act")
                yield
                vv = sendX[:, :].rearrange("(h r) c -> r h c", h=16)[96:160, :, :].rearrange("t2 h (u s d) -> (t2 u) h s d", u=2, s=nsx)
                tsv = Trk()
                send_trks.append(tsv)
                S.op("pool", lambda: nc.gpsimd.dma_start(out=vv[:, :, s8, :], in_=Vt[:, :, :]), r=[T_Vt], w=[tsv], dma=True)
                tv = tsend[0:1, :].rearrange("o (s h) -> o s h", s=NSLOT)
                tsv = Trk()
                send_trks.append(tsv)
                S.op("pool", lambda: nc.gpsimd.dma_start(out=tv[:, slot, :], in_=cA[127:128, :]), r=[T_cA], w=[tsv], dma=True)

        RG = [[0, 1, 2, 3], [4, 5, 6, 7]]
        T_ag = [[Trk() for _ in range(16)] for _ in range(2)]
        AGG = [(h, 1) for h in range(16)]
        AGOF = {}
        for (h0, ng) in AGG:
            for k in range(ng):
                AGOF[h0 + k] = (h0, ng, k)

        def issue_ags(hf, sel=None):
            for (h0, ng) in AGG:
                if sel is not None and h0 not in sel:
                    continue
                S.op("pool", lambda h0=h0, ng=ng: nc.gpsimd.collective_compute("AllGather", ALU.bypass, replica_groups=RG,
                                                                               ins=[sendH[hf][h0 * 160:(h0 + ng) * 160, :]],
                                                                               outs=[gathH[hf][h0 * 640:(h0 + ng) * 640, :]]),
                     r=list(send_trks), w=[T_ag[hf][h0 + k] for k in range(ng)], dma=True, cc=True)

        zt = sb("zt", [23, NSH[1] * 128], BF16, sa)
        T_zt = Trk()
        S.op("pool", lambda: nc.gpsimd.memset(zt[:], 0.0), w=[T_zt])
        for hf in range(2):
            for hh_ in range(8):
                tz = Trk()
                send_trks.append(tz)
                r0_ = (8 + hh_) * 160 + 73
                S.op("pool", lambda hf=hf, r0_=r0_: nc.gpsimd.dma_start(out=sendH[hf][r0_:r0_ + 23, :], in_=zt[:, 0:NSH[hf] * 128]),
                     r=[T_zt], w=[tz], dma=True)
        NI = 3
        sets = [make_mixer_set(i) for i in range(NI)]
        done = set()
        state = {"agA": False}

        def inst_gen(i):
            for lb in range(i, 17, NI):
                yield from mixer_block(lb, sets[i])
                done.add(lb)
                if not state["agA"] and all(x in done for x in range(HS + 1)):
                    state["agA"] = True
                    issue_ags(0)

        gens = [inst_gen(i) for i in range(NI)]
        alive = []
        STAG = 6
        step = 0
        started = 0
        while alive or started < NI:
            if started < NI and step == started * STAG:
                alive.append(started)
                started += 1
            for i in list(alive):
                try:
                    next(gens[i])
                except StopIteration:
                    alive.remove(i)
            step += 1
        S.barrier()
        T_g2 = Trk()
        issue_ags(1, sel=(0, 1, 2, 3))
        S.op("pool", lambda: nc.gpsimd.collective_compute("AllGather", ALU.bypass, replica_groups=RG, ins=[tsend[:, :]], outs=[tgath[:, :]]),
             r=list(send_trks), w=[T_g2], dma=True, cc=True)
        issue_ags(1, sel=tuple(range(4, 16)))

    with ExitStack() as sbk:
        psOT = sbk.enter_context(nc.psum_tensor("psOT", [128, 1024], F32))
        psS_t = sbk.enter_context(nc.psum_tensor("psS", [128, 3, 1024], F32))
        psS = [(psS_t[:, i, :], Trk()) for i in range(3)]
        mask = sb("mask", [128, 8, 128], F32, sbk)
        T_mask = Trk()
        S.op("sp", lambda: nc.sync.dma_start(out=mask[:], in_=mask_d[:, :, :]), w=[T_mask], dma=True)
        T_pOT = [Trk() for _ in range(2)]
        kT = [(sb("kT%d" % i, [96, 65 * 128], BF16, sbk), Trk()) for i in range(2)]
        vS = [(sb("vS%d" % i, [128, 65, 128], BF16, sbk), Trk()) for i in range(2)]
        vstage = [(sb("vst%d" % i, [128, 65 * 64], BF16, sbk), Trk()) for i in range(2)]
        qT = [(sb("qT%d" % i, [96, 2048], BF16, sbk), Trk()) for i in range(2)]
        pT = [(sb("pT%d" % i, [128, 1024], BF16, sbk), Trk()) for i in range(4)]
        osb = sb("osb", [128, 2048], F32, sbk)
        T_osb = [Trk() for _ in range(2)]
        rc = sb("rc", [128, 2048], F32, sbk)
        T_rc = Trk()
        otn = [(sb("otn%d" % i, [128, 2048], BF16, sbk), Trk()) for i in range(2)]
        ones3 = sb("ones3", [128, 2048], BF16, sbk)
        T_o3 = Trk()
        S.op("pool", lambda: nc.gpsimd.memset(ones3[:], 1.0), w=[T_o3])
        for i_, (vt, tv_) in enumerate(vS):
            oo = 64 if i_ == 0 else 0
            S.op("pool", lambda vt=vt, oo=oo: nc.gpsimd.memset(vt[:, :, oo:oo + 64], 1.0), w=[tv_])
            S.op("pool", lambda vt=vt, oo=oo: nc.gpsimd.memset(vt[0:112, 64, oo:oo + 64], 0.0), w=[tv_])
        S.defer_begin()
        TT = sb("TT", [128, 4, NSLOT, 8], F32, sbk)
        T_TT = Trk()
        for r_ in range(4):
            src = tgath[r_:r_ + 1, :].partition_broadcast(128)
            S.op("sp", lambda r_=r_, src=src: nc.sync.dma_start(out=TT[:, r_, :, :].rearrange("p s h -> p (s h)"), in_=src), r=[T_g2], w=[T_TT], dma=True)
        TM = sb("TM", [128, 8], F32, sbk)
        T_TM = Trk()
        S.op("sp", lambda: nc.sync.dma_start(out=TM[:], in_=mt_s[0:1, :].partition_broadcast(128)), w=[T_TM], dma=True)
        RP = sb("RP", [128, NSLOT + 1, 8], F32, sbk)
        T_RP = Trk()
        OFF = sb("OFF", [128, 4, NSLOT, 8], F32, sbk)
        T_OFF = Trk()
        S.op("dve", lambda: nc.vector.tensor_copy(out=RP[:, 0, :], in_=TM[:]), r=[T_TM], w=[T_RP])
        for s_ in range(NSLOT):
            S.op("dve", lambda s_=s_: nc.vector.tensor_tensor(out=OFF[:, 0, s_, :], in0=TT[:, 0, s_, :], in1=TT[:, 1, s_, :], op=ALU.add), r=[T_TT], w=[T_OFF])
            S.op("dve", lambda s_=s_: nc.vector.tensor_tensor(out=OFF[:, 1, s_, :], in0=TT[:, 2, s_, :], in1=TT[:, 3, s_, :], op=ALU.add), r=[T_TT], w=[T_OFF])
            S.op("dve", lambda s_=s_: nc.vector.tensor_tensor(out=OFF[:, 0, s_, :], in0=OFF[:, 0, s_, :], in1=OFF[:, 1, s_, :], op=ALU.add), r=[T_OFF], w=[T_OFF])
            S.op("dve", lambda s_=s_: nc.vector.tensor_tensor(out=RP[:, s_ + 1, :], in0=RP[:, s_, :], in1=OFF[:, 0, s_, :], op=ALU.add), r=[T_OFF, T_RP], w=[T_RP])
        for s_ in range(NSLOT):
            order = [0, 1, 2, 3] if s_ % 2 == 0 else [3, 2, 1, 0]
            prev = None
            for c_ in order:
                if prev is None:
                    S.op("dve", lambda s_=s_, c_=c_: nc.vector.tensor_copy(out=OFF[:, c_, s_, :], in_=RP[:, s_, :]), r=[T_RP, T_OFF], w=[T_OFF])
                else:
                    S.op("dve", lambda s_=s_, c_=c_, p_=prev: nc.vector.tensor_tensor(out=OFF[:, c_, s_, :], in0=OFF[:, p_, s_, :], in1=TT[:, p_, s_, :], op=ALU.add),
                         r=[T_OFF, T_TT], w=[T_OFF])
                prev = c_
        OWN = sb("OWN", [128, NSLOT, 8], F32, sbk)
        T_OWN = Trk()
        S.op("dve", lambda: nc.vector.tensor_scalar(out=OWN[:], in0=OFF[:, 0], scalar1=gv[:, G_SEL:G_SEL + 1], scalar2=None, op0=ALU.mult),
             r=[T_OFF, T_gv], w=[T_OWN])
        for c_ in range(1, 4):
            S.op("dve", lambda c_=c_: nc.vector.scalar_tensor_tensor(out=OWN[:], in0=OFF[:, c_], scalar=gv[:, G_SEL + c_:G_SEL + c_ + 1], in1=OWN[:],
                                                                     op0=ALU.mult, op1=ALU.add), r=[T_OFF, T_gv, T_OWN], w=[T_OWN])
        S.op("dve", lambda: nc.vector.tensor_scalar(out=OWN[:], in0=OWN[:], scalar1=-1.0, scalar2=None, op0=ALU.mult), r=[T_OWN], w=[T_OWN])
        pidx = sb("pidx", [128, 1], F32, sbk)
        T_pidx = Trk()
        S.op("pool", lambda: nc.gpsimd.iota(pidx[:], pattern=[[0, 1]], base=0, channel_multiplier=1, allow_small_or_imprecise_dtypes=True), w=[T_pidx])
        msel = sb("msel", [128, 3], F32, sbk)
        T_msel = Trk()
        for r_ in range(3):
            S.op("dve", lambda r_=r_: nc.vector.tensor_scalar(out=msel[:, r_:r_ + 1], in0=pidx[:], scalar1=float(64 + r_), scalar2=None, op0=ALU.is_equal),
                 r=[T_pidx], w=[T_msel])
        hb = sb("hb", [128, NSLOT, 8], BF16, sbk)
        T_hb = Trk()
        r1 = sb("r1", [128, NSLOT, 8], F32, sbk)
        T_r1 = Trk()
        acc = sb("acc", [128, NSLOT, 8], F32, sbk)
        T_acc = Trk()
        OFFP = sb("OFFP", [128, 8, NSLOT], BF16, sbk)
        T_OFFP = Trk()
        S.op("dve", lambda: nc.vector.tensor_copy(out=r1[:], in_=OWN[:]), r=[T_OWN], w=[T_r1])
        for r_ in range(3):
            S.op("dve", lambda: nc.vector.tensor_copy(out=hb[:], in_=r1[:]), r=[T_r1], w=[T_hb])
            if r_ == 0:
                S.op("dve", lambda: nc.vector.tensor_scalar(out=acc[:], in0=hb[:], scalar1=msel[:, 0:1], scalar2=None, op0=ALU.mult),
                     r=[T_hb, T_msel], w=[T_acc])
            else:
                S.op("dve", lambda r_=r_: nc.vector.scalar_tensor_tensor(out=acc[:], in0=hb[:], scalar=msel[:, r_:r_ + 1], in1=acc[:], op0=ALU.mult, op1=ALU.add),
                     r=[T_hb, T_msel, T_acc], w=[T_acc])
            if r_ < 2:
                S.op("dve", lambda: nc.vector.tensor_tensor(out=r1[:], in0=r1[:], in1=hb[:], op=ALU.subtract), r=[T_r1, T_hb], w=[T_r1])
        S.op("dve", lambda: nc.vector.tensor_copy(out=OFFP[:], in_=acc[:].rearrange("p s h -> p h s")), r=[T_acc], w=[T_OFFP])

        off_ops = S.defer_end()

        def load_head(hd):
            fox = hd >= 8
            hh = hd % 8
            dk = 73 if fox else 96
            kt, tk = kT[hd % 2]
            vt, tv_ = vS[hd % 2]
            vst, tvst = vstage[hd % 2]
            qt, tq = qT[hd % 2]
            h0_, ng_, k_ = AGOF[hd]
            for r_ in range(4):
                gb_ = h0_ * 640 + r_ * ng_ * 160 + k_ * 160
                for hf in range(2):
                    ns_ = NSH[hf]
                    s0_ = 0 if hf == 0 else HS
                    kc = (r_ * 16 + s0_) * 128
                    S.op("sp", lambda kc=kc, gb_=gb_, hf=hf, ns_=ns_: nc.sync.dma_start(out=kt[0:dk, kc:kc + ns_ * 128], in_=gathH[hf][gb_:gb_ + dk, :]),
                         r=[T_ag[hf][hd]], w=[tk], dma=True, join=("ld", hd))
                    vsrc = gathH[hf][gb_ + 96:gb_ + 160, :].rearrange("t2 (u c) -> (t2 u) c", u=2)
                    vc = (r_ * 16 + s0_) * 64
                    S.op("sp", lambda vc=vc, vsrc=vsrc, ns_=ns_: nc.sync.dma_start(out=vst[:, vc:vc + ns_ * 64], in_=vsrc), r=[T_ag[hf][hd]], w=[tvst], dma=True, join=("ld", hd))
            msrc = (mf_s if fox else mk_s)[hh, :, :]
            S.op("sp", lambda: nc.sync.dma_start(out=kt[0:dk, 64 * 128:65 * 128], in_=msrc), w=[tk], dma=True, join=("ld", hd))
            S.op("sp", lambda: nc.sync.dma_start(out=vst[:, 64 * 64:65 * 64], in_=mv_s[hd, :, :]), w=[tvst], dma=True, join=("ld", hd))
            vo = 0 if hd % 2 == 0 else 64
            S.op("pool", lambda: nc.gpsimd.tensor_copy(out=vt[:, :, vo:vo + 64], in_=vst[:, :].rearrange("p (k d) -> p k d", d=64)), r=[tvst], w=[tv_])
            qsrc = (qf_s if fox else qm_s)[:, :, hh, :].rearrange("s d t -> d s t")
            if fox:
                S.op("dve", lambda: nc.vector.tensor_tensor(out=qt[64:67, :].rearrange("p (s t) -> p s t", s=NSLOT),
                                                            in0=ones3[64:67, :].rearrange("p (s t) -> p s t", s=NSLOT),
                                                            in1=OFFP[64:67, hh, :].unsqueeze(2).to_broadcast([3, NSLOT, 128]), op=ALU.mult),
                     r=[T_o3, T_OFFP], w=[tq], join=("ld", hd))
                S.op("sp", lambda: nc.sync.dma_start(out=qt[0:64, :].rearrange("d (s t) -> d s t", s=NSLOT), in_=qsrc[0:64, :, :]), w=[tq], dma=True, join=("ld", hd))
                S.op("sp", lambda: nc.sync.dma_start(out=qt[67:73, :].rearrange("d (s t) -> d s t", s=NSLOT), in_=qsrc[67:73, :, :]), w=[tq], dma=True, join=("ld", hd))
            else:
                S.op("sp", lambda: nc.sync.dma_start(out=qt[0:dk, :].rearrange("d (s t) -> d s t", s=NSLOT), in_=qsrc), w=[tq], dma=True)

        load_head(0)
        jobs = []
        for hd in range(16):
            for half in range(2):
                qlo = half * 1024
                kbs = [(None, None)] + [(c_, s_) for s_ in range(8 * (half + 1)) for c_ in range(4)]
                lastc = {}
                for idx, (c_, s_) in enumerate(kbs):
                    q0 = qlo if c_ is None else max(qlo, s_ * 128)
                    for ch in range((q0 - qlo) // 512, 2):
                        lastc[ch] = idx
                for idx, (c_, s_) in enumerate(kbs):
                    jobs.append(dict(hd=hd, half=half, idx=idx, c_=c_, s_=s_, lastc=lastc, first=(idx == 0 and half == 0), last=(idx == len(kbs) - 1)))

        off_pos = [0]

        def emit_S(j, n):
            hd, half, c_, s_ = j["hd"], j["half"], j["c_"], j["s_"]
            if hd >= 2 and off_pos[0] < len(off_ops):
                S.replay(off_ops[off_pos[0]:off_pos[0] + 2])
                off_pos[0] += 2
            fox = hd >= 8
            hh = hd % 8
            dk = 73 if fox else 96
            kt, tk = kT[hd % 2]
            qt, tq = qT[hd % 2]
            qlo = half * 1024
            meta = c_ is None
            kb = 64 if meta else c_ * 16 + s_
            q0 = qlo if meta else max(qlo, s_ * 128)
            r0 = q0 - qlo
            pt_, tpt_ = pT[n % len(pT)]
            ps_, tps_ = psS[n % 3]
            chunks = [(ch, max(r0, ch * 512), (ch + 1) * 512) for ch in range(r0 // 512, 2)]
            j.update(kb=kb, r0=r0, pt=pt_, tpt=tpt_, chunks=chunks)
            for ch, a0, a1 in chunks:
                S.op("pe", lambda a0=a0, a1=a1: nc.tensor.matmul(ps_[:, a0:a1], lhsT=kt[0:dk, kb * 128:(kb + 1) * 128], rhs=qt[0:dk, qlo + a0:qlo + a1],
                                                                 start=True, stop=True), r=[tk, tq], w=[tps_])
            if not meta and s_ * 128 >= qlo:
                mi = c_ * 2 + (s_ % 2)
                S.op("dve", lambda: nc.vector.tensor_tensor(out=ps_[:, r0:r0 + 128], in0=ps_[:, r0:r0 + 128], in1=mask[:, mi, :], op=ALU.add),
                     r=[tps_, T_mask], w=[tps_])
            if fox and not meta:
                bias = OFF[:, c_, s_, hh:hh + 1]
                S.op("act", lambda: nc.scalar.activation(out=pt_[:, r0:1024], in_=ps_[:, r0:1024], func=AF.Exp, bias=bias), r=[tps_, T_OFF], w=[tpt_])
            else:
                S.op("act", lambda: nc.scalar.activation(out=pt_[:, r0:1024], in_=ps_[:, r0:1024], func=AF.Exp), r=[tps_], w=[tpt_])

        def emit_PV(j):
            hd, half, idx, lastc = j["hd"], j["half"], j["idx"], j["lastc"]
            odd = hd % 2 == 1
            vt, tv_ = vS[hd % 2]
            lhs = vt[:, j["kb"], :]
            pt_, tpt_ = j["pt"], j["tpt"]
            for ch, a0, a1 in j["chunks"]:
                S.op("pe", lambda ch=ch, a0=a0, a1=a1: nc.tensor.matmul(psOT[:, a0:a1], lhsT=lhs, rhs=pt_[:, a0:a1], start=(idx == 0), stop=(lastc[ch] == idx)),
                     r=[tv_, tpt_], w=[T_pOT[ch]])
            if not j["last"]:
                return
            qlo = half * 1024
            on, ton = otn[(hd // 2) % 2]
            lo, hi_ = (64, 128) if odd else (0, 64)
            so, sh = (0, 64) if odd else (64, 128)
            for ch in range(2):
                S.op("dve", lambda ch=ch: nc.vector.tensor_copy(out=osb[:, qlo + ch * 512:qlo + (ch + 1) * 512], in_=psOT[:, ch * 512:(ch + 1) * 512]),
                     r=[T_pOT[ch]], w=[T_osb[half]])
            S.op("dve", lambda: nc.vector.reciprocal(out=rc[lo:hi_, qlo:qlo + 1024], in_=osb[so:sh, qlo:qlo + 1024]), r=[T_osb[half]], w=[T_rc])
            S.op("dve", lambda: nc.vector.tensor_tensor(out=on[lo:hi_, qlo:qlo + 1024], in0=osb[lo:hi_, qlo:qlo + 1024], in1=rc[lo:hi_, qlo:qlo + 1024], op=ALU.mult),
                 r=[T_osb[half], T_rc], w=[ton])
            if odd and half == 1:
                S.op("pool", lambda: nc.gpsimd.dma_start(out=ot_s[hd // 2, :, :], in_=on[:, :]), r=[ton], w=[Trk()], dma=True)
            if half == 1 and hd + 2 < 16:
                load_head(hd + 2)
            if half == 1 and hd in (0, 2, 4):
                cast_phase_c(hd // 2, None)

        load_head(1)
        LA = 2
        for n in range(len(jobs) + LA):
            if n < len(jobs):
                emit_S(jobs[n], n)
            if n >= LA:
                emit_PV(jobs[n - LA])
        S.barrier()

    with ExitStack() as sc:
        PSC = make_psum(sc, "c")
        BC = make_ffn_bufs(sc, "c")
        wout = sb("wout", [128, 8, D], BF16, sc)
        T_wout = Trk()
        S.op("sp", lambda: nc.sync.dma_start(out=wout[:], in_=wout_s[:, :].rearrange("(k p) c -> p k c", p=128)), r=cast_ops["wout"], w=[T_wout], dma=True)
        load_wd(BC, w2d_s, "w2d")
        hb2 = [sb("hb2_%d" % i, [128, 4, D], F32, sc) for i in range(2)]
        T_h2 = [[Trk() for _ in range(4)] for _ in range(2)]
        otg = [(sb("otg%d" % i, [128, 8, 512], BF16, sc), Trk()) for i in range(2)]
        for g in range(4):
            hbuf2 = hb2[g % 2]
            og, tog = otg[g % 2]
            blocks = []
            S.op("sp", lambda g=g, og=og: nc.sync.dma_start(out=og[:], in_=ot_s[:, :, g * 512:(g + 1) * 512].rearrange("k p t -> p k t")), w=[tog], dma=True)
            for j in range(4):
                slot = g * 4 + j
                S.op("sp", lambda j=j, slot=slot, hbuf2=hbuf2: nc.sync.dma_start(out=hbuf2[:, j, :], in_=h1_s[slot + 1, :, :]), w=[T_h2[g % 2][j]], dma=True)
                blocks.append((hbuf2[:, j, :], T_h2[g % 2][j]))
            for j, (h, th) in enumerate(blocks):
                for half in range(2):
                    po, tpo = PSC["O"][half]
                    for k in range(8):
                        S.op("pe", lambda k=k, j=j, half=half, po=po, og=og: nc.tensor.matmul(po[:, :], lhsT=og[:, k, j * 128:(j + 1) * 128],
                                                                                               rhs=wout[:, k, half * 512:(half + 1) * 512], start=(k == 0), stop=(k == 7)),
                             r=[tog, T_wout], w=[tpo])
                    S.op("dve", lambda h=h, half=half, po=po: nc.vector.tensor_tensor(out=h[:, half * 512:(half + 1) * 512], in0=po[:, :],
                                                                                      in1=h[:, half * 512:(half + 1) * 512], op=ALU.add),
                         r=[tpo, th], w=[th])
            ffn(BC, PSC, blocks, G_FFN2, w2g_s, w2u_s, "w2g", "w2u")
            for j, (h, th) in enumerate(blocks):
                S.op("pool", lambda h=h, slot=g * 4 + j: nc.gpsimd.dma_start(out=y_d[slot, :, :], in_=h), r=[th], w=[Trk()], dma=True)
    S.emit()
    es.close()
    return nc


def _gb(c, s):
    return 4 * s + 1 + c if s % 2 == 0 else 4 * s + 4 - c


_NC = None


def kernel(x, meta_tokens, g_ffn1, w1_gate, w1_up, w1_down, g_mix, w_in, g_cq, w_uq, g_ckv, w_ukv, g_q_mla, g_k_mla,
           b_forget, g_q_fox, g_k_fox, w_out, g_ffn2, w2_gate, w2_up, w2_down):
    global _NC
    f32 = np.float32
    x = np.asarray(x, f32)
    B_, SEQ, _ = x.shape
    nc = build()
    meta_blk = np.zeros((128, D), f32)
    meta_blk[112:] = np.asarray(meta_tokens, f32)
    half = 16
    inv_freq = 1.0 / (10000.0 ** (np.arange(half, dtype=np.float64) / float(half)))
    in_maps = []
    common = {
        "meta": meta_blk,
        "w1g": np.ascontiguousarray(np.asarray(w1_gate, f32)[0]), "w1u": np.ascontiguousarray(np.asarray(w1_up, f32)[0]),
        "w1d": np.ascontiguousarray(np.asarray(w1_down, f32)[0]),
        "w2g": np.ascontiguousarray(np.asarray(w2_gate, f32)[0]), "w2u": np.ascontiguousarray(np.asarray(w2_up, f32)[0]),
        "w2d": np.ascontiguousarray(np.asarray(w2_down, f32)[0]),
        "w_in": np.ascontiguousarray(np.asarray(w_in, f32)[0]), "w_uq": np.ascontiguousarray(np.asarray(w_uq, f32)[0]),
        "w_ukv": np.ascontiguousarray(np.asarray(w_ukv, f32)[0]), "w_out": np.ascontiguousarray(np.asarray(w_out, f32)[0]),
    }
    gparts = [g_ffn1, g_mix, g_ffn2, g_cq, g_ckv, g_q_mla, g_k_mla, g_q_fox, g_k_fox, b_forget]
    grow = np.concatenate([np.asarray(a, f32)[0] for a in gparts])
    tri_m = (np.arange(128)[:, None] <= np.arange(128)[None, :]).astype(f32)
    for core in range(8):
        b, c = core // 4, core % 4
        gbs = [_gb(c, s) for s in range(NSLOT)]
        xl = np.stack([x[b, (g - 1) * 128:g * 128, :] for g in gbs])
        sel = np.zeros(4, f32)
        sel[c] = 1.0
        gvec = np.tile(np.concatenate([grow, sel])[None, :], (128, 1)).astype(f32)
        cs = np.zeros((128, 17, 32), f32)
        for lb in range(17):
            g = 0 if lb == 0 else gbs[lb - 1]
            idx = g * 128 + np.arange(128)
            pos = np.maximum(idx - 112, 0).astype(np.float64)
            ang = pos[:, None] * inv_freq[None, :]
            cs[:, lb, 0:16] = np.cos(ang).astype(f32)
            cs[:, lb, 16:32] = np.sin(ang).astype(f32)
        masks = np.zeros((128, 8, 128), f32)
        for cp in range(4):
            for par in range(2):
                if cp == c:
                    m = (tri_m - 1.0) * NEGM
                elif (par == 0 and cp < c) or (par == 1 and cp > c):
                    m = np.zeros((128, 128), f32)
                else:
                    m = np.full((128, 128), -NEGM, f32)
                masks[:, cp * 2 + par, :] = m
        d = dict(common)
        d.update({"x": np.ascontiguousarray(xl), "gvec": gvec, "cossin": cs, "masks": masks})
        in_maps.append(d)
    res = run_bass_kernel_spmd(nc, in_maps, core_ids=list(range(8)))
    if os.environ.get("KDEBUG"):
        global DBG
        DBG = res.results
    out = np.zeros((B_, SEQ, D), f32)
    for core in range(8):
        b, c = core // 4, core % 4
        y = np.asarray(res.results[core]["y"], f32)
        for s in range(NSLOT):
            g = _gb(c, s)
            out[b, (g - 1) * 128:g * 128, :] = y[s]
    return out
```

```python
from contextlib import ExitStack
import os
import numpy as np
import concourse.bass as bass
import concourse.mybir as mybir
from concourse.bass_utils import run_bass_kernel_spmd

F32 = mybir.dt.float32
BF16 = mybir.dt.bfloat16
ALU = mybir.AluOpType
AF = mybir.ActivationFunctionType

D = 1024
DFF = 2816
NF = 22
DIN = 1960
NSLOT = 16
HS = 6
EPS = 1e-6
NEGM = 30000.0
G_FFN1, G_MIX, G_FFN2, G_CQ, G_CKV, G_QM, G_KM, G_QF, G_KF, G_B, G_SEL, G_END = (
    0, 1024, 2048, 3072, 3328, 3456, 3552, 3648, 3712, 3776, 3784, 3788)
R_KM, R_KF, R_V, R_END = 0, 768, 768 + 584, 768 + 584 + 1024


class Trk:
    __slots__ = ("ws", "rs", "grp")

    def __init__(self):
        self.ws = []
        self.rs = []
        self.grp = None


class Op:
    __slots__ = ("eng", "fn", "deps", "dma", "sig", "cnt", "sem", "val", "prev")


class Sched:
    def __init__(self, nc, es):
        self.nc = nc
        self.engs = {"pe": nc.tensor, "act": nc.scalar, "dve": nc.vector, "pool": nc.gpsimd, "sp": nc.sync}
        self.ops = {e: [] for e in self.engs}
        self.sems = {e: es.enter_context(nc.semaphore("s_" + e)) for e in self.engs}
        self.dsems = {q: [es.enter_context(nc.semaphore("d_%s%d" % (q, i))) for i in range(n)]
                      for q, n in (("sp", 24), ("pool", 12), ("act", 12))}
        self.dcount = {"sp": 0, "pool": 0, "act": 0}
        self.es = es
        self.ncc = 0
        self.last = {}

    def op(self, eng, fn, r=(), w=(), dma=False, cc=False, join=None):
        o = Op()
        o.eng, o.fn, o.dma, o.sig, o.cnt, o.prev = eng, fn, dma, False, 0, None
        deps = []
        for t in r:
            for x in t.ws:
                deps.append((x, True))
        for t in w:
            if not (join is not None and t.grp == join):
                for x in t.ws:
                    deps.append((x, False))
            for x in t.rs:
                deps.append((x, False))
        o.deps = []
        for d, raw in deps:
            if d is o:
                continue
            if not d.dma and d.eng == eng:
                if eng in ("pe", "sp"):
                    continue
            if d not in o.deps:
                o.deps.append(d)
                d.sig = True
        for t in w:
            if join is not None and t.grp == join:
                t.ws.append(o)
            else:
                t.ws = [o]
            t.grp = join
            t.rs = []
        for t in r:
            if o not in t.rs:
                t.rs.append(o)
        if cc:
            o.sem = self.es.enter_context(self.nc.semaphore("cc%d" % self.ncc))
            self.ncc += 1
            o.val = 1
        elif dma:
            n = self.dcount[eng]
            self.dcount[eng] = n + 1
            pool = self.dsems[eng]
            o.sem = pool[n % len(pool)]
            o.val = 16 * (n // len(pool) + 1)
            key = (eng, n % len(pool))
            o.prev = self.last.get(key)
            self.last[key] = o
        self.ops[eng].append(o)
        return o

    def barrier(self):
        dm = {}
        for q in ("sp", "pool", "act"):
            for o in self.ops[q]:
                if o.dma:
                    dm[id(o.sem)] = o
        lasts = {}
        for e2 in self.engs:
            for p in reversed(self.ops[e2]):
                if not p.dma and p.fn is not None:
                    lasts[e2] = p
                    break
        for e in self.engs:
            o = Op()
            o.eng, o.fn, o.dma, o.sig, o.cnt, o.prev = e, None, False, False, 0, None
            o.deps = []
            for e2, p in lasts.items():
                if e2 != e:
                    p.sig = True
                    o.deps.append(p)
            o.deps.extend(dm.values())
            self.ops[e].append(o)

    def emit(self):
        for e, lst in self.ops.items():
            c = 0
            for o in lst:
                if o.sig and not o.dma and o.fn is not None:
                    c += 1
                    o.cnt = c
        for e, lst in self.ops.items():
            eng = self.engs[e]
            seen = {}
            dseen = {}

            def wait_dma(d):
                k = id(d.sem)
                if dseen.get(k, 0) < d.val:
                    eng.wait_ge(d.sem, d.val)
                    dseen[k] = d.val

            for o in lst:
                for d in o.deps:
                    if d.dma:
                        wait_dma(d)
                    elif d.fn is None:
                        continue
                    else:
                        if seen.get(d.eng, 0) < d.cnt:
                            eng.wait_ge(self.sems[d.eng], d.cnt)
                            seen[d.eng] = d.cnt
                if o.fn is None:
                    continue
                if o.dma and o.prev is not None:
                    wait_dma(o.prev)
                ins = o.fn()
                if o.dma and o.val == 1:
                    ins.then_inc(o.sem)
                elif o.dma:
                    ins.then_inc(o.sem, 16)
                elif o.sig:
                    ins.then_inc(self.sems[e], 1)
            if e in self.dsems:
                for o in lst:
                    if o.dma:
                        wait_dma(o)


def build():
    nc = bass.Bass("TRN2", target_bir_lowering=False)
    es = ExitStack()
    S = Sched(nc, es)

    def din(name, shape, dt=F32):
        return nc.dram_tensor(name, shape, dt, kind="ExternalInput")

    x_d = din("x", [NSLOT, 128, D])
    meta_d = din("meta", [128, D])
    gv_d = din("gvec", [128, G_END])
    cs_d = din("cossin", [128, 17, 32])
    mask_d = din("masks", [128, 8, 128])
    w1g_d, w1u_d, w1d_d = din("w1g", [D, DFF]), din("w1u", [D, DFF]), din("w1d", [DFF, D])
    w2g_d, w2u_d, w2d_d = din("w2g", [D, DFF]), din("w2u", [D, DFF]), din("w2d", [DFF, D])
    win_d, wuq_d, wukv_d, wout_d = din("w_in", [D, DIN]), din("w_uq", [256, 768]), din("w_ukv", [128, 1024]), din("w_out", [D, D])
    y_d = nc.dram_tensor("y", [NSLOT, 128, D], F32, kind="ExternalOutput")

    def scr(name, shape, dt=BF16):
        if os.environ.get("KDEBUG") and name in ("h_s", "qm_s", "qf_s", "mk_s", "mf_s", "mv_s", "mt_s", "ot_s"):
            return nc.dram_tensor(name, shape, dt, kind="ExternalOutput")
        return nc.dram_tensor(name, shape, dt)

    w1g_s, w1u_s, w1d_s = scr("w1g_s", [11, 128, 8, 256]), scr("w1u_s", [11, 128, 8, 256]), scr("w1d_s", [DFF, D])
    w2g_s, w2u_s, w2d_s = scr("w2g_s", [11, 128, 8, 256]), scr("w2u_s", [11, 128, 8, 256]), scr("w2d_s", [DFF, D])
    win_s, wuq_s, wukv_s, wout_s = scr("win_s", [D, DIN]), scr("wuq_s", [256, 768]), scr("wukv_s", [128, 1024]), scr("wout_s", [D, D])
    h_s = scr("h_s", [NSLOT, 128, D], F32)
    qm_s = scr("qm_s", [NSLOT, 96, 8, 128])
    qf_s = scr("qf_s", [NSLOT, 73, 8, 128])
    mk_s = scr("mk_s", [8, 96, 128])
    mf_s = scr("mf_s", [8, 73, 128])
    mv_s = scr("mv_s", [16, 128, 64])
    mt_s = scr("mt_s", [1, 8], F32)
    ot_s = scr("ot_s", [8, 128, 2048])
    NSH = [HS, NSLOT - HS]
    sendH = [scr("sendA", [16 * 160, NSH[0] * 128]), scr("sendB", [16 * 160, NSH[1] * 128])]
    gathH = [scr("gathA", [16 * 640, NSH[0] * 128]), scr("gathB", [16 * 640, NSH[1] * 128])]
    tsend = scr("tsend", [1, 128], F32)
    tgath = scr("tgath", [4, 128], F32)

    KD = bool(os.environ.get("KDEBUG"))
    if KD:
        dbg_osb = nc.dram_tensor("dbg_osb", [16, 128, 2048], F32, kind="ExternalOutput")
        dbg_g0 = nc.dram_tensor("dbg_g0", [2, 640, 2048], BF16, kind="ExternalOutput")
        dbg_g1 = nc.dram_tensor("dbg_g1", [2, 640, 2048], BF16, kind="ExternalOutput")

    def sb(name, shape, dt=F32, stack=es):
        return stack.enter_context(nc.sbuf_tensor(name, shape, dt))

    T_cast = {}

    gv = sb("gv", [128, G_END])
    T_gv = Trk()
    S.op("sp", lambda: nc.sync.dma_start(out=gv[:], in_=gv_d[:, :]), w=[T_gv], dma=True)
    cs = sb("cs", [128, 17, 32])
    T_cs = Trk()
    S.op("sp", lambda: nc.sync.dma_start(out=cs[:], in_=cs_d[:, :, :]), w=[T_cs], dma=True)
    ident = sb("ident", [128, 128], BF16)
    T_id = Trk()
    S.op("pool", lambda: nc.gpsimd.memset(ident[:], 0.0), w=[T_id])
    S.op("pool", lambda: nc.gpsimd.affine_select(out=ident[:], in_=ident[:], pattern=[[-1, 128]], compare_op=ALU.not_equal,
                                                 fill=1.0, base=0, channel_multiplier=1), r=[T_id], w=[T_id])
    tri = sb("tri", [128, 128], F32)
    T_tri = Trk()
    S.op("pool", lambda: nc.gpsimd.memset(tri[:], 1.0), w=[T_tri])
    S.op("pool", lambda: nc.gpsimd.affine_select(out=tri[:], in_=tri[:], pattern=[[1, 128]], compare_op=ALU.is_ge,
                                                 fill=0.0, base=0, channel_multiplier=-1), r=[T_tri], w=[T_tri])
    S.op("dve", lambda: nc.vector.tensor_scalar(out=gv[:, G_QM:G_QM + 96], in0=gv[:, G_QM:G_QM + 96], scalar1=float(96 ** -0.5),
                                                scalar2=None, op0=ALU.mult), r=[T_gv], w=[T_gv])
    S.op("dve", lambda: nc.vector.tensor_scalar(out=gv[:, G_QF:G_QF + 64], in0=gv[:, G_QF:G_QF + 64], scalar1=0.125,
                                                scalar2=None, op0=ALU.mult), r=[T_gv], w=[T_gv])

    cast_ops = {}

    def cast_all(dst, src, rows, key, gate=None):
        lst = []
        for i_, r0 in enumerate(range(0, rows, 256)):
            r1 = min(rows, r0 + 256)
            t = Trk()
            S.op("pool", lambda d=dst, s=src, a=r0, b=r1: nc.gpsimd.dma_start(out=d[a:b, :], in_=s[a:b, :]),
                 r=[gate[(2 * i_) % NF]] if gate else [], w=[t], dma=True)
            lst.append(t)
        cast_ops[key] = lst

    def cast_cols(dst, src, key, fp, gate=None):
        t = Trk()
        S.op("pool", lambda: nc.gpsimd.dma_start(out=dst[fp, :, :, :].rearrange("p k c -> k p c"),
                                                 in_=src[:, fp * 256:(fp + 1) * 256].rearrange("(k p) c -> k p c", p=128)),
             r=[gate[(2 * fp) % NF]] if gate else [], w=[t], dma=True)
        cast_ops.setdefault(key, []).append(t)

    for fp in range(11):
        cast_cols(w1g_s, w1g_d, "w1g", fp)
        cast_cols(w1u_s, w1u_d, "w1u", fp)
    cast_all(w1d_s, w1d_d, DFF, "w1d")
    cast_all(win_s, win_d, D, "win")
    cast_all(wuq_s, wuq_d, 256, "wuq")
    cast_all(wukv_s, wukv_d, 128, "wukv")
    def cast_phase_c(part, gate):
        if part == 0:
            cast_all(wout_s, wout_d, D, "wout", gate)
            for fp in range(5):
                cast_cols(w2g_s, w2g_d, "w2g", fp, gate)
                cast_cols(w2u_s, w2u_d, "w2u", fp, gate)
        elif part == 1:
            for fp in range(5, 11):
                cast_cols(w2g_s, w2g_d, "w2g", fp, gate)
                cast_cols(w2u_s, w2u_d, "w2u", fp, gate)
        else:
            cast_all(w2d_s, w2d_d, DFF, "w2d", gate)

    def make_ffn_bufs(stack, tg, ntmax=512):
        B = {}
        B["wd"] = sb("wd" + tg, [128, NF, D], BF16, stack)
        B["T_wd"] = Trk()
        B["slab"] = [(sb("sg%d" % i + tg, [128, 8, 256], BF16, stack), sb("su%d" % i + tg, [128, 8, 256], BF16, stack), Trk()) for i in range(2)]
        B["xn"] = [(sb("xn%d" % i + tg, [128, D], BF16, stack), Trk()) for i in range(2)]
        B["xT"] = sb("xT" + tg, [128, 8, ntmax], BF16, stack)
        B["T_xT"] = Trk()
        B["aT"] = sb("aT" + tg, [128, NF, ntmax], BF16, stack)
        B["T_aT"] = [Trk() for _ in range(NF)]
        B["sg"] = [(sb("sgl%d" % i + tg, [128, ntmax], F32, stack), Trk()) for i in range(2)]
        B["junk"] = sb("junk" + tg, [128, D], BF16, stack)
        B["T_junk"] = Trk()
        B["ss"] = sb("ss" + tg, [128, 8], F32, stack)
        B["T_ss"] = Trk()
        B["rs"] = sb("rs" + tg, [128, 8], F32, stack)
        B["T_rs"] = Trk()
        B["T_ss2"] = Trk()
        B["T_rs2"] = Trk()
        B["nslab"] = 0
        return B

    def load_wd(B, wd_s, key):
        for f0 in range(0, NF, 11):
            S.op("sp", lambda a=f0: nc.sync.dma_start(out=B["wd"][:, a:a + 11, :],
                                                      in_=wd_s[a * 128:(a + 11) * 128, :].rearrange("(f p) c -> p f c", p=128)),
                 r=cast_ops[key], w=[B["T_wd"]], dma=True)

    def rms_scale(B, PS, blocks, g_off, xT=None, T_xT=None, so=0):
        nb = len(blocks)
        if xT is None:
            xT, T_xT = B["xT"], B["T_xT"]
        ss_, rs_ = B["ss"][:, so:so + 4], B["rs"][:, so:so + 4]
        T_ss_, T_rs_ = (B["T_ss"], B["T_rs"]) if so == 0 else (B["T_ss2"], B["T_rs2"])
        for j, (h, th) in enumerate(blocks):
            S.op("act", lambda h=h, j=j: nc.scalar.activation(out=B["junk"][:], in_=h, func=AF.Square, accum_out=ss_[:, j:j + 1]),
                 r=[th], w=[B["T_junk"], T_ss_])
        S.op("act", lambda: nc.scalar.activation(out=rs_[:, 0:nb], in_=ss_[:, 0:nb], func=AF.Ln, scale=1.0 / D, bias=EPS),
             r=[T_ss_], w=[T_rs_])
        S.op("act", lambda: nc.scalar.activation(out=rs_[:, 0:nb], in_=rs_[:, 0:nb], func=AF.Exp, scale=-0.5),
             r=[T_rs_], w=[T_rs_])
        for j, (h, th) in enumerate(blocks):
            xn, txn = B["xn"][j % len(B["xn"])]
            S.op("dve", lambda h=h, j=j, xn=xn: nc.vector.scalar_tensor_tensor(out=xn[:], in0=h, scalar=rs_[:, j:j + 1],
                                                                                 in1=gv[:, g_off:g_off + D], op0=ALU.mult, op1=ALU.mult),
                 r=[th, T_rs_, T_gv], w=[txn])
            pt, tpt = PS["T"][j % 2]
            for k in range(8):
                S.op("pe", lambda k=k, xn=xn, pt=pt: nc.tensor.transpose(out=pt[:, k * 128:(k + 1) * 128], in_=xn[:, k * 128:(k + 1) * 128],
                                                                           identity=ident[:]),
                     r=[txn, T_id], w=[tpt])
            S.op("act", lambda j=j, pt=pt: nc.scalar.copy(out=xT[:, :, j * 128:(j + 1) * 128],
                                                          in_=pt[:, :].rearrange("p (k t) -> p k t", k=8)),
                 r=[tpt], w=[T_xT])

    def ffn(B, PS, blocks, g_off, wg_s, wu_s, kg, ku):
        for _ in ffn_gen(B, PS, blocks, g_off, wg_s, wu_s, kg, ku):
            pass

    def ffn_gen(B, PS, blocks, g_off, wg_s, wu_s, kg, ku):
        nb = len(blocks)
        NT = nb * 128
        rms_scale(B, PS, blocks, g_off)
        yield

        def load_slab(fp):
            sgt, sut, tsl = B["slab"][B["nslab"] % 2]
            B["nslab"] += 1
            S.op("sp", lambda: nc.sync.dma_start(out=sgt[:], in_=wg_s[fp, :, :, :]),
                 r=[cast_ops[kg][fp]], w=[tsl], dma=True)
            S.op("sp", lambda: nc.sync.dma_start(out=sut[:], in_=wu_s[fp, :, :, :]),
                 r=[cast_ops[ku][fp]], w=[tsl], dma=True)
            return sgt, sut, tsl

        nxt = load_slab(0)
        for fp in range(11):
            sgt, sut, tsl = nxt
            if fp + 1 < 11:
                nxt = load_slab(fp + 1)
            for j2 in range(2):
                f = 2 * fp + j2
                pg, tpg = PS["G"][f % len(PS["G"])]
                pu, tpu = PS["U"][f % len(PS["U"])]
                for k in range(8):
                    S.op("pe", lambda k=k, pg=pg, sgt=sgt, j2=j2: nc.tensor.matmul(pg[:, 0:NT], lhsT=sgt[:, k, j2 * 128:(j2 + 1) * 128],
                                                                                  rhs=B["xT"][:, k, 0:NT], start=(k == 0), stop=(k == 7)),
                         r=[tsl, B["T_xT"]], w=[tpg])
                for k in range(8):
                    S.op("pe", lambda k=k, pu=pu, sut=sut, j2=j2: nc.tensor.matmul(pu[:, 0:NT], lhsT=sut[:, k, j2 * 128:(j2 + 1) * 128],
                                                                                  rhs=B["xT"][:, k, 0:NT], start=(k == 0), stop=(k == 7)),
                         r=[tsl, B["T_xT"]], w=[tpu])
                sg, tsg = B["sg"][f % 2]
                S.op("act", lambda sg=sg, pg=pg: nc.scalar.activation(out=sg[:, 0:NT], in_=pg[:, 0:NT], func=AF.Silu), r=[tpg], w=[tsg])
                S.op("dve", lambda sg=sg, pu=pu, f=f: nc.vector.tensor_tensor(out=B["aT"][:, f, 0:NT], in0=sg[:, 0:NT], in1=pu[:, 0:NT], op=ALU.mult),
                     r=[tsg, tpu], w=[B["T_aT"][f]])
            yield
        if B.get("wd_pending"):
            load_wd(B, *B["wd_pending"])
            B["wd_pending"] = None
        for j, (h, th) in enumerate(blocks):
            for half in range(2):
                po, tpo = PS["O"][half]
                for f in range(NF):
                    S.op("pe", lambda f=f, j=j, half=half, po=po: nc.tensor.matmul(po[:, :], lhsT=B["aT"][:, f, j * 128:(j + 1) * 128],
                                                                                    rhs=B["wd"][:, f, half * 512:(half + 1) * 512],
                                                                                    start=(f == 0), stop=(f == NF - 1)),
                         r=[B["T_aT"][f], B["T_wd"]], w=[tpo])
                S.op("dve", lambda h=h, half=half, po=po: nc.vector.scalar_tensor_tensor(out=h[:, half * 512:(half + 1) * 512], in0=po[:, :], scalar=0.5,
                                                                                          in1=h[:, half * 512:(half + 1) * 512], op0=ALU.mult, op1=ALU.add),
                     r=[tpo, th], w=[th])
                yield

    def make_psum(stack, tg):
        psA = stack.enter_context(nc.psum_tensor("psA" + tg, [128, 6, 512], F32))
        psT = stack.enter_context(nc.psum_tensor("psT" + tg, [128, 2, 1024], BF16))
        PS = {"G": [(psA[:, 0, :], Trk()), (psA[:, 1, :], Trk())], "U": [(psA[:, 2, :], Trk()), (psA[:, 3, :], Trk())],
              "O": [(psA[:, 4, :], Trk()), (psA[:, 5, :], Trk())], "T": [(psT[:, 0, :], Trk()), (psT[:, 1, :], Trk())], "A": psA}
        return PS

    h1_s = scr("h1_s", [17, 128, D], F32)
    with ExitStack() as sa1:
        PSA1 = make_psum(sa1, "a")
        BA1 = make_ffn_bufs(sa1, "a", 512)
        BA1["wd_pending"] = (w1d_s, "w1d")
        hbA = [sb("hbufa%d" % i, [128, 4, D], F32, sa1) for i in range(2)]
        T_hbA = [[Trk() for _ in range(4)] for _ in range(2)]
        groupsA = [[0, 1, 2, 3], [4, 5, 6, 7], [8, 9, 10], [11, 12, 13], [14, 15, 16]]
        for gi, grp in enumerate(groupsA):
            blocks = []
            for j, lb in enumerate(grp):
                src = meta_d[:, :] if lb == 0 else x_d[lb - 1, :, :]
                S.op("sp", lambda j=j, src=src, gi=gi: nc.sync.dma_start(out=hbA[gi % 2][:, j, :], in_=src), w=[T_hbA[gi % 2][j]], dma=True)
                blocks.append((hbA[gi % 2][:, j, :], T_hbA[gi % 2][j]))
            ffn(BA1, PSA1, blocks, G_FFN1, w1g_s, w1u_s, "w1g", "w1u")
            for j, lb in enumerate(grp):
                S.op("pool", lambda j=j, lb=lb, gi=gi: nc.gpsimd.dma_start(out=h1_s[lb, :, :], in_=hbA[gi % 2][:, j, :]),
                     r=[T_hbA[gi % 2][j]], w=[Trk()], dma=True)
        S.barrier()

    with ExitStack() as sa:
        psA2 = sa.enter_context(nc.psum_tensor("psA2", [128, 6, 512], F32))
        psT2 = sa.enter_context(nc.psum_tensor("psT2", [128, 2, 1024], BF16))
        PST = {"T": [(psT2[:, 0, :], Trk()), (psT2[:, 1, :], Trk())]}
        win = sb("win", [128, 8, DIN], BF16, sa)
        T_win = Trk()
        wuq = sb("wuq", [128, 2, 768], BF16, sa)
        T_wuq = Trk()
        wukv = sb("wukv", [128, 1024], BF16, sa)
        T_wukv = Trk()
        for k0 in range(0, 8, 2):
            S.op("sp", lambda a=k0: nc.sync.dma_start(out=win[:, a:a + 2, :], in_=win_s[a * 128:(a + 2) * 128, :].rearrange("(k p) c -> p k c", p=128)),
                 r=cast_ops["win"], w=[T_win], dma=True, join="win")
        S.op("sp", lambda: nc.sync.dma_start(out=wuq[:], in_=wuq_s[:, :].rearrange("(k p) c -> p k c", p=128)), r=cast_ops["wuq"], w=[T_wuq], dma=True)
        S.op("sp", lambda: nc.sync.dma_start(out=wukv[:], in_=wukv_s[:, :]), r=cast_ops["wukv"], w=[T_wukv], dma=True)

        def make_mixer_set(i):
            M = {}

            def t(name, shape, dt=F32):
                M[name] = sb("%s_%d" % (name, i), shape, dt, sa)
                M["T_" + name] = Trk()
            t("h", [128, D])
            t("pj", [128, DIN])
            t("qkv", [128, 1792])
            t("cn", [128, 384], BF16)
            t("cT", [128, 3, 128], BF16)
            t("st", [128, 40])
            t("rr", [128, 40])
            t("tmp1", [128, 768])
            t("tmp2", [128, 576])
            t("rp", [128, 4, 8, 16])
            t("kr", [128, 64])
            t("Qm", [128, 8, 96], BF16)
            t("Km", [128, 8, 96], BF16)
            t("Qf", [128, 8, 80], BF16)
            t("Kf", [128, 8, 80], BF16)
            t("Vt", [128, 16, 64], BF16)
            t("flb", [128, 8])
            t("cA", [128, 8])
            t("c32", [128, 8])
            t("xTm", [128, 8, 128], BF16)
            M["stg"] = [(sb("stg%d_%d" % (k, i), [96, 8, 128], BF16, sa), Trk()) for k in range(4)]
            Bm = {"junk": sb("junk_%d" % i, [128, D], BF16, sa), "T_junk": Trk(),
                  "ss": sb("ss_%d" % i, [128, 8], F32, sa), "T_ss": Trk(), "rs": sb("rs_%d" % i, [128, 8], F32, sa), "T_rs": Trk(),
                  "T_ss2": Trk(), "T_rs2": Trk(), "xn": [(sb("xn_%d" % i, [128, D], BF16, sa), Trk())]}
            M["B"] = Bm
            M["PSM"] = [(psA2[:, 2 * i, :], Trk()), (psA2[:, 2 * i + 1, :], Trk())]
            Qf_, Kf_ = M["Qf"], M["Kf"]
            S.op("pool", lambda: nc.gpsimd.memset(Qf_[:, :, 64:67], 0.0), w=[M["T_Qf"]])
            S.op("pool", lambda: nc.gpsimd.memset(Qf_[:, :, 70:73], 1.0), w=[M["T_Qf"]])
            S.op("pool", lambda: nc.gpsimd.memset(Kf_[:, :, 64:70], 1.0), w=[M["T_Kf"]])
            return M

        send_trks = []

        def mixer_block(lb, M):
            slot = lb - 1
            BA, PSA, PSM = M["B"], PST, M["PSM"]
            h, th = M["h"][:, :], M["T_h"]
            pj, T_pj, qkv, T_qkv, cn, T_cn, cT, T_cT = M["pj"], M["T_pj"], M["qkv"], M["T_qkv"], M["cn"], M["T_cn"], M["cT"], M["T_cT"]
            st, T_st, rr, T_rr, tmp1, T_t1, tmp2, T_t2 = M["st"], M["T_st"], M["rr"], M["T_rr"], M["tmp1"], M["T_tmp1"], M["tmp2"], M["T_tmp2"]
            rp, T_rp, kr, T_kr = M["rp"], M["T_rp"], M["kr"], M["T_kr"]
            Qm, Km, Qf, Kf, Vt = M["Qm"], M["Km"], M["Qf"], M["Kf"], M["Vt"]
            T_Qm, T_Km, T_Qf, T_Kf, T_Vt = M["T_Qm"], M["T_Km"], M["T_Qf"], M["T_Kf"], M["T_Vt"]
            flb, T_flb, cA, T_cA, c32, T_c32 = M["flb"], M["T_flb"], M["cA"], M["T_cA"], M["c32"], M["T_c32"]
            xTm, T_xTm, stg = M["xTm"], M["T_xTm"], M["stg"]
            S.op("pool", lambda: nc.gpsimd.dma_start(out=h, in_=h1_s[lb, :, :]), w=[th], dma=True)
            rms_scale(BA, PSA, [(h, th)], G_MIX, xTm, T_xTm)
            yield
            banks = [PSM[0], PSM[1], PSM[0], PSM[1]]
            for cb in range(4):
                c0, c1 = cb * 512, min(DIN, (cb + 1) * 512)
                pb, tpb = banks[cb]
                for k in range(8):
                    S.op("pe", lambda k=k, pb=pb, c0=c0, c1=c1: nc.tensor.matmul(pb[:, 0:c1 - c0], lhsT=xTm[:, k, 0:128], rhs=win[:, k, c0:c1],
                                                                               start=(k == 0), stop=(k == 7)),
                         r=[T_xTm, T_win], w=[tpb])
                S.op("act", lambda pb=pb, c0=c0, c1=c1: nc.scalar.copy(out=pj[:, c0:c1], in_=pb[:, 0:c1 - c0]), r=[tpb], w=[T_pj])
                yield
            S.op("dve", lambda: nc.vector.tensor_tensor(out=flb[:], in0=pj[:, 1952:1960], in1=gv[:, G_B:G_B + 8], op=ALU.add), r=[T_pj, T_gv], w=[T_flb])
            S.op("act", lambda: nc.scalar.activation(out=flb[:], in_=flb[:], func=AF.Exp, scale=-1.0), r=[T_flb], w=[T_flb])
            S.op("act", lambda: nc.scalar.activation(out=flb[:], in_=flb[:], func=AF.Ln, bias=1.0), r=[T_flb], w=[T_flb])
            yield "S1END"
            S.op("act", lambda: nc.scalar.activation(out=BA["junk"][:, 0:256], in_=pj[:, 0:256], func=AF.Square, accum_out=st[:, 0:1]),
                 r=[T_pj], w=[BA["T_junk"], T_st])
            S.op("act", lambda: nc.scalar.activation(out=BA["junk"][:, 0:128], in_=pj[:, 256:384], func=AF.Square, accum_out=st[:, 1:2]),
                 r=[T_pj], w=[BA["T_junk"], T_st])
            S.op("act", lambda: nc.scalar.activation(out=BA["junk"][:, 0:32], in_=pj[:, 384:416], func=AF.Square, accum_out=st[:, 2:3]),
                 r=[T_pj], w=[BA["T_junk"], T_st])
            S.op("act", lambda: nc.scalar.activation(out=rr[:, 0:1], in_=st[:, 0:1], func=AF.Ln, scale=1.0 / 256, bias=EPS), r=[T_st], w=[T_rr])
            S.op("act", lambda: nc.scalar.activation(out=rr[:, 1:2], in_=st[:, 1:2], func=AF.Ln, scale=1.0 / 128, bias=EPS), r=[T_st], w=[T_rr])
            S.op("act", lambda: nc.scalar.activation(out=rr[:, 0:2], in_=rr[:, 0:2], func=AF.Exp, scale=-0.5), r=[T_rr], w=[T_rr])
            S.op("dve", lambda: nc.vector.scalar_tensor_tensor(out=cn[:, 0:256], in0=pj[:, 0:256], scalar=rr[:, 0:1], in1=gv[:, G_CQ:G_CQ + 256],
                                                               op0=ALU.mult, op1=ALU.mult), r=[T_pj, T_rr, T_gv], w=[T_cn])
            S.op("dve", lambda: nc.vector.scalar_tensor_tensor(out=cn[:, 256:384], in0=pj[:, 256:384], scalar=rr[:, 1:2], in1=gv[:, G_CKV:G_CKV + 128],
                                                               op0=ALU.mult, op1=ALU.mult), r=[T_pj, T_rr, T_gv], w=[T_cn])
            pt, tpt = PSA["T"][0]
            for k in range(3):
                S.op("pe", lambda k=k: nc.tensor.transpose(out=pt[:, k * 128:(k + 1) * 128], in_=cn[:, k * 128:(k + 1) * 128], identity=ident[:]),
                     r=[T_cn, T_id], w=[tpt])
            S.op("act", lambda: nc.scalar.copy(out=cT[:, :, :], in_=pt[:, 0:384].rearrange("p (k t) -> p k t", k=3)), r=[tpt], w=[T_cT])
            yield
            pq0, tq0 = PSM[0]
            pq1, tq1 = PSM[1]
            for k in range(2):
                S.op("pe", lambda k=k: nc.tensor.matmul(pq0[:, 0:512], lhsT=cT[:, k, :], rhs=wuq[:, k, 0:512], start=(k == 0), stop=(k == 1)),
                     r=[T_cT, T_wuq], w=[tq0])
            for k in range(2):
                S.op("pe", lambda k=k: nc.tensor.matmul(pq1[:, 0:256], lhsT=cT[:, k, :], rhs=wuq[:, k, 512:768], start=(k == 0), stop=(k == 1)),
                     r=[T_cT, T_wuq], w=[tq1])
            S.op("act", lambda: nc.scalar.copy(out=qkv[:, 0:512], in_=pq0[:, 0:512]), r=[tq0], w=[T_qkv])
            S.op("act", lambda: nc.scalar.copy(out=qkv[:, 512:768], in_=pq1[:, 0:256]), r=[tq1], w=[T_qkv])
            S.op("pe", lambda: nc.tensor.matmul(pq0[:, 0:512], lhsT=cT[:, 2, :], rhs=wukv[:, 0:512], start=True, stop=True), r=[T_cT, T_wukv], w=[tq0])
            S.op("pe", lambda: nc.tensor.matmul(pq1[:, 0:512], lhsT=cT[:, 2, :], rhs=wukv[:, 512:1024], start=True, stop=True), r=[T_cT, T_wukv], w=[tq1])
            S.op("act", lambda: nc.scalar.copy(out=qkv[:, 768:1280], in_=pq0[:, 0:512]), r=[tq0], w=[T_qkv])
            S.op("act", lambda: nc.scalar.copy(out=qkv[:, 1280:1792], in_=pq1[:, 0:512]), r=[tq1], w=[T_qkv])
            yield
            AX = mybir.AxisListType.X
            S.op("act", lambda: nc.scalar.activation(out=tmp1[:, 0:768], in_=qkv[:, 0:768], func=AF.Square), r=[T_qkv], w=[T_t1])
            S.op("dve", lambda: nc.vector.tensor_reduce(out=st[:, 8:16], in_=tmp1[:, 0:768].rearrange("p (h d) -> p h d", h=8), axis=AX, op=ALU.add),
                 r=[T_t1], w=[T_st])
            S.op("act", lambda: nc.scalar.activation(out=tmp2[:, 0:512].rearrange("p (h d) -> p h d", h=8),
                                                     in_=qkv[:, 768:1792].rearrange("p (h d) -> p h d", h=8)[:, :, 0:64], func=AF.Square), r=[T_qkv], w=[T_t2])
            S.op("dve", lambda: nc.vector.tensor_reduce(out=st[:, 16:24], in_=tmp2[:, 0:512].rearrange("p (h d) -> p h d", h=8), axis=AX, op=ALU.add),
                 r=[T_t2], w=[T_st])
            S.op("act", lambda: nc.scalar.activation(out=BA["junk"][:, 0:1024], in_=pj[:, 416:1440], func=AF.Square), r=[T_pj], w=[BA["T_junk"]])
            S.op("dve", lambda: nc.vector.tensor_reduce(out=st[:, 24:40], in_=BA["junk"][:, 0:1024].rearrange("p (h d) -> p h d", h=16), axis=AX, op=ALU.add),
                 r=[BA["T_junk"]], w=[T_st])
            S.op("dve", lambda: nc.vector.tensor_scalar(out=st[:, 16:24], in0=st[:, 16:24], scalar1=st[:, 2:3], scalar2=None, op0=ALU.add),
                 r=[T_st], w=[T_st])
            S.op("act", lambda: nc.scalar.activation(out=rr[:, 8:24], in_=st[:, 8:24], func=AF.Ln, scale=1.0 / 96, bias=EPS), r=[T_st], w=[T_rr])
            S.op("act", lambda: nc.scalar.activation(out=rr[:, 24:40], in_=st[:, 24:40], func=AF.Ln, scale=1.0 / 64, bias=EPS), r=[T_st], w=[T_rr])
            S.op("act", lambda: nc.scalar.activation(out=rr[:, 8:40], in_=rr[:, 8:40], func=AF.Exp, scale=-0.5), r=[T_rr], w=[T_rr])

            cosb = cs[:, lb, 0:16]
            sinb = cs[:, lb, 16:32]

            def bc_h(ap2, n):
                return ap2.unsqueeze(1).to_broadcast([128, 8, n])

            def bc_d(ap2, n):
                return ap2.unsqueeze(2).to_broadcast([128, 8, n])

            yield
            q3 = qkv[:, 0:768].rearrange("p (h d) -> p h d", h=8)
            t13 = tmp1[:, 0:768].rearrange("p (h d) -> p h d", h=8)
            S.op("dve", lambda: nc.vector.tensor_tensor(out=t13, in0=q3, in1=bc_d(rr[:, 8:16], 96), op=ALU.mult), r=[T_qkv, T_rr], w=[T_t1])
            S.op("dve", lambda: nc.vector.tensor_tensor(out=t13, in0=t13, in1=bc_h(gv[:, G_QM:G_QM + 96], 96), op=ALU.mult), r=[T_t1, T_gv], w=[T_t1])
            S.op("act", lambda: nc.scalar.copy(out=Qm[:, :, 0:64], in_=t13[:, :, 0:64]), r=[T_t1], w=[T_Qm])
            x1, x2 = t13[:, :, 64:80], t13[:, :, 80:96]
            S.op("dve", lambda: nc.vector.tensor_tensor(out=rp[:, 0], in0=x1, in1=bc_h(cosb, 16), op=ALU.mult), r=[T_t1, T_cs], w=[T_rp])
            S.op("dve", lambda: nc.vector.tensor_tensor(out=rp[:, 1], in0=x2, in1=bc_h(sinb, 16), op=ALU.mult), r=[T_t1, T_cs], w=[T_rp])
            S.op("dve", lambda: nc.vector.tensor_tensor(out=rp[:, 2], in0=x2, in1=bc_h(cosb, 16), op=ALU.mult), r=[T_t1, T_cs], w=[T_rp])
            S.op("dve", lambda: nc.vector.tensor_tensor(out=rp[:, 3], in0=x1, in1=bc_h(sinb, 16), op=ALU.mult), r=[T_t1, T_cs], w=[T_rp])
            S.op("dve", lambda: nc.vector.tensor_tensor(out=Qm[:, :, 64:80], in0=rp[:, 0], in1=rp[:, 1], op=ALU.subtract), r=[T_rp], w=[T_Qm])
            S.op("dve", lambda: nc.vector.tensor_tensor(out=Qm[:, :, 80:96], in0=rp[:, 2], in1=rp[:, 3], op=ALU.add), r=[T_rp], w=[T_Qm])
            yield
            kv3 = qkv[:, 768:1792].rearrange("p (h d) -> p h d", h=8)
            t23 = tmp2[:, 0:512].rearrange("p (h d) -> p h d", h=8)
            S.op("dve", lambda: nc.vector.tensor_tensor(out=t23, in0=kv3[:, :, 0:64], in1=bc_d(rr[:, 16:24], 64), op=ALU.mult), r=[T_qkv, T_rr], w=[T_t2])
            S.op("dve", lambda: nc.vector.tensor_tensor(out=Km[:, :, 0:64], in0=t23, in1=bc_h(gv[:, G_KM:G_KM + 64], 64), op=ALU.mult),
                 r=[T_t2, T_gv], w=[T_Km])
            S.op("dve", lambda: nc.vector.tensor_tensor(out=kr[:, 0:32], in0=pj[:, 384:416], in1=gv[:, G_KM + 64:G_KM + 96], op=ALU.mult),
                 r=[T_pj, T_gv], w=[T_kr])
            S.op("dve", lambda: nc.vector.tensor_tensor(out=tmp2[:, 512:528], in0=kr[:, 0:16], in1=cosb, op=ALU.mult), r=[T_kr, T_cs], w=[T_t2])
            S.op("dve", lambda: nc.vector.tensor_tensor(out=tmp2[:, 528:544], in0=kr[:, 16:32], in1=sinb, op=ALU.mult), r=[T_kr, T_cs], w=[T_t2])
            S.op("dve", lambda: nc.vector.tensor_tensor(out=tmp2[:, 544:560], in0=kr[:, 16:32], in1=cosb, op=ALU.mult), r=[T_kr, T_cs], w=[T_t2])
            S.op("dve", lambda: nc.vector.tensor_tensor(out=tmp2[:, 560:576], in0=kr[:, 0:16], in1=sinb, op=ALU.mult), r=[T_kr, T_cs], w=[T_t2])
            S.op("dve", lambda: nc.vector.tensor_tensor(out=kr[:, 32:48], in0=tmp2[:, 512:528], in1=tmp2[:, 528:544], op=ALU.subtract), r=[T_t2], w=[T_kr])
            S.op("dve", lambda: nc.vector.tensor_tensor(out=kr[:, 48:64], in0=tmp2[:, 544:560], in1=tmp2[:, 560:576], op=ALU.add), r=[T_t2], w=[T_kr])
            S.op("dve", lambda: nc.vector.tensor_tensor(out=Km[:, :, 64:96], in0=bc_h(kr[:, 32:64], 32), in1=bc_d(rr[:, 16:24], 32), op=ALU.mult),
                 r=[T_kr, T_rr], w=[T_Km])
            yield
            fq3 = pj[:, 416:928].rearrange("p (h d) -> p h d", h=8)
            fk3 = pj[:, 928:1440].rearrange("p (h d) -> p h d", h=8)
            t1f = tmp1[:, 0:512].rearrange("p (h d) -> p h d", h=8)
            S.op("dve", lambda: nc.vector.tensor_tensor(out=t1f, in0=fq3, in1=bc_d(rr[:, 24:32], 64), op=ALU.mult), r=[T_pj, T_rr], w=[T_t1])
            S.op("dve", lambda: nc.vector.tensor_tensor(out=Qf[:, :, 0:64], in0=t1f, in1=bc_h(gv[:, G_QF:G_QF + 64], 64), op=ALU.mult),
                 r=[T_t1, T_gv], w=[T_Qf])
            S.op("dve", lambda: nc.vector.tensor_tensor(out=t23, in0=fk3, in1=bc_d(rr[:, 32:40], 64), op=ALU.mult), r=[T_pj, T_rr], w=[T_t2])
            S.op("dve", lambda: nc.vector.tensor_tensor(out=Kf[:, :, 0:64], in0=t23, in1=bc_h(gv[:, G_KF:G_KF + 64], 64), op=ALU.mult),
                 r=[T_t2, T_gv], w=[T_Kf])
            yield
            S.op("act", lambda: nc.scalar.copy(out=Vt[:, 0:8, :], in_=kv3[:, :, 64:128]), r=[T_qkv], w=[T_Vt])
            S.op("act", lambda: nc.scalar.copy(out=Vt[:, 8:16, :], in_=pj[:, 1440:1952].rearrange("p (h d) -> p h d", h=8)), r=[T_pj], w=[T_Vt])
            yield
            pc, tpc = PSM[0]
            S.op("pe", lambda: nc.tensor.matmul(pc[:, 0:8], lhsT=tri[:], rhs=flb[:], start=True, stop=True), r=[T_tri, T_flb], w=[tpc])
            S.op("dve", lambda: nc.vector.tensor_copy(out=cA[:], in_=pc[:, 0:8]), r=[tpc], w=[T_cA])
            S.op("dve", lambda: nc.vector.tensor_copy(out=Kf[:, :, 70], in_=cA[:]), r=[T_cA], w=[T_Kf])
            S.op("dve", lambda: nc.vector.tensor_tensor(out=c32[:], in0=cA[:], in1=Kf[:, :, 70], op=ALU.subtract), r=[T_cA, T_Kf], w=[T_c32])
            S.op("dve", lambda: nc.vector.tensor_copy(out=Kf[:, :, 71], in_=c32[:]), r=[T_c32], w=[T_Kf])
            S.op("dve", lambda: nc.vector.tensor_tensor(out=c32[:], in0=c32[:], in1=Kf[:, :, 71], op=ALU.subtract), r=[T_c32, T_Kf], w=[T_c32])
            S.op("dve", lambda: nc.vector.tensor_copy(out=Kf[:, :, 72], in_=c32[:]), r=[T_c32], w=[T_Kf])
            S.op("dve", lambda: nc.vector.tensor_scalar(out=Qf[:, :, 67:70], in0=Kf[:, :, 70:73], scalar1=-1.0, scalar2=None, op0=ALU.mult),
                 r=[T_Kf], w=[T_Qf])
            yield
            def tr_store(src, tsrc, dk, si, dsts, q="sp"):
                pt, tpt = PSA["T"][si % 2]
                for hh in range(8):
                    S.op("pe", lambda hh=hh, pt=pt: nc.tensor.transpose(out=pt[0:dk, hh * 128:(hh + 1) * 128], in_=src[:, hh, 0:dk], identity=ident[:]),
                         r=[tsrc, T_id], w=[tpt])
                sg_, tsg_ = stg[si % 4]
                S.op("act", lambda pt=pt, sg_=sg_: nc.scalar.copy(out=sg_[0:dk, :, :], in_=pt[0:dk, :].rearrange("p (h t) -> p h t", h=8)),
                     r=[tpt], w=[tsg_])
                for dst in dsts:
                    tsv = Trk()
                    send_trks.append(tsv)
                    eng_ = {"sp": nc.sync, "act": nc.scalar, "pool": nc.gpsimd}[q]
                    S.op(q, lambda sg_=sg_, dst=dst, eng_=eng_: eng_.dma_start(out=dst, in_=sg_[0:dk, :, :]), r=[tsg_], w=[tsv], dma=True)

            hfx = 0 if slot < HS else 1
            sendX = sendH[hfx]
            nsx = NSH[hfx]
            s8 = slot - (0 if hfx == 0 else HS)

            def send_view(r0, dk):
                v = sendX[r0:r0 + 8 * 160, :].rearrange("(h r) (s t) -> r h s t", h=8, s=nsx)
                return v[0:dk, :, s8, :]

            if lb == 0:
                tr_store(Km, T_Km, 96, 0, [mk_s[:, :, :].rearrange("h d t -> d h t")])
                yield
                tr_store(Kf, T_Kf, 73, 1, [mf_s[:, :, :].rearrange("h d t -> d h t")])
                yield
                S.op("pool", lambda: nc.gpsimd.dma_start(out=mv_s[:, :, :].rearrange("h t d -> t h d"), in_=Vt[:, :, :]), r=[T_Vt], w=[Trk()], dma=True)
                S.op("pool", lambda: nc.gpsimd.dma_start(out=mt_s[0:1, :], in_=cA[127:128, :]), r=[T_cA], w=[Trk()], dma=True)
            else:
                tr_store(Qm, T_Qm, 96, 2, [qm_s[slot, :, :, :]])
                yield
                tr_store(Km, T_Km, 96, 0, [send_view(0, 96)])
                yield
                tr_store(Qf, T_Qf, 73, 3, [qf_s[slot, :, :, :]])
                yield
                tr_store(Kf, T_Kf, 73, 1, [send_view(8 * 160, 73)], q="act")
                yield
                vv = sendX[:, :].rearrange("(h r) c -> r h c", h=16)[96:160, :, :].rearrange("t2 h (u s d) -> (t2 u) h s d", u=2, s=nsx)
                tsv = Trk()
                send_trks.append(tsv)
                S.op("pool", lambda: nc.gpsimd.dma_start(out=vv[:, :, s8, :], in_=Vt[:, :, :]), r=[T_Vt], w=[tsv], dma=True)
                tv = tsend[0:1, :].rearrange("o (s h) -> o s h", s=NSLOT)
                tsv = Trk()
                send_trks.append(tsv)
                S.op("pool", lambda: nc.gpsimd.dma_start(out=tv[:, slot, :], in_=cA[127:128, :]), r=[T_cA], w=[tsv], dma=True)

        RG = [[0, 1, 2, 3], [4, 5, 6, 7]]
        T_ag = [[Trk() for _ in range(16)] for _ in range(2)]
        AGG = [(h, 1) for h in range(16)]
        AGOF = {}
        for (h0, ng) in AGG:
            for k in range(ng):
                AGOF[h0 + k] = (h0, ng, k)

        def issue_ags(hf):
            for (h0, ng) in AGG:
                S.op("pool", lambda h0=h0, ng=ng: nc.gpsimd.collective_compute("AllGather", ALU.bypass, replica_groups=RG,
                                                                               ins=[sendH[hf][h0 * 160:(h0 + ng) * 160, :]],
                                                                               outs=[gathH[hf][h0 * 640:(h0 + ng) * 640, :]]),
                     r=list(send_trks), w=[T_ag[hf][h0 + k] for k in range(ng)], dma=True, cc=True)

        zt = sb("zt", [23, NSH[1] * 128], BF16, sa)
        T_zt = Trk()
        S.op("pool", lambda: nc.gpsimd.memset(zt[:], 0.0), w=[T_zt])
        for hf in range(2):
            for hh_ in range(8):
                tz = Trk()
                send_trks.append(tz)
                r0_ = (8 + hh_) * 160 + 73
                S.op("pool", lambda hf=hf, r0_=r0_: nc.gpsimd.dma_start(out=sendH[hf][r0_:r0_ + 23, :], in_=zt[:, 0:NSH[hf] * 128]),
                     r=[T_zt], w=[tz], dma=True)
        NI = 3
        sets = [make_mixer_set(i) for i in range(NI)]
        done = set()
        state = {"agA": False}

        def inst_gen(i):
            for lb in range(i, 17, NI):
                yield from mixer_block(lb, sets[i])
                done.add(lb)
                if not state["agA"] and all(x in done for x in range(HS + 1)):
                    state["agA"] = True
                    issue_ags(0)

        gens = [inst_gen(i) for i in range(NI)]
        alive = []
        STAG = 6
        step = 0
        started = 0
        while alive or started < NI:
            if started < NI and step == started * STAG:
                alive.append(started)
                started += 1
            for i in list(alive):
                try:
                    next(gens[i])
                except StopIteration:
                    alive.remove(i)
            step += 1
        S.barrier()
        T_g2 = Trk()
        S.op("pool", lambda: nc.gpsimd.collective_compute("AllGather", ALU.bypass, replica_groups=RG, ins=[tsend[:, :]], outs=[tgath[:, :]]),
             r=list(send_trks), w=[T_g2], dma=True, cc=True)
        issue_ags(1)

    with ExitStack() as sbk:
        psOT = sbk.enter_context(nc.psum_tensor("psOT", [128, 1024], F32))
        psS_t = sbk.enter_context(nc.psum_tensor("psS", [128, 3, 1024], F32))
        psS = [(psS_t[:, i, :], Trk()) for i in range(3)]
        mask = sb("mask", [128, 8, 128], F32, sbk)
        T_mask = Trk()
        S.op("sp", lambda: nc.sync.dma_start(out=mask[:], in_=mask_d[:, :, :]), w=[T_mask], dma=True)
        T_pOT = [Trk() for _ in range(2)]
        kT = [(sb("kT%d" % i, [96, 65 * 128], BF16, sbk), Trk()) for i in range(2)]
        vS = [(sb("vS%d" % i, [128, 65, 128], BF16, sbk), Trk()) for i in range(2)]
        vstage = [(sb("vst%d" % i, [128, 65 * 64], BF16, sbk), Trk()) for i in range(2)]
        qT = [(sb("qT%d" % i, [96, 2048], BF16, sbk), Trk()) for i in range(2)]
        pT = [(sb("pT%d" % i, [128, 1024], BF16, sbk), Trk()) for i in range(4)]
        osb = sb("osb", [128, 2048], F32, sbk)
        T_osb = [Trk() for _ in range(2)]
        rc = sb("rc", [128, 2048], F32, sbk)
        T_rc = Trk()
        otn = [(sb("otn%d" % i, [128, 2048], BF16, sbk), Trk()) for i in range(2)]
        ones3 = sb("ones3", [128, 2048], BF16, sbk)
        T_o3 = Trk()
        S.op("pool", lambda: nc.gpsimd.memset(ones3[:], 1.0), w=[T_o3])
        for i_, (vt, tv_) in enumerate(vS):
            oo = 64 if i_ == 0 else 0
            S.op("pool", lambda vt=vt, oo=oo: nc.gpsimd.memset(vt[:, :, oo:oo + 64], 1.0), w=[tv_])
            S.op("pool", lambda vt=vt, oo=oo: nc.gpsimd.memset(vt[0:112, 64, oo:oo + 64], 0.0), w=[tv_])
        TT = sb("TT", [128, 4, NSLOT, 8], F32, sbk)
        T_TT = Trk()
        for r_ in range(4):
            src = tgath[r_:r_ + 1, :].partition_broadcast(128)
            S.op("sp", lambda r_=r_, src=src: nc.sync.dma_start(out=TT[:, r_, :, :].rearrange("p s h -> p (s h)"), in_=src), r=[T_g2], w=[T_TT], dma=True)
        TM = sb("TM", [128, 8], F32, sbk)
        T_TM = Trk()
        S.op("sp", lambda: nc.sync.dma_start(out=TM[:], in_=mt_s[0:1, :].partition_broadcast(128)), w=[T_TM], dma=True)
        RP = sb("RP", [128, NSLOT + 1, 8], F32, sbk)
        T_RP = Trk()
        OFF = sb("OFF", [128, 4, NSLOT, 8], F32, sbk)
        T_OFF = Trk()
        S.op("dve", lambda: nc.vector.tensor_copy(out=RP[:, 0, :], in_=TM[:]), r=[T_TM], w=[T_RP])
        for s_ in range(NSLOT):
            S.op("dve", lambda s_=s_: nc.vector.tensor_tensor(out=OFF[:, 0, s_, :], in0=TT[:, 0, s_, :], in1=TT[:, 1, s_, :], op=ALU.add), r=[T_TT], w=[T_OFF])
            S.op("dve", lambda s_=s_: nc.vector.tensor_tensor(out=OFF[:, 1, s_, :], in0=TT[:, 2, s_, :], in1=TT[:, 3, s_, :], op=ALU.add), r=[T_TT], w=[T_OFF])
            S.op("dve", lambda s_=s_: nc.vector.tensor_tensor(out=OFF[:, 0, s_, :], in0=OFF[:, 0, s_, :], in1=OFF[:, 1, s_, :], op=ALU.add), r=[T_OFF], w=[T_OFF])
            S.op("dve", lambda s_=s_: nc.vector.tensor_tensor(out=RP[:, s_ + 1, :], in0=RP[:, s_, :], in1=OFF[:, 0, s_, :], op=ALU.add), r=[T_OFF, T_RP], w=[T_RP])
        for s_ in range(NSLOT):
            order = [0, 1, 2, 3] if s_ % 2 == 0 else [3, 2, 1, 0]
            prev = None
            for c_ in order:
                if prev is None:
                    S.op("dve", lambda s_=s_, c_=c_: nc.vector.tensor_copy(out=OFF[:, c_, s_, :], in_=RP[:, s_, :]), r=[T_RP, T_OFF], w=[T_OFF])
                else:
                    S.op("dve", lambda s_=s_, c_=c_, p_=prev: nc.vector.tensor_tensor(out=OFF[:, c_, s_, :], in0=OFF[:, p_, s_, :], in1=TT[:, p_, s_, :], op=ALU.add),
                         r=[T_OFF, T_TT], w=[T_OFF])
                prev = c_
        OWN = sb("OWN", [128, NSLOT, 8], F32, sbk)
        T_OWN = Trk()
        S.op("dve", lambda: nc.vector.tensor_scalar(out=OWN[:], in0=OFF[:, 0], scalar1=gv[:, G_SEL:G_SEL + 1], scalar2=None, op0=ALU.mult),
             r=[T_OFF, T_gv], w=[T_OWN])
        for c_ in range(1, 4):
            S.op("dve", lambda c_=c_: nc.vector.scalar_tensor_tensor(out=OWN[:], in0=OFF[:, c_], scalar=gv[:, G_SEL + c_:G_SEL + c_ + 1], in1=OWN[:],
                                                                     op0=ALU.mult, op1=ALU.add), r=[T_OFF, T_gv, T_OWN], w=[T_OWN])
        S.op("dve", lambda: nc.vector.tensor_scalar(out=OWN[:], in0=OWN[:], scalar1=-1.0, scalar2=None, op0=ALU.mult), r=[T_OWN], w=[T_OWN])
        pidx = sb("pidx", [128, 1], F32, sbk)
        T_pidx = Trk()
        S.op("pool", lambda: nc.gpsimd.iota(pidx[:], pattern=[[0, 1]], base=0, channel_multiplier=1, allow_small_or_imprecise_dtypes=True), w=[T_pidx])
        msel = sb("msel", [128, 3], F32, sbk)
        T_msel = Trk()
        for r_ in range(3):
            S.op("dve", lambda r_=r_: nc.vector.tensor_scalar(out=msel[:, r_:r_ + 1], in0=pidx[:], scalar1=float(64 + r_), scalar2=None, op0=ALU.is_equal),
                 r=[T_pidx], w=[T_msel])
        hb = sb("hb", [128, NSLOT, 8], BF16, sbk)
        T_hb = Trk()
        r1 = sb("r1", [128, NSLOT, 8], F32, sbk)
        T_r1 = Trk()
        acc = sb("acc", [128, NSLOT, 8], F32, sbk)
        T_acc = Trk()
        OFFP = sb("OFFP", [128, 8, NSLOT], BF16, sbk)
        T_OFFP = Trk()
        S.op("dve", lambda: nc.vector.tensor_copy(out=r1[:], in_=OWN[:]), r=[T_OWN], w=[T_r1])
        for r_ in range(3):
            S.op("dve", lambda: nc.vector.tensor_copy(out=hb[:], in_=r1[:]), r=[T_r1], w=[T_hb])
            if r_ == 0:
                S.op("dve", lambda: nc.vector.tensor_scalar(out=acc[:], in0=hb[:], scalar1=msel[:, 0:1], scalar2=None, op0=ALU.mult),
                     r=[T_hb, T_msel], w=[T_acc])
            else:
                S.op("dve", lambda r_=r_: nc.vector.scalar_tensor_tensor(out=acc[:], in0=hb[:], scalar=msel[:, r_:r_ + 1], in1=acc[:], op0=ALU.mult, op1=ALU.add),
                     r=[T_hb, T_msel, T_acc], w=[T_acc])
            if r_ < 2:
                S.op("dve", lambda: nc.vector.tensor_tensor(out=r1[:], in0=r1[:], in1=hb[:], op=ALU.subtract), r=[T_r1, T_hb], w=[T_r1])
        S.op("dve", lambda: nc.vector.tensor_copy(out=OFFP[:], in_=acc[:].rearrange("p s h -> p h s")), r=[T_acc], w=[T_OFFP])

        def load_head(hd):
            fox = hd >= 8
            hh = hd % 8
            dk = 73 if fox else 96
            kt, tk = kT[hd % 2]
            vt, tv_ = vS[hd % 2]
            vst, tvst = vstage[hd % 2]
            qt, tq = qT[hd % 2]
            h0_, ng_, k_ = AGOF[hd]
            for r_ in range(4):
                gb_ = h0_ * 640 + r_ * ng_ * 160 + k_ * 160
                for hf in range(2):
                    ns_ = NSH[hf]
                    s0_ = 0 if hf == 0 else HS
                    kc = (r_ * 16 + s0_) * 128
                    S.op("sp", lambda kc=kc, gb_=gb_, hf=hf, ns_=ns_: nc.sync.dma_start(out=kt[0:dk, kc:kc + ns_ * 128], in_=gathH[hf][gb_:gb_ + dk, :]),
                         r=[T_ag[hf][hd]], w=[tk], dma=True, join=("ld", hd))
                    vsrc = gathH[hf][gb_ + 96:gb_ + 160, :].rearrange("t2 (u c) -> (t2 u) c", u=2)
                    vc = (r_ * 16 + s0_) * 64
                    S.op("sp", lambda vc=vc, vsrc=vsrc, ns_=ns_: nc.sync.dma_start(out=vst[:, vc:vc + ns_ * 64], in_=vsrc), r=[T_ag[hf][hd]], w=[tvst], dma=True, join=("ld", hd))
            msrc = (mf_s if fox else mk_s)[hh, :, :]
            S.op("sp", lambda: nc.sync.dma_start(out=kt[0:dk, 64 * 128:65 * 128], in_=msrc), w=[tk], dma=True, join=("ld", hd))
            S.op("sp", lambda: nc.sync.dma_start(out=vst[:, 64 * 64:65 * 64], in_=mv_s[hd, :, :]), w=[tvst], dma=True, join=("ld", hd))
            vo = 0 if hd % 2 == 0 else 64
            S.op("pool", lambda: nc.gpsimd.tensor_copy(out=vt[:, :, vo:vo + 64], in_=vst[:, :].rearrange("p (k d) -> p k d", d=64)), r=[tvst], w=[tv_])
            qsrc = (qf_s if fox else qm_s)[:, :, hh, :].rearrange("s d t -> d s t")
            if fox:
                S.op("dve", lambda: nc.vector.tensor_tensor(out=qt[64:67, :].rearrange("p (s t) -> p s t", s=NSLOT),
                                                            in0=ones3[64:67, :].rearrange("p (s t) -> p s t", s=NSLOT),
                                                            in1=OFFP[64:67, hh, :].unsqueeze(2).to_broadcast([3, NSLOT, 128]), op=ALU.mult),
                     r=[T_o3, T_OFFP], w=[tq], join=("ld", hd))
                S.op("sp", lambda: nc.sync.dma_start(out=qt[0:64, :].rearrange("d (s t) -> d s t", s=NSLOT), in_=qsrc[0:64, :, :]), w=[tq], dma=True, join=("ld", hd))
                S.op("sp", lambda: nc.sync.dma_start(out=qt[67:73, :].rearrange("d (s t) -> d s t", s=NSLOT), in_=qsrc[67:73, :, :]), w=[tq], dma=True, join=("ld", hd))
            else:
                S.op("sp", lambda: nc.sync.dma_start(out=qt[0:dk, :].rearrange("d (s t) -> d s t", s=NSLOT), in_=qsrc), w=[tq], dma=True)

        load_head(0)
        jobs = []
        for hd in range(16):
            for half in range(2):
                qlo = half * 1024
                kbs = [(None, None)] + [(c_, s_) for s_ in range(8 * (half + 1)) for c_ in range(4)]
                lastc = {}
                for idx, (c_, s_) in enumerate(kbs):
                    q0 = qlo if c_ is None else max(qlo, s_ * 128)
                    for ch in range((q0 - qlo) // 512, 2):
                        lastc[ch] = idx
                for idx, (c_, s_) in enumerate(kbs):
                    jobs.append(dict(hd=hd, half=half, idx=idx, c_=c_, s_=s_, lastc=lastc, first=(idx == 0 and half == 0), last=(idx == len(kbs) - 1)))

        def emit_S(j, n):
            hd, half, c_, s_ = j["hd"], j["half"], j["c_"], j["s_"]
            fox = hd >= 8
            hh = hd % 8
            dk = 73 if fox else 96
            kt, tk = kT[hd % 2]
            qt, tq = qT[hd % 2]
            qlo = half * 1024
            meta = c_ is None
            kb = 64 if meta else c_ * 16 + s_
            q0 = qlo if meta else max(qlo, s_ * 128)
            r0 = q0 - qlo
            pt_, tpt_ = pT[n % len(pT)]
            ps_, tps_ = psS[n % 3]
            chunks = [(ch, max(r0, ch * 512), (ch + 1) * 512) for ch in range(r0 // 512, 2)]
            j.update(kb=kb, r0=r0, pt=pt_, tpt=tpt_, chunks=chunks)
            for ch, a0, a1 in chunks:
                S.op("pe", lambda a0=a0, a1=a1: nc.tensor.matmul(ps_[:, a0:a1], lhsT=kt[0:dk, kb * 128:(kb + 1) * 128], rhs=qt[0:dk, qlo + a0:qlo + a1],
                                                                 start=True, stop=True), r=[tk, tq], w=[tps_])
            if not meta and s_ * 128 >= qlo:
                mi = c_ * 2 + (s_ % 2)
                S.op("dve", lambda: nc.vector.tensor_tensor(out=ps_[:, r0:r0 + 128], in0=ps_[:, r0:r0 + 128], in1=mask[:, mi, :], op=ALU.add),
                     r=[tps_, T_mask], w=[tps_])
            if fox and not meta:
                bias = OFF[:, c_, s_, hh:hh + 1]
                S.op("act", lambda: nc.scalar.activation(out=pt_[:, r0:1024], in_=ps_[:, r0:1024], func=AF.Exp, bias=bias), r=[tps_, T_OFF], w=[tpt_])
            else:
                S.op("act", lambda: nc.scalar.activation(out=pt_[:, r0:1024], in_=ps_[:, r0:1024], func=AF.Exp), r=[tps_], w=[tpt_])

        def emit_PV(j):
            hd, half, idx, lastc = j["hd"], j["half"], j["idx"], j["lastc"]
            odd = hd % 2 == 1
            vt, tv_ = vS[hd % 2]
            lhs = vt[:, j["kb"], :]
            pt_, tpt_ = j["pt"], j["tpt"]
            for ch, a0, a1 in j["chunks"]:
                S.op("pe", lambda ch=ch, a0=a0, a1=a1: nc.tensor.matmul(psOT[:, a0:a1], lhsT=lhs, rhs=pt_[:, a0:a1], start=(idx == 0), stop=(lastc[ch] == idx)),
                     r=[tv_, tpt_], w=[T_pOT[ch]])
            if not j["last"]:
                return
            qlo = half * 1024
            on, ton = otn[(hd // 2) % 2]
            lo, hi_ = (64, 128) if odd else (0, 64)
            so, sh = (0, 64) if odd else (64, 128)
            for ch in range(2):
                S.op("dve", lambda ch=ch: nc.vector.tensor_copy(out=osb[:, qlo + ch * 512:qlo + (ch + 1) * 512], in_=psOT[:, ch * 512:(ch + 1) * 512]),
                     r=[T_pOT[ch]], w=[T_osb[half]])
            S.op("dve", lambda: nc.vector.reciprocal(out=rc[lo:hi_, qlo:qlo + 1024], in_=osb[so:sh, qlo:qlo + 1024]), r=[T_osb[half]], w=[T_rc])
            S.op("dve", lambda: nc.vector.tensor_tensor(out=on[lo:hi_, qlo:qlo + 1024], in0=osb[lo:hi_, qlo:qlo + 1024], in1=rc[lo:hi_, qlo:qlo + 1024], op=ALU.mult),
                 r=[T_osb[half], T_rc], w=[ton])
            if odd and half == 1:
                S.op("pool", lambda: nc.gpsimd.dma_start(out=ot_s[hd // 2, :, :], in_=on[:, :]), r=[ton], w=[Trk()], dma=True)
            if half == 1 and hd + 2 < 16:
                load_head(hd + 2)
            if half == 1 and hd in (0, 2, 4):
                cast_phase_c(hd // 2, None)

        load_head(1)
        LA = 2
        for n in range(len(jobs) + LA):
            if n < len(jobs):
                emit_S(jobs[n], n)
            if n >= LA:
                emit_PV(jobs[n - LA])
        S.barrier()

    with ExitStack() as sc:
        PSC = make_psum(sc, "c")
        BC = make_ffn_bufs(sc, "c")
        wout = sb("wout", [128, 8, D], BF16, sc)
        T_wout = Trk()
        S.op("sp", lambda: nc.sync.dma_start(out=wout[:], in_=wout_s[:, :].rearrange("(k p) c -> p k c", p=128)), r=cast_ops["wout"], w=[T_wout], dma=True)
        BC["wd_pending"] = (w2d_s, "w2d")
        hb2 = [sb("hb2_%d" % i, [128, 4, D], F32, sc) for i in range(2)]
        T_h2 = [[Trk() for _ in range(4)] for _ in range(2)]
        otg = [(sb("otg%d" % i, [128, 8, 512], BF16, sc), Trk()) for i in range(2)]
        for g in range(4):
            hbuf2 = hb2[g % 2]
            og, tog = otg[g % 2]
            blocks = []
            S.op("sp", lambda g=g, og=og: nc.sync.dma_start(out=og[:], in_=ot_s[:, :, g * 512:(g + 1) * 512].rearrange("k p t -> p k t")), w=[tog], dma=True)
            for j in range(4):
                slot = g * 4 + j
                S.op("sp", lambda j=j, slot=slot, hbuf2=hbuf2: nc.sync.dma_start(out=hbuf2[:, j, :], in_=h1_s[slot + 1, :, :]), w=[T_h2[g % 2][j]], dma=True)
                blocks.append((hbuf2[:, j, :], T_h2[g % 2][j]))
            for j, (h, th) in enumerate(blocks):
                for half in range(2):
                    po, tpo = PSC["O"][half]
                    for k in range(8):
                        S.op("pe", lambda k=k, j=j, half=half, po=po, og=og: nc.tensor.matmul(po[:, :], lhsT=og[:, k, j * 128:(j + 1) * 128],
                                                                                               rhs=wout[:, k, half * 512:(half + 1) * 512], start=(k == 0), stop=(k == 7)),
                             r=[tog, T_wout], w=[tpo])
                    S.op("dve", lambda h=h, half=half, po=po: nc.vector.tensor_tensor(out=h[:, half * 512:(half + 1) * 512], in0=po[:, :],
                                                                                      in1=h[:, half * 512:(half + 1) * 512], op=ALU.add),
                         r=[tpo, th], w=[th])
            ffn(BC, PSC, blocks, G_FFN2, w2g_s, w2u_s, "w2g", "w2u")
            for j, (h, th) in enumerate(blocks):
                S.op("pool", lambda h=h, slot=g * 4 + j: nc.gpsimd.dma_start(out=y_d[slot, :, :], in_=h), r=[th], w=[Trk()], dma=True)
    S.emit()
    es.close()
    return nc


def _gb(c, s):
    return 4 * s + 1 + c if s % 2 == 0 else 4 * s + 4 - c


_NC = None


def kernel(x, meta_tokens, g_ffn1, w1_gate, w1_up, w1_down, g_mix, w_in, g_cq, w_uq, g_ckv, w_ukv, g_q_mla, g_k_mla,
           b_forget, g_q_fox, g_k_fox, w_out, g_ffn2, w2_gate, w2_up, w2_down):
    global _NC
    f32 = np.float32
    x = np.asarray(x, f32)
    B_, SEQ, _ = x.shape
    nc = build()
    meta_blk = np.zeros((128, D), f32)
    meta_blk[112:] = np.asarray(meta_tokens, f32)
    half = 16
    inv_freq = 1.0 / (10000.0 ** (np.arange(half, dtype=np.float64) / float(half)))
    in_maps = []
    common = {
        "meta": meta_blk,
        "w1g": np.ascontiguousarray(np.asarray(w1_gate, f32)[0]), "w1u": np.ascontiguousarray(np.asarray(w1_up, f32)[0]),
        "w1d": np.ascontiguousarray(np.asarray(w1_down, f32)[0]),
        "w2g": np.ascontiguousarray(np.asarray(w2_gate, f32)[0]), "w2u": np.ascontiguousarray(np.asarray(w2_up, f32)[0]),
        "w2d": np.ascontiguousarray(np.asarray(w2_down, f32)[0]),
        "w_in": np.ascontiguousarray(np.asarray(w_in, f32)[0]), "w_uq": np.ascontiguousarray(np.asarray(w_uq, f32)[0]),
        "w_ukv": np.ascontiguousarray(np.asarray(w_ukv, f32)[0]), "w_out": np.ascontiguousarray(np.asarray(w_out, f32)[0]),
    }
    gparts = [g_ffn1, g_mix, g_ffn2, g_cq, g_ckv, g_q_mla, g_k_mla, g_q_fox, g_k_fox, b_forget]
    grow = np.concatenate([np.asarray(a, f32)[0] for a in gparts])
    tri_m = (np.arange(128)[:, None] <= np.arange(128)[None, :]).astype(f32)
    for core in range(8):
        b, c = core // 4, core % 4
        gbs = [_gb(c, s) for s in range(NSLOT)]
        xl = np.stack([x[b, (g - 1) * 128:g * 128, :] for g in gbs])
        sel = np.zeros(4, f32)
        sel[c] = 1.0
        gvec = np.tile(np.concatenate([grow, sel])[None, :], (128, 1)).astype(f32)
        cs = np.zeros((128, 17, 32), f32)
        for lb in range(17):
            g = 0 if lb == 0 else gbs[lb - 1]
            idx = g * 128 + np.arange(128)
            pos = np.maximum(idx - 112, 0).astype(np.float64)
            ang = pos[:, None] * inv_freq[None, :]
            cs[:, lb, 0:16] = np.cos(ang).astype(f32)
            cs[:, lb, 16:32] = np.sin(ang).astype(f32)
        masks = np.zeros((128, 8, 128), f32)
        for cp in range(4):
            for par in range(2):
                if cp == c:
                    m = (tri_m - 1.0) * NEGM
                elif (par == 0 and cp < c) or (par == 1 and cp > c):
                    m = np.zeros((128, 128), f32)
                else:
                    m = np.full((128, 128), -NEGM, f32)
                masks[:, cp * 2 + par, :] = m
        d = dict(common)
        d.update({"x": np.ascontiguousarray(xl), "gvec": gvec, "cossin": cs, "masks": masks})
        in_maps.append(d)
    res = run_bass_kernel_spmd(nc, in_maps, core_ids=list(range(8)))
    if os.environ.get("KDEBUG"):
        global DBG
        DBG = res.results
    out = np.zeros((B_, SEQ, D), f32)
    for core in range(8):
        b, c = core // 4, core % 4
        y = np.asarray(res.results[core]["y"], f32)
        for s in range(NSLOT):
            g = _gb(c, s)
            out[b, (g - 1) * 128:g * 128, :] = y[s]
    return out
```

```python
from contextlib import ExitStack
import os
import numpy as np
import concourse.bass as bass
import concourse.mybir as mybir
from concourse.bass_utils import run_bass_kernel_spmd

F32 = mybir.dt.float32
BF16 = mybir.dt.bfloat16
ALU = mybir.AluOpType
AF = mybir.ActivationFunctionType

D = 1024
DFF = 2816
NF = 22
DIN = 1960
NSLOT = 16
HS = 6
EPS = 1e-6
NEGM = 30000.0
G_FFN1, G_MIX, G_FFN2, G_CQ, G_CKV, G_QM, G_KM, G_QF, G_KF, G_B, G_SEL, G_END = (
    0, 1024, 2048, 3072, 3328, 3456, 3552, 3648, 3712, 3776, 3784, 3788)
R_KM, R_KF, R_V, R_END = 0, 768, 768 + 584, 768 + 584 + 1024


class Trk:
    __slots__ = ("ws", "rs", "grp")

    def __init__(self):
        self.ws = []
        self.rs = []
        self.grp = None


class Op:
    __slots__ = ("eng", "fn", "deps", "dma", "sig", "cnt", "sem", "val", "prev")


class Sched:
    def __init__(self, nc, es):
        self.nc = nc
        self.engs = {"pe": nc.tensor, "act": nc.scalar, "dve": nc.vector, "pool": nc.gpsimd, "sp": nc.sync}
        self.ops = {e: [] for e in self.engs}
        self.sems = {e: es.enter_context(nc.semaphore("s_" + e)) for e in self.engs}
        self.dsems = {q: [es.enter_context(nc.semaphore("d_%s%d" % (q, i))) for i in range(n)]
                      for q, n in (("sp", 24), ("pool", 12), ("act", 12))}
        self.dcount = {"sp": 0, "pool": 0, "act": 0}
        self.es = es
        self.ncc = 0
        self.last = {}

    def op(self, eng, fn, r=(), w=(), dma=False, cc=False, join=None):
        o = Op()
        o.eng, o.fn, o.dma, o.sig, o.cnt, o.prev = eng, fn, dma, False, 0, None
        deps = []
        for t in r:
            for x in t.ws:
                deps.append((x, True))
        for t in w:
            if not (join is not None and t.grp == join):
                for x in t.ws:
                    deps.append((x, False))
            for x in t.rs:
                deps.append((x, False))
        o.deps = []
        for d, raw in deps:
            if d is o:
                continue
            if not d.dma and d.eng == eng:
                if eng in ("pe", "sp"):
                    continue
            if d not in o.deps:
                o.deps.append(d)
                d.sig = True
        for t in w:
            if join is not None and t.grp == join:
                t.ws.append(o)
            else:
                t.ws = [o]
            t.grp = join
            t.rs = []
        for t in r:
            if o not in t.rs:
                t.rs.append(o)
        if cc:
            o.sem = self.es.enter_context(self.nc.semaphore("cc%d" % self.ncc))
            self.ncc += 1
            o.val = 1
        elif dma:
            n = self.dcount[eng]
            self.dcount[eng] = n + 1
            pool = self.dsems[eng]
            o.sem = pool[n % len(pool)]
            o.val = 16 * (n // len(pool) + 1)
            key = (eng, n % len(pool))
            o.prev = self.last.get(key)
            self.last[key] = o
        self.ops[eng].append(o)
        return o

    def barrier(self):
        dm = {}
        for q in ("sp", "pool", "act"):
            for o in self.ops[q]:
                if o.dma:
                    dm[id(o.sem)] = o
        lasts = {}
        for e2 in self.engs:
            for p in reversed(self.ops[e2]):
                if not p.dma and p.fn is not None:
                    lasts[e2] = p
                    break
        for e in self.engs:
            o = Op()
            o.eng, o.fn, o.dma, o.sig, o.cnt, o.prev = e, None, False, False, 0, None
            o.deps = []
            for e2, p in lasts.items():
                if e2 != e:
                    p.sig = True
                    o.deps.append(p)
            o.deps.extend(dm.values())
            self.ops[e].append(o)

    def emit(self):
        for e, lst in self.ops.items():
            c = 0
            for o in lst:
                if o.sig and not o.dma and o.fn is not None:
                    c += 1
                    o.cnt = c
        for e, lst in self.ops.items():
            eng = self.engs[e]
            seen = {}
            dseen = {}

            def wait_dma(d):
                k = id(d.sem)
                if dseen.get(k, 0) < d.val:
                    eng.wait_ge(d.sem, d.val)
                    dseen[k] = d.val

            for o in lst:
                for d in o.deps:
                    if d.dma:
                        wait_dma(d)
                    elif d.fn is None:
                        continue
                    else:
                        if seen.get(d.eng, 0) < d.cnt:
                            eng.wait_ge(self.sems[d.eng], d.cnt)
                            seen[d.eng] = d.cnt
                if o.fn is None:
                    continue
                if o.dma and o.prev is not None:
                    wait_dma(o.prev)
                ins = o.fn()
                if o.dma and o.val == 1:
                    ins.then_inc(o.sem)
                elif o.dma:
                    ins.then_inc(o.sem, 16)
                elif o.sig:
                    ins.then_inc(self.sems[e], 1)
            if e in self.dsems:
                for o in lst:
                    if o.dma:
                        wait_dma(o)


def build():
    nc = bass.Bass("TRN2", target_bir_lowering=False)
    es = ExitStack()
    S = Sched(nc, es)

    def din(name, shape, dt=F32):
        return nc.dram_tensor(name, shape, dt, kind="ExternalInput")

    x_d = din("x", [NSLOT, 128, D])
    meta_d = din("meta", [128, D])
    gv_d = din("gvec", [128, G_END])
    cs_d = din("cossin", [128, 17, 32])
    mask_d = din("masks", [128, 8, 128])
    w1g_d, w1u_d, w1d_d = din("w1g", [D, DFF]), din("w1u", [D, DFF]), din("w1d", [DFF, D])
    w2g_d, w2u_d, w2d_d = din("w2g", [D, DFF]), din("w2u", [D, DFF]), din("w2d", [DFF, D])
    win_d, wuq_d, wukv_d, wout_d = din("w_in", [D, DIN]), din("w_uq", [256, 768]), din("w_ukv", [128, 1024]), din("w_out", [D, D])
    y_d = nc.dram_tensor("y", [NSLOT, 128, D], F32, kind="ExternalOutput")

    def scr(name, shape, dt=BF16):
        if os.environ.get("KDEBUG") and name in ("h_s", "qm_s", "qf_s", "mk_s", "mf_s", "mv_s", "mt_s", "ot_s"):
            return nc.dram_tensor(name, shape, dt, kind="ExternalOutput")
        return nc.dram_tensor(name, shape, dt)

    w1g_s, w1u_s, w1d_s = scr("w1g_s", [11, 128, 8, 256]), scr("w1u_s", [11, 128, 8, 256]), scr("w1d_s", [DFF, D])
    w2g_s, w2u_s, w2d_s = scr("w2g_s", [11, 128, 8, 256]), scr("w2u_s", [11, 128, 8, 256]), scr("w2d_s", [DFF, D])
    win_s, wuq_s, wukv_s, wout_s = scr("win_s", [D, DIN]), scr("wuq_s", [256, 768]), scr("wukv_s", [128, 1024]), scr("wout_s", [D, D])
    h_s = scr("h_s", [NSLOT, 128, D], F32)
    qm_s = scr("qm_s", [NSLOT, 96, 8, 128])
    qf_s = scr("qf_s", [NSLOT, 73, 8, 128])
    mk_s = scr("mk_s", [8, 96, 128])
    mf_s = scr("mf_s", [8, 73, 128])
    mv_s = scr("mv_s", [16, 128, 64])
    mt_s = scr("mt_s", [1, 8], F32)
    ot_s = scr("ot_s", [8, 128, 2048])
    NSH = [HS, NSLOT - HS]
    sendH = [scr("sendA", [16 * 160, NSH[0] * 128]), scr("sendB", [16 * 160, NSH[1] * 128])]
    gathH = [scr("gathA", [16 * 640, NSH[0] * 128]), scr("gathB", [16 * 640, NSH[1] * 128])]
    tsend = scr("tsend", [1, 128], F32)
    tgath = scr("tgath", [4, 128], F32)

    KD = bool(os.environ.get("KDEBUG"))
    if KD:
        dbg_osb = nc.dram_tensor("dbg_osb", [16, 128, 2048], F32, kind="ExternalOutput")
        dbg_g0 = nc.dram_tensor("dbg_g0", [2, 640, 2048], BF16, kind="ExternalOutput")
        dbg_g1 = nc.dram_tensor("dbg_g1", [2, 640, 2048], BF16, kind="ExternalOutput")

    def sb(name, shape, dt=F32, stack=es):
        return stack.enter_context(nc.sbuf_tensor(name, shape, dt))

    T_cast = {}

    gv = sb("gv", [128, G_END])
    T_gv = Trk()
    S.op("sp", lambda: nc.sync.dma_start(out=gv[:], in_=gv_d[:, :]), w=[T_gv], dma=True)
    cs = sb("cs", [128, 17, 32])
    T_cs = Trk()
    S.op("sp", lambda: nc.sync.dma_start(out=cs[:], in_=cs_d[:, :, :]), w=[T_cs], dma=True)
    ident = sb("ident", [128, 128], BF16)
    T_id = Trk()
    S.op("pool", lambda: nc.gpsimd.memset(ident[:], 0.0), w=[T_id])
    S.op("pool", lambda: nc.gpsimd.affine_select(out=ident[:], in_=ident[:], pattern=[[-1, 128]], compare_op=ALU.not_equal,
                                                 fill=1.0, base=0, channel_multiplier=1), r=[T_id], w=[T_id])
    tri = sb("tri", [128, 128], F32)
    T_tri = Trk()
    S.op("pool", lambda: nc.gpsimd.memset(tri[:], 1.0), w=[T_tri])
    S.op("pool", lambda: nc.gpsimd.affine_select(out=tri[:], in_=tri[:], pattern=[[1, 128]], compare_op=ALU.is_ge,
                                                 fill=0.0, base=0, channel_multiplier=-1), r=[T_tri], w=[T_tri])
    S.op("dve", lambda: nc.vector.tensor_scalar(out=gv[:, G_QM:G_QM + 96], in0=gv[:, G_QM:G_QM + 96], scalar1=float(96 ** -0.5),
                                                scalar2=None, op0=ALU.mult), r=[T_gv], w=[T_gv])
    S.op("dve", lambda: nc.vector.tensor_scalar(out=gv[:, G_QF:G_QF + 64], in0=gv[:, G_QF:G_QF + 64], scalar1=0.125,
                                                scalar2=None, op0=ALU.mult), r=[T_gv], w=[T_gv])

    cast_ops = {}

    def cast_all(dst, src, rows, key, gate=None):
        lst = []
        for i_, r0 in enumerate(range(0, rows, 256)):
            r1 = min(rows, r0 + 256)
            t = Trk()
            S.op("pool", lambda d=dst, s=src, a=r0, b=r1: nc.gpsimd.dma_start(out=d[a:b, :], in_=s[a:b, :]),
                 r=[gate[(2 * i_) % NF]] if gate else [], w=[t], dma=True)
            lst.append(t)
        cast_ops[key] = lst

    def cast_cols(dst, src, key, fp, gate=None):
        t = Trk()
        S.op("pool", lambda: nc.gpsimd.dma_start(out=dst[fp, :, :, :].rearrange("p k c -> k p c"),
                                                 in_=src[:, fp * 256:(fp + 1) * 256].rearrange("(k p) c -> k p c", p=128)),
             r=[gate[(2 * fp) % NF]] if gate else [], w=[t], dma=True)
        cast_ops.setdefault(key, []).append(t)

    for fp in range(11):
        cast_cols(w1g_s, w1g_d, "w1g", fp)
        cast_cols(w1u_s, w1u_d, "w1u", fp)
    cast_all(w1d_s, w1d_d, DFF, "w1d")
    cast_all(win_s, win_d, D, "win")
    cast_all(wuq_s, wuq_d, 256, "wuq")
    cast_all(wukv_s, wukv_d, 128, "wukv")
    def cast_phase_c(part, gate):
        if part == 0:
            cast_all(wout_s, wout_d, D, "wout", gate)
            for fp in range(5):
                cast_cols(w2g_s, w2g_d, "w2g", fp, gate)
                cast_cols(w2u_s, w2u_d, "w2u", fp, gate)
        elif part == 1:
            for fp in range(5, 11):
                cast_cols(w2g_s, w2g_d, "w2g", fp, gate)
                cast_cols(w2u_s, w2u_d, "w2u", fp, gate)
        else:
            cast_all(w2d_s, w2d_d, DFF, "w2d", gate)

    def make_ffn_bufs(stack, tg, ntmax=512):
        B = {}
        B["wd"] = sb("wd" + tg, [128, NF, D], BF16, stack)
        B["T_wd"] = Trk()
        B["slab"] = [(sb("sg%d" % i + tg, [128, 8, 256], BF16, stack), sb("su%d" % i + tg, [128, 8, 256], BF16, stack), Trk()) for i in range(2)]
        B["xn"] = [(sb("xn%d" % i + tg, [128, D], BF16, stack), Trk()) for i in range(2)]
        B["xT"] = sb("xT" + tg, [128, 8, ntmax], BF16, stack)
        B["T_xT"] = Trk()
        B["aT"] = sb("aT" + tg, [128, NF, ntmax], BF16, stack)
        B["T_aT"] = [Trk() for _ in range(NF)]
        B["sg"] = [(sb("sgl%d" % i + tg, [128, ntmax], F32, stack), Trk()) for i in range(2)]
        B["junk"] = sb("junk" + tg, [128, D], BF16, stack)
        B["T_junk"] = Trk()
        B["ss"] = sb("ss" + tg, [128, 8], F32, stack)
        B["T_ss"] = Trk()
        B["rs"] = sb("rs" + tg, [128, 8], F32, stack)
        B["T_rs"] = Trk()
        B["T_ss2"] = Trk()
        B["T_rs2"] = Trk()
        B["nslab"] = 0
        return B

    def load_wd(B, wd_s, key):
        for f0 in range(0, NF, 11):
            S.op("sp", lambda a=f0: nc.sync.dma_start(out=B["wd"][:, a:a + 11, :],
                                                      in_=wd_s[a * 128:(a + 11) * 128, :].rearrange("(f p) c -> p f c", p=128)),
                 r=cast_ops[key], w=[B["T_wd"]], dma=True)

    def rms_scale(B, PS, blocks, g_off, xT=None, T_xT=None, so=0):
        nb = len(blocks)
        if xT is None:
            xT, T_xT = B["xT"], B["T_xT"]
        ss_, rs_ = B["ss"][:, so:so + 4], B["rs"][:, so:so + 4]
        T_ss_, T_rs_ = (B["T_ss"], B["T_rs"]) if so == 0 else (B["T_ss2"], B["T_rs2"])
        for j, (h, th) in enumerate(blocks):
            S.op("act", lambda h=h, j=j: nc.scalar.activation(out=B["junk"][:], in_=h, func=AF.Square, accum_out=ss_[:, j:j + 1]),
                 r=[th], w=[B["T_junk"], T_ss_])
        S.op("act", lambda: nc.scalar.activation(out=rs_[:, 0:nb], in_=ss_[:, 0:nb], func=AF.Ln, scale=1.0 / D, bias=EPS),
             r=[T_ss_], w=[T_rs_])
        S.op("act", lambda: nc.scalar.activation(out=rs_[:, 0:nb], in_=rs_[:, 0:nb], func=AF.Exp, scale=-0.5),
             r=[T_rs_], w=[T_rs_])
        for j, (h, th) in enumerate(blocks):
            xn, txn = B["xn"][j % len(B["xn"])]
            S.op("dve", lambda h=h, j=j, xn=xn: nc.vector.scalar_tensor_tensor(out=xn[:], in0=h, scalar=rs_[:, j:j + 1],
                                                                                 in1=gv[:, g_off:g_off + D], op0=ALU.mult, op1=ALU.mult),
                 r=[th, T_rs_, T_gv], w=[txn])
            pt, tpt = PS["T"][j % 2]
            for k in range(8):
                S.op("pe", lambda k=k, xn=xn, pt=pt: nc.tensor.transpose(out=pt[:, k * 128:(k + 1) * 128], in_=xn[:, k * 128:(k + 1) * 128],
                                                                           identity=ident[:]),
                     r=[txn, T_id], w=[tpt])
            S.op("act", lambda j=j, pt=pt: nc.scalar.copy(out=xT[:, :, j * 128:(j + 1) * 128],
                                                          in_=pt[:, :].rearrange("p (k t) -> p k t", k=8)),
                 r=[tpt], w=[T_xT])

    def ffn(B, PS, blocks, g_off, wg_s, wu_s, kg, ku):
        for _ in ffn_gen(B, PS, blocks, g_off, wg_s, wu_s, kg, ku):
            pass

    def ffn_gen(B, PS, blocks, g_off, wg_s, wu_s, kg, ku):
        nb = len(blocks)
        NT = nb * 128
        rms_scale(B, PS, blocks, g_off)
        yield

        def load_slab(fp):
            sgt, sut, tsl = B["slab"][B["nslab"] % 2]
            B["nslab"] += 1
            S.op("sp", lambda: nc.sync.dma_start(out=sgt[:], in_=wg_s[fp, :, :, :]),
                 r=[cast_ops[kg][fp]], w=[tsl], dma=True)
            S.op("sp", lambda: nc.sync.dma_start(out=sut[:], in_=wu_s[fp, :, :, :]),
                 r=[cast_ops[ku][fp]], w=[tsl], dma=True)
            return sgt, sut, tsl

        nxt = load_slab(0)
        for fp in range(11):
            sgt, sut, tsl = nxt
            if fp + 1 < 11:
                nxt = load_slab(fp + 1)
            for j2 in range(2):
                f = 2 * fp + j2
                pg, tpg = PS["G"][f % len(PS["G"])]
                pu, tpu = PS["U"][f % len(PS["U"])]
                for k in range(8):
                    S.op("pe", lambda k=k, pg=pg, sgt=sgt, j2=j2: nc.tensor.matmul(pg[:, 0:NT], lhsT=sgt[:, k, j2 * 128:(j2 + 1) * 128],
                                                                                  rhs=B["xT"][:, k, 0:NT], start=(k == 0), stop=(k == 7)),
                         r=[tsl, B["T_xT"]], w=[tpg])
                for k in range(8):
                    S.op("pe", lambda k=k, pu=pu, sut=sut, j2=j2: nc.tensor.matmul(pu[:, 0:NT], lhsT=sut[:, k, j2 * 128:(j2 + 1) * 128],
                                                                                  rhs=B["xT"][:, k, 0:NT], start=(k == 0), stop=(k == 7)),
                         r=[tsl, B["T_xT"]], w=[tpu])
                sg, tsg = B["sg"][f % 2]
                S.op("act", lambda sg=sg, pg=pg: nc.scalar.activation(out=sg[:, 0:NT], in_=pg[:, 0:NT], func=AF.Silu), r=[tpg], w=[tsg])
                S.op("dve", lambda sg=sg, pu=pu, f=f: nc.vector.tensor_tensor(out=B["aT"][:, f, 0:NT], in0=sg[:, 0:NT], in1=pu[:, 0:NT], op=ALU.mult),
                     r=[tsg, tpu], w=[B["T_aT"][f]])
            yield
        if B.get("wd_pending"):
            load_wd(B, *B["wd_pending"])
            B["wd_pending"] = None
        for j, (h, th) in enumerate(blocks):
            for half in range(2):
                po, tpo = PS["O"][half]
                for f in range(NF):
                    S.op("pe", lambda f=f, j=j, half=half, po=po: nc.tensor.matmul(po[:, :], lhsT=B["aT"][:, f, j * 128:(j + 1) * 128],
                                                                                    rhs=B["wd"][:, f, half * 512:(half + 1) * 512],
                                                                                    start=(f == 0), stop=(f == NF - 1)),
                         r=[B["T_aT"][f], B["T_wd"]], w=[tpo])
                S.op("dve", lambda h=h, half=half, po=po: nc.vector.scalar_tensor_tensor(out=h[:, half * 512:(half + 1) * 512], in0=po[:, :], scalar=0.5,
                                                                                          in1=h[:, half * 512:(half + 1) * 512], op0=ALU.mult, op1=ALU.add),
                     r=[tpo, th], w=[th])
                yield

    def make_psum(stack, tg):
        psA = stack.enter_context(nc.psum_tensor("psA" + tg, [128, 6, 512], F32))
        psT = stack.enter_context(nc.psum_tensor("psT" + tg, [128, 2, 1024], BF16))
        PS = {"G": [(psA[:, 0, :], Trk()), (psA[:, 1, :], Trk())], "U": [(psA[:, 2, :], Trk()), (psA[:, 3, :], Trk())],
              "O": [(psA[:, 4, :], Trk()), (psA[:, 5, :], Trk())], "T": [(psT[:, 0, :], Trk()), (psT[:, 1, :], Trk())], "A": psA}
        return PS

    h1_s = scr("h1_s", [17, 128, D], F32)
    with ExitStack() as sa1:
        PSA1 = make_psum(sa1, "a")
        BA1 = make_ffn_bufs(sa1, "a", 512)
        BA1["wd_pending"] = (w1d_s, "w1d")
        hbA = [sb("hbufa%d" % i, [128, 4, D], F32, sa1) for i in range(2)]
        T_hbA = [[Trk() for _ in range(4)] for _ in range(2)]
        groupsA = [[0, 1, 2, 3], [4, 5, 6, 7], [8, 9, 10], [11, 12, 13], [14, 15, 16]]
        for gi, grp in enumerate(groupsA):
            blocks = []
            for j, lb in enumerate(grp):
                src = meta_d[:, :] if lb == 0 else x_d[lb - 1, :, :]
                S.op("sp", lambda j=j, src=src, gi=gi: nc.sync.dma_start(out=hbA[gi % 2][:, j, :], in_=src), w=[T_hbA[gi % 2][j]], dma=True)
                blocks.append((hbA[gi % 2][:, j, :], T_hbA[gi % 2][j]))
            ffn(BA1, PSA1, blocks, G_FFN1, w1g_s, w1u_s, "w1g", "w1u")
            for j, lb in enumerate(grp):
                S.op("pool", lambda j=j, lb=lb, gi=gi: nc.gpsimd.dma_start(out=h1_s[lb, :, :], in_=hbA[gi % 2][:, j, :]),
                     r=[T_hbA[gi % 2][j]], w=[Trk()], dma=True)
        S.barrier()

    with ExitStack() as sa:
        psA2 = sa.enter_context(nc.psum_tensor("psA2", [128, 6, 512], F32))
        psT2 = sa.enter_context(nc.psum_tensor("psT2", [128, 2, 1024], BF16))
        PST = {"T": [(psT2[:, 0, :], Trk()), (psT2[:, 1, :], Trk())]}
        win = sb("win", [128, 8, DIN], BF16, sa)
        T_win = Trk()
        wuq = sb("wuq", [128, 2, 768], BF16, sa)
        T_wuq = Trk()
        wukv = sb("wukv", [128, 1024], BF16, sa)
        T_wukv = Trk()
        for k0 in range(0, 8, 2):
            S.op("sp", lambda a=k0: nc.sync.dma_start(out=win[:, a:a + 2, :], in_=win_s[a * 128:(a + 2) * 128, :].rearrange("(k p) c -> p k c", p=128)),
                 r=cast_ops["win"], w=[T_win], dma=True, join="win")
        S.op("sp", lambda: nc.sync.dma_start(out=wuq[:], in_=wuq_s[:, :].rearrange("(k p) c -> p k c", p=128)), r=cast_ops["wuq"], w=[T_wuq], dma=True)
        S.op("sp", lambda: nc.sync.dma_start(out=wukv[:], in_=wukv_s[:, :]), r=cast_ops["wukv"], w=[T_wukv], dma=True)

        def make_mixer_set(i):
            M = {}

            def t(name, shape, dt=F32):
                M[name] = sb("%s_%d" % (name, i), shape, dt, sa)
                M["T_" + name] = Trk()
            t("h", [128, D])
            t("pj", [128, DIN])
            t("qkv", [128, 1792])
            t("cn", [128, 384], BF16)
            t("cT", [128, 3, 128], BF16)
            t("st", [128, 40])
            t("rr", [128, 40])
            t("tmp1", [128, 768])
            t("tmp2", [128, 576])
            t("rp", [128, 4, 8, 16])
            t("kr", [128, 64])
            t("Qm", [128, 8, 96], BF16)
            t("Km", [128, 8, 96], BF16)
            t("Qf", [128, 8, 80], BF16)
            t("Kf", [128, 8, 80], BF16)
            t("Vt", [128, 16, 64], BF16)
            t("flb", [128, 8])
            t("cA", [128, 8])
            t("c32", [128, 8])
            t("xTm", [128, 8, 128], BF16)
            M["stg"] = [(sb("stg%d_%d" % (k, i), [96, 8, 128], BF16, sa), Trk()) for k in range(4)]
            Bm = {"junk": sb("junk_%d" % i, [128, D], BF16, sa), "T_junk": Trk(),
                  "ss": sb("ss_%d" % i, [128, 8], F32, sa), "T_ss": Trk(), "rs": sb("rs_%d" % i, [128, 8], F32, sa), "T_rs": Trk(),
                  "T_ss2": Trk(), "T_rs2": Trk(), "xn": [(sb("xn_%d" % i, [128, D], BF16, sa), Trk())]}
            M["B"] = Bm
            M["PSM"] = [(psA2[:, 2 * i, :], Trk()), (psA2[:, 2 * i + 1, :], Trk())]
            Qf_, Kf_ = M["Qf"], M["Kf"]
            S.op("pool", lambda: nc.gpsimd.memset(Qf_[:, :, 64:67], 0.0), w=[M["T_Qf"]])
            S.op("pool", lambda: nc.gpsimd.memset(Qf_[:, :, 70:73], 1.0), w=[M["T_Qf"]])
            S.op("pool", lambda: nc.gpsimd.memset(Kf_[:, :, 64:70], 1.0), w=[M["T_Kf"]])
            return M

        send_trks = []

        def mixer_block(lb, M):
            slot = lb - 1
            BA, PSA, PSM = M["B"], PST, M["PSM"]
            h, th = M["h"][:, :], M["T_h"]
            pj, T_pj, qkv, T_qkv, cn, T_cn, cT, T_cT = M["pj"], M["T_pj"], M["qkv"], M["T_qkv"], M["cn"], M["T_cn"], M["cT"], M["T_cT"]
            st, T_st, rr, T_rr, tmp1, T_t1, tmp2, T_t2 = M["st"], M["T_st"], M["rr"], M["T_rr"], M["tmp1"], M["T_tmp1"], M["tmp2"], M["T_tmp2"]
            rp, T_rp, kr, T_kr = M["rp"], M["T_rp"], M["kr"], M["T_kr"]
            Qm, Km, Qf, Kf, Vt = M["Qm"], M["Km"], M["Qf"], M["Kf"], M["Vt"]
            T_Qm, T_Km, T_Qf, T_Kf, T_Vt = M["T_Qm"], M["T_Km"], M["T_Qf"], M["T_Kf"], M["T_Vt"]
            flb, T_flb, cA, T_cA, c32, T_c32 = M["flb"], M["T_flb"], M["cA"], M["T_cA"], M["c32"], M["T_c32"]
            xTm, T_xTm, stg = M["xTm"], M["T_xTm"], M["stg"]
            S.op("pool", lambda: nc.gpsimd.dma_start(out=h, in_=h1_s[lb, :, :]), w=[th], dma=True)
            rms_scale(BA, PSA, [(h, th)], G_MIX, xTm, T_xTm)
            yield
            banks = [PSM[0], PSM[1], PSM[0], PSM[1]]
            for cb in range(4):
                c0, c1 = cb * 512, min(DIN, (cb + 1) * 512)
                pb, tpb = banks[cb]
                for k in range(8):
                    S.op("pe", lambda k=k, pb=pb, c0=c0, c1=c1: nc.tensor.matmul(pb[:, 0:c1 - c0], lhsT=xTm[:, k, 0:128], rhs=win[:, k, c0:c1],
                                                                               start=(k == 0), stop=(k == 7)),
                         r=[T_xTm, T_win], w=[tpb])
                S.op("act", lambda pb=pb, c0=c0, c1=c1: nc.scalar.copy(out=pj[:, c0:c1], in_=pb[:, 0:c1 - c0]), r=[tpb], w=[T_pj])
                yield
            S.op("dve", lambda: nc.vector.tensor_tensor(out=flb[:], in0=pj[:, 1952:1960], in1=gv[:, G_B:G_B + 8], op=ALU.add), r=[T_pj, T_gv], w=[T_flb])
            S.op("act", lambda: nc.scalar.activation(out=flb[:], in_=flb[:], func=AF.Exp, scale=-1.0), r=[T_flb], w=[T_flb])
            S.op("act", lambda: nc.scalar.activation(out=flb[:], in_=flb[:], func=AF.Ln, bias=1.0), r=[T_flb], w=[T_flb])
            yield "S1END"
            S.op("act", lambda: nc.scalar.activation(out=BA["junk"][:, 0:256], in_=pj[:, 0:256], func=AF.Square, accum_out=st[:, 0:1]),
                 r=[T_pj], w=[BA["T_junk"], T_st])
            S.op("act", lambda: nc.scalar.activation(out=BA["junk"][:, 0:128], in_=pj[:, 256:384], func=AF.Square, accum_out=st[:, 1:2]),
                 r=[T_pj], w=[BA["T_junk"], T_st])
            S.op("act", lambda: nc.scalar.activation(out=BA["junk"][:, 0:32], in_=pj[:, 384:416], func=AF.Square, accum_out=st[:, 2:3]),
                 r=[T_pj], w=[BA["T_junk"], T_st])
            S.op("act", lambda: nc.scalar.activation(out=rr[:, 0:1], in_=st[:, 0:1], func=AF.Ln, scale=1.0 / 256, bias=EPS), r=[T_st], w=[T_rr])
            S.op("act", lambda: nc.scalar.activation(out=rr[:, 1:2], in_=st[:, 1:2], func=AF.Ln, scale=1.0 / 128, bias=EPS), r=[T_st], w=[T_rr])
            S.op("act", lambda: nc.scalar.activation(out=rr[:, 0:2], in_=rr[:, 0:2], func=AF.Exp, scale=-0.5), r=[T_rr], w=[T_rr])
            S.op("dve", lambda: nc.vector.scalar_tensor_tensor(out=cn[:, 0:256], in0=pj[:, 0:256], scalar=rr[:, 0:1], in1=gv[:, G_CQ:G_CQ + 256],
                                                               op0=ALU.mult, op1=ALU.mult), r=[T_pj, T_rr, T_gv], w=[T_cn])
            S.op("dve", lambda: nc.vector.scalar_tensor_tensor(out=cn[:, 256:384], in0=pj[:, 256:384], scalar=rr[:, 1:2], in1=gv[:, G_CKV:G_CKV + 128],
                                                               op0=ALU.mult, op1=ALU.mult), r=[T_pj, T_rr, T_gv], w=[T_cn])
            pt, tpt = PSA["T"][0]
            for k in range(3):
                S.op("pe", lambda k=k: nc.tensor.transpose(out=pt[:, k * 128:(k + 1) * 128], in_=cn[:, k * 128:(k + 1) * 128], identity=ident[:]),
                     r=[T_cn, T_id], w=[tpt])
            S.op("act", lambda: nc.scalar.copy(out=cT[:, :, :], in_=pt[:, 0:384].rearrange("p (k t) -> p k t", k=3)), r=[tpt], w=[T_cT])
            yield
            pq0, tq0 = PSM[0]
            pq1, tq1 = PSM[1]
            for k in range(2):
                S.op("pe", lambda k=k: nc.tensor.matmul(pq0[:, 0:512], lhsT=cT[:, k, :], rhs=wuq[:, k, 0:512], start=(k == 0), stop=(k == 1)),
                     r=[T_cT, T_wuq], w=[tq0])
            for k in range(2):
                S.op("pe", lambda k=k: nc.tensor.matmul(pq1[:, 0:256], lhsT=cT[:, k, :], rhs=wuq[:, k, 512:768], start=(k == 0), stop=(k == 1)),
                     r=[T_cT, T_wuq], w=[tq1])
            S.op("act", lambda: nc.scalar.copy(out=qkv[:, 0:512], in_=pq0[:, 0:512]), r=[tq0], w=[T_qkv])
            S.op("act", lambda: nc.scalar.copy(out=qkv[:, 512:768], in_=pq1[:, 0:256]), r=[tq1], w=[T_qkv])
            S.op("pe", lambda: nc.tensor.matmul(pq0[:, 0:512], lhsT=cT[:, 2, :], rhs=wukv[:, 0:512], start=True, stop=True), r=[T_cT, T_wukv], w=[tq0])
            S.op("pe", lambda: nc.tensor.matmul(pq1[:, 0:512], lhsT=cT[:, 2, :], rhs=wukv[:, 512:1024], start=True, stop=True), r=[T_cT, T_wukv], w=[tq1])
            S.op("act", lambda: nc.scalar.copy(out=qkv[:, 768:1280], in_=pq0[:, 0:512]), r=[tq0], w=[T_qkv])
            S.op("act", lambda: nc.scalar.copy(out=qkv[:, 1280:1792], in_=pq1[:, 0:512]), r=[tq1], w=[T_qkv])
            yield
            AX = mybir.AxisListType.X
            S.op("act", lambda: nc.scalar.activation(out=tmp1[:, 0:768], in_=qkv[:, 0:768], func=AF.Square), r=[T_qkv], w=[T_t1])
            S.op("dve", lambda: nc.vector.tensor_reduce(out=st[:, 8:16], in_=tmp1[:, 0:768].rearrange("p (h d) -> p h d", h=8), axis=AX, op=ALU.add),
                 r=[T_t1], w=[T_st])
            S.op("act", lambda: nc.scalar.activation(out=tmp2[:, 0:512].rearrange("p (h d) -> p h d", h=8),
                                                     in_=qkv[:, 768:1792].rearrange("p (h d) -> p h d", h=8)[:, :, 0:64], func=AF.Square), r=[T_qkv], w=[T_t2])
            S.op("dve", lambda: nc.vector.tensor_reduce(out=st[:, 16:24], in_=tmp2[:, 0:512].rearrange("p (h d) -> p h d", h=8), axis=AX, op=ALU.add),
                 r=[T_t2], w=[T_st])
            S.op("act", lambda: nc.scalar.activation(out=BA["junk"][:, 0:1024], in_=pj[:, 416:1440], func=AF.Square), r=[T_pj], w=[BA["T_junk"]])
            S.op("dve", lambda: nc.vector.tensor_reduce(out=st[:, 24:40], in_=BA["junk"][:, 0:1024].rearrange("p (h d) -> p h d", h=16), axis=AX, op=ALU.add),
                 r=[BA["T_junk"]], w=[T_st])
            S.op("dve", lambda: nc.vector.tensor_scalar(out=st[:, 16:24], in0=st[:, 16:24], scalar1=st[:, 2:3], scalar2=None, op0=ALU.add),
                 r=[T_st], w=[T_st])
            S.op("act", lambda: nc.scalar.activation(out=rr[:, 8:24], in_=st[:, 8:24], func=AF.Ln, scale=1.0 / 96, bias=EPS), r=[T_st], w=[T_rr])
            S.op("act", lambda: nc.scalar.activation(out=rr[:, 24:40], in_=st[:, 24:40], func=AF.Ln, scale=1.0 / 64, bias=EPS), r=[T_st], w=[T_rr])
            S.op("act", lambda: nc.scalar.activation(out=rr[:, 8:40], in_=rr[:, 8:40], func=AF.Exp, scale=-0.5), r=[T_rr], w=[T_rr])

            cosb = cs[:, lb, 0:16]
            sinb = cs[:, lb, 16:32]

            def bc_h(ap2, n):
                return ap2.unsqueeze(1).to_broadcast([128, 8, n])

            def bc_d(ap2, n):
                return ap2.unsqueeze(2).to_broadcast([128, 8, n])

            yield
            q3 = qkv[:, 0:768].rearrange("p (h d) -> p h d", h=8)
            t13 = tmp1[:, 0:768].rearrange("p (h d) -> p h d", h=8)
            S.op("dve", lambda: nc.vector.tensor_tensor(out=t13, in0=q3, in1=bc_d(rr[:, 8:16], 96), op=ALU.mult), r=[T_qkv, T_rr], w=[T_t1])
            S.op("dve", lambda: nc.vector.tensor_tensor(out=t13, in0=t13, in1=bc_h(gv[:, G_QM:G_QM + 96], 96), op=ALU.mult), r=[T_t1, T_gv], w=[T_t1])
            S.op("act", lambda: nc.scalar.copy(out=Qm[:, :, 0:64], in_=t13[:, :, 0:64]), r=[T_t1], w=[T_Qm])
            x1, x2 = t13[:, :, 64:80], t13[:, :, 80:96]
            S.op("dve", lambda: nc.vector.tensor_tensor(out=rp[:, 0], in0=x1, in1=bc_h(cosb, 16), op=ALU.mult), r=[T_t1, T_cs], w=[T_rp])
            S.op("dve", lambda: nc.vector.tensor_tensor(out=rp[:, 1], in0=x2, in1=bc_h(sinb, 16), op=ALU.mult), r=[T_t1, T_cs], w=[T_rp])
            S.op("dve", lambda: nc.vector.tensor_tensor(out=rp[:, 2], in0=x2, in1=bc_h(cosb, 16), op=ALU.mult), r=[T_t1, T_cs], w=[T_rp])
            S.op("dve", lambda: nc.vector.tensor_tensor(out=rp[:, 3], in0=x1, in1=bc_h(sinb, 16), op=ALU.mult), r=[T_t1, T_cs], w=[T_rp])
            S.op("dve", lambda: nc.vector.tensor_tensor(out=Qm[:, :, 64:80], in0=rp[:, 0], in1=rp[:, 1], op=ALU.subtract), r=[T_rp], w=[T_Qm])
            S.op("dve", lambda: nc.vector.tensor_tensor(out=Qm[:, :, 80:96], in0=rp[:, 2], in1=rp[:, 3], op=ALU.add), r=[T_rp], w=[T_Qm])
            yield
            kv3 = qkv[:, 768:1792].rearrange("p (h d) -> p h d", h=8)
            t23 = tmp2[:, 0:512].rearrange("p (h d) -> p h d", h=8)
            S.op("dve", lambda: nc.vector.tensor_tensor(out=t23, in0=kv3[:, :, 0:64], in1=bc_d(rr[:, 16:24], 64), op=ALU.mult), r=[T_qkv, T_rr], w=[T_t2])
            S.op("dve", lambda: nc.vector.tensor_tensor(out=Km[:, :, 0:64], in0=t23, in1=bc_h(gv[:, G_KM:G_KM + 64], 64), op=ALU.mult),
                 r=[T_t2, T_gv], w=[T_Km])
            S.op("dve", lambda: nc.vector.tensor_tensor(out=kr[:, 0:32], in0=pj[:, 384:416], in1=gv[:, G_KM + 64:G_KM + 96], op=ALU.mult),
                 r=[T_pj, T_gv], w=[T_kr])
            S.op("dve", lambda: nc.vector.tensor_tensor(out=tmp2[:, 512:528], in0=kr[:, 0:16], in1=cosb, op=ALU.mult), r=[T_kr, T_cs], w=[T_t2])
            S.op("dve", lambda: nc.vector.tensor_tensor(out=tmp2[:, 528:544], in0=kr[:, 16:32], in1=sinb, op=ALU.mult), r=[T_kr, T_cs], w=[T_t2])
            S.op("dve", lambda: nc.vector.tensor_tensor(out=tmp2[:, 544:560], in0=kr[:, 16:32], in1=cosb, op=ALU.mult), r=[T_kr, T_cs], w=[T_t2])
            S.op("dve", lambda: nc.vector.tensor_tensor(out=tmp2[:, 560:576], in0=kr[:, 0:16], in1=sinb, op=ALU.mult), r=[T_kr, T_cs], w=[T_t2])
            S.op("dve", lambda: nc.vector.tensor_tensor(out=kr[:, 32:48], in0=tmp2[:, 512:528], in1=tmp2[:, 528:544], op=ALU.subtract), r=[T_t2], w=[T_kr])
            S.op("dve", lambda: nc.vector.tensor_tensor(out=kr[:, 48:64], in0=tmp2[:, 544:560], in1=tmp2[:, 560:576], op=ALU.add), r=[T_t2], w=[T_kr])
            S.op("dve", lambda: nc.vector.tensor_tensor(out=Km[:, :, 64:96], in0=bc_h(kr[:, 32:64], 32), in1=bc_d(rr[:, 16:24], 32), op=ALU.mult),
                 r=[T_kr, T_rr], w=[T_Km])
            yield
            fq3 = pj[:, 416:928].rearrange("p (h d) -> p h d", h=8)
            fk3 = pj[:, 928:1440].rearrange("p (h d) -> p h d", h=8)
            t1f = tmp1[:, 0:512].rearrange("p (h d) -> p h d", h=8)
            S.op("dve", lambda: nc.vector.tensor_tensor(out=t1f, in0=fq3, in1=bc_d(rr[:, 24:32], 64), op=ALU.mult), r=[T_pj, T_rr], w=[T_t1])
            S.op("dve", lambda: nc.vector.tensor_tensor(out=Qf[:, :, 0:64], in0=t1f, in1=bc_h(gv[:, G_QF:G_QF + 64], 64), op=ALU.mult),
                 r=[T_t1, T_gv], w=[T_Qf])
            S.op("dve", lambda: nc.vector.tensor_tensor(out=t23, in0=fk3, in1=bc_d(rr[:, 32:40], 64), op=ALU.mult), r=[T_pj, T_rr], w=[T_t2])
            S.op("dve", lambda: nc.vector.tensor_tensor(out=Kf[:, :, 0:64], in0=t23, in1=bc_h(gv[:, G_KF:G_KF + 64], 64), op=ALU.mult),
                 r=[T_t2, T_gv], w=[T_Kf])
            yield
            S.op("act", lambda: nc.scalar.copy(out=Vt[:, 0:8, :], in_=kv3[:, :, 64:128]), r=[T_qkv], w=[T_Vt])
            S.op("act", lambda: nc.scalar.copy(out=Vt[:, 8:16, :], in_=pj[:, 1440:1952].rearrange("p (h d) -> p h d", h=8)), r=[T_pj], w=[T_Vt])
            yield
            pc, tpc = PSM[0]
            S.op("pe", lambda: nc.tensor.matmul(pc[:, 0:8], lhsT=tri[:], rhs=flb[:], start=True, stop=True), r=[T_tri, T_flb], w=[tpc])
            S.op("dve", lambda: nc.vector.tensor_copy(out=cA[:], in_=pc[:, 0:8]), r=[tpc], w=[T_cA])
            S.op("dve", lambda: nc.vector.tensor_copy(out=Kf[:, :, 70], in_=cA[:]), r=[T_cA], w=[T_Kf])
            S.op("dve", lambda: nc.vector.tensor_tensor(out=c32[:], in0=cA[:], in1=Kf[:, :, 70], op=ALU.subtract), r=[T_cA, T_Kf], w=[T_c32])
            S.op("dve", lambda: nc.vector.tensor_copy(out=Kf[:, :, 71], in_=c32[:]), r=[T_c32], w=[T_Kf])
            S.op("dve", lambda: nc.vector.tensor_tensor(out=c32[:], in0=c32[:], in1=Kf[:, :, 71], op=ALU.subtract), r=[T_c32, T_Kf], w=[T_c32])
            S.op("dve", lambda: nc.vector.tensor_copy(out=Kf[:, :, 72], in_=c32[:]), r=[T_c32], w=[T_Kf])
            S.op("dve", lambda: nc.vector.tensor_scalar(out=Qf[:, :, 67:70], in0=Kf[:, :, 70:73], scalar1=-1.0, scalar2=None, op0=ALU.mult),
                 r=[T_Kf], w=[T_Qf])
            yield
            def tr_store(src, tsrc, dk, si, dsts, q="sp"):
                pt, tpt = PSA["T"][si % 2]
                for hh in range(8):
                    S.op("pe", lambda hh=hh, pt=pt: nc.tensor.transpose(out=pt[0:dk, hh * 128:(hh + 1) * 128], in_=src[:, hh, 0:dk], identity=ident[:]),
                         r=[tsrc, T_id], w=[tpt])
                sg_, tsg_ = stg[si % 4]
                S.op("act", lambda pt=pt, sg_=sg_: nc.scalar.copy(out=sg_[0:dk, :, :], in_=pt[0:dk, :].rearrange("p (h t) -> p h t", h=8)),
                     r=[tpt], w=[tsg_])
                for dst in dsts:
                    tsv = Trk()
                    send_trks.append(tsv)
                    eng_ = {"sp": nc.sync, "act": nc.scalar, "pool": nc.gpsimd}[q]
                    S.op(q, lambda sg_=sg_, dst=dst, eng_=eng_: eng_.dma_start(out=dst, in_=sg_[0:dk, :, :]), r=[tsg_], w=[tsv], dma=True)

            hfx = 0 if slot < HS else 1
            sendX = sendH[hfx]
            nsx = NSH[hfx]
            s8 = slot - (0 if hfx == 0 else HS)

            def send_view(r0, dk):
                v = sendX[r0:r0 + 8 * 160, :].rearrange("(h r) (s t) -> r h s t", h=8, s=nsx)
                return v[0:dk, :, s8, :]

            if lb == 0:
                tr_store(Km, T_Km, 96, 0, [mk_s[:, :, :].rearrange("h d t -> d h t")])
                yield
                tr_store(Kf, T_Kf, 73, 1, [mf_s[:, :, :].rearrange("h d t -> d h t")])
                yield
                S.op("pool", lambda: nc.gpsimd.dma_start(out=mv_s[:, :, :].rearrange("h t d -> t h d"), in_=Vt[:, :, :]), r=[T_Vt], w=[Trk()], dma=True)
                S.op("pool", lambda: nc.gpsimd.dma_start(out=mt_s[0:1, :], in_=cA[127:128, :]), r=[T_cA], w=[Trk()], dma=True)
            else:
                tr_store(Qm, T_Qm, 96, 2, [qm_s[slot, :, :, :]])
                yield
                tr_store(Km, T_Km, 96, 0, [send_view(0, 96)])
                yield
                tr_store(Qf, T_Qf, 73, 3, [qf_s[slot, :, :, :]])
                yield
                tr_store(Kf, T_Kf, 73, 1, [send_view(8 * 160, 73)], q="act")
                yield
                vv = sendX[:, :].rearrange("(h r) c -> r h c", h=16)[96:160, :, :].rearrange("t2 h (u s d) -> (t2 u) h s d", u=2, s=nsx)
                tsv = Trk()
                send_trks.append(tsv)
                S.op("pool", lambda: nc.gpsimd.dma_start(out=vv[:, :, s8, :], in_=Vt[:, :, :]), r=[T_Vt], w=[tsv], dma=True)
                tv = tsend[0:1, :].rearrange("o (s h) -> o s h", s=NSLOT)
                tsv = Trk()
                send_trks.append(tsv)
                S.op("pool", lambda: nc.gpsimd.dma_start(out=tv[:, slot, :], in_=cA[127:128, :]), r=[T_cA], w=[tsv], dma=True)

        RG = [[0, 1, 2, 3], [4, 5, 6, 7]]
        T_ag = [[Trk() for _ in range(16)] for _ in range(2)]
        AGG = [(h, 1) for h in range(16)]
        AGOF = {}
        for (h0, ng) in AGG:
            for k in range(ng):
                AGOF[h0 + k] = (h0, ng, k)

        def issue_ags(hf):
            for (h0, ng) in AGG:
                S.op("pool", lambda h0=h0, ng=ng: nc.gpsimd.collective_compute("AllGather", ALU.bypass, replica_groups=RG,
                                                                               ins=[sendH[hf][h0 * 160:(h0 + ng) * 160, :]],
                                                                               outs=[gathH[hf][h0 * 640:(h0 + ng) * 640, :]]),
                     r=list(send_trks), w=[T_ag[hf][h0 + k] for k in range(ng)], dma=True, cc=True)

        zt = sb("zt", [23, NSH[1] * 128], BF16, sa)
        T_zt = Trk()
        S.op("pool", lambda: nc.gpsimd.memset(zt[:], 0.0), w=[T_zt])
        for hf in range(2):
            for hh_ in range(8):
                tz = Trk()
                send_trks.append(tz)
                r0_ = (8 + hh_) * 160 + 73
                S.op("pool", lambda hf=hf, r0_=r0_: nc.gpsimd.dma_start(out=sendH[hf][r0_:r0_ + 23, :], in_=zt[:, 0:NSH[hf] * 128]),
                     r=[T_zt], w=[tz], dma=True)
        NI = 3
        sets = [make_mixer_set(i) for i in range(NI)]
        done = set()
        state = {"agA": False}

        def inst_gen(i):
            for lb in range(i, 17, NI):
                yield from mixer_block(lb, sets[i])
                done.add(lb)
                if not state["agA"] and all(x in done for x in range(HS + 1)):
                    state["agA"] = True
                    issue_ags(0)

        gens = [inst_gen(i) for i in range(NI)]
        alive = []
        STAG = 9
        step = 0
        started = 0
        while alive or started < NI:
            if started < NI and step == started * STAG:
                alive.append(started)
                started += 1
            for i in list(alive):
                try:
                    next(gens[i])
                except StopIteration:
                    alive.remove(i)
            step += 1
        S.barrier()
        T_g2 = Trk()
        S.op("pool", lambda: nc.gpsimd.collective_compute("AllGather", ALU.bypass, replica_groups=RG, ins=[tsend[:, :]], outs=[tgath[:, :]]),
             r=list(send_trks), w=[T_g2], dma=True, cc=True)
        issue_ags(1)

    with ExitStack() as sbk:
        psOT = sbk.enter_context(nc.psum_tensor("psOT", [128, 1024], F32))
        psS_t = sbk.enter_context(nc.psum_tensor("psS", [128, 3, 1024], F32))
        psS = [(psS_t[:, i, :], Trk()) for i in range(3)]
        mask = sb("mask", [128, 8, 128], F32, sbk)
        T_mask = Trk()
        S.op("sp", lambda: nc.sync.dma_start(out=mask[:], in_=mask_d[:, :, :]), w=[T_mask], dma=True)
        T_pOT = [Trk() for _ in range(2)]
        kT = [(sb("kT%d" % i, [96, 65 * 128], BF16, sbk), Trk()) for i in range(2)]
        vS = [(sb("vS%d" % i, [128, 65, 128], BF16, sbk), Trk()) for i in range(2)]
        vstage = [(sb("vst%d" % i, [128, 65 * 64], BF16, sbk), Trk()) for i in range(2)]
        qT = [(sb("qT%d" % i, [96, 2048], BF16, sbk), Trk()) for i in range(2)]
        pT = [(sb("pT%d" % i, [128, 1024], BF16, sbk), Trk()) for i in range(4)]
        osb = sb("osb", [128, 2048], F32, sbk)
        T_osb = [Trk() for _ in range(2)]
        rc = sb("rc", [128, 2048], F32, sbk)
        T_rc = Trk()
        otn = [(sb("otn%d" % i, [128, 2048], BF16, sbk), Trk()) for i in range(2)]
        ones3 = sb("ones3", [128, 2048], BF16, sbk)
        T_o3 = Trk()
        S.op("pool", lambda: nc.gpsimd.memset(ones3[:], 1.0), w=[T_o3])
        for i_, (vt, tv_) in enumerate(vS):
            oo = 64 if i_ == 0 else 0
            S.op("pool", lambda vt=vt, oo=oo: nc.gpsimd.memset(vt[:, :, oo:oo + 64], 1.0), w=[tv_])
            S.op("pool", lambda vt=vt, oo=oo: nc.gpsimd.memset(vt[0:112, 64, oo:oo + 64], 0.0), w=[tv_])
        TT = sb("TT", [128, 4, NSLOT, 8], F32, sbk)
        T_TT = Trk()
        for r_ in range(4):
            src = tgath[r_:r_ + 1, :].partition_broadcast(128)
            S.op("sp", lambda r_=r_, src=src: nc.sync.dma_start(out=TT[:, r_, :, :].rearrange("p s h -> p (s h)"), in_=src), r=[T_g2], w=[T_TT], dma=True)
        TM = sb("TM", [128, 8], F32, sbk)
        T_TM = Trk()
        S.op("sp", lambda: nc.sync.dma_start(out=TM[:], in_=mt_s[0:1, :].partition_broadcast(128)), w=[T_TM], dma=True)
        RP = sb("RP", [128, NSLOT + 1, 8], F32, sbk)
        T_RP = Trk()
        OFF = sb("OFF", [128, 4, NSLOT, 8], F32, sbk)
        T_OFF = Trk()
        S.op("dve", lambda: nc.vector.tensor_copy(out=RP[:, 0, :], in_=TM[:]), r=[T_TM], w=[T_RP])
        for s_ in range(NSLOT):
            S.op("dve", lambda s_=s_: nc.vector.tensor_tensor(out=OFF[:, 0, s_, :], in0=TT[:, 0, s_, :], in1=TT[:, 1, s_, :], op=ALU.add), r=[T_TT], w=[T_OFF])
            S.op("dve", lambda s_=s_: nc.vector.tensor_tensor(out=OFF[:, 1, s_, :], in0=TT[:, 2, s_, :], in1=TT[:, 3, s_, :], op=ALU.add), r=[T_TT], w=[T_OFF])
            S.op("dve", lambda s_=s_: nc.vector.tensor_tensor(out=OFF[:, 0, s_, :], in0=OFF[:, 0, s_, :], in1=OFF[:, 1, s_, :], op=ALU.add), r=[T_OFF], w=[T_OFF])
            S.op("dve", lambda s_=s_: nc.vector.tensor_tensor(out=RP[:, s_ + 1, :], in0=RP[:, s_, :], in1=OFF[:, 0, s_, :], op=ALU.add), r=[T_OFF, T_RP], w=[T_RP])
        for s_ in range(NSLOT):
            order = [0, 1, 2, 3] if s_ % 2 == 0 else [3, 2, 1, 0]
            prev = None
            for c_ in order:
                if prev is None:
                    S.op("dve", lambda s_=s_, c_=c_: nc.vector.tensor_copy(out=OFF[:, c_, s_, :], in_=RP[:, s_, :]), r=[T_RP, T_OFF], w=[T_OFF])
                else:
                    S.op("dve", lambda s_=s_, c_=c_, p_=prev: nc.vector.tensor_tensor(out=OFF[:, c_, s_, :], in0=OFF[:, p_, s_, :], in1=TT[:, p_, s_, :], op=ALU.add),
                         r=[T_OFF, T_TT], w=[T_OFF])
                prev = c_
        OWN = sb("OWN", [128, NSLOT, 8], F32, sbk)
        T_OWN = Trk()
        S.op("dve", lambda: nc.vector.tensor_scalar(out=OWN[:], in0=OFF[:, 0], scalar1=gv[:, G_SEL:G_SEL + 1], scalar2=None, op0=ALU.mult),
             r=[T_OFF, T_gv], w=[T_OWN])
        for c_ in range(1, 4):
            S.op("dve", lambda c_=c_: nc.vector.scalar_tensor_tensor(out=OWN[:], in0=OFF[:, c_], scalar=gv[:, G_SEL + c_:G_SEL + c_ + 1], in1=OWN[:],
                                                                     op0=ALU.mult, op1=ALU.add), r=[T_OFF, T_gv, T_OWN], w=[T_OWN])
        S.op("dve", lambda: nc.vector.tensor_scalar(out=OWN[:], in0=OWN[:], scalar1=-1.0, scalar2=None, op0=ALU.mult), r=[T_OWN], w=[T_OWN])
        pidx = sb("pidx", [128, 1], F32, sbk)
        T_pidx = Trk()
        S.op("pool", lambda: nc.gpsimd.iota(pidx[:], pattern=[[0, 1]], base=0, channel_multiplier=1, allow_small_or_imprecise_dtypes=True), w=[T_pidx])
        msel = sb("msel", [128, 3], F32, sbk)
        T_msel = Trk()
        for r_ in range(3):
            S.op("dve", lambda r_=r_: nc.vector.tensor_scalar(out=msel[:, r_:r_ + 1], in0=pidx[:], scalar1=float(64 + r_), scalar2=None, op0=ALU.is_equal),
                 r=[T_pidx], w=[T_msel])
        hb = sb("hb", [128, NSLOT, 8], BF16, sbk)
        T_hb = Trk()
        r1 = sb("r1", [128, NSLOT, 8], F32, sbk)
        T_r1 = Trk()
        acc = sb("acc", [128, NSLOT, 8], F32, sbk)
        T_acc = Trk()
        OFFP = sb("OFFP", [128, 8, NSLOT], BF16, sbk)
        T_OFFP = Trk()
        S.op("dve", lambda: nc.vector.tensor_copy(out=r1[:], in_=OWN[:]), r=[T_OWN], w=[T_r1])
        for r_ in range(3):
            S.op("dve", lambda: nc.vector.tensor_copy(out=hb[:], in_=r1[:]), r=[T_r1], w=[T_hb])
            if r_ == 0:
                S.op("dve", lambda: nc.vector.tensor_scalar(out=acc[:], in0=hb[:], scalar1=msel[:, 0:1], scalar2=None, op0=ALU.mult),
                     r=[T_hb, T_msel], w=[T_acc])
            else:
                S.op("dve", lambda r_=r_: nc.vector.scalar_tensor_tensor(out=acc[:], in0=hb[:], scalar=msel[:, r_:r_ + 1], in1=acc[:], op0=ALU.mult, op1=ALU.add),
                     r=[T_hb, T_msel, T_acc], w=[T_acc])
            if r_ < 2:
                S.op("dve", lambda: nc.vector.tensor_tensor(out=r1[:], in0=r1[:], in1=hb[:], op=ALU.subtract), r=[T_r1, T_hb], w=[T_r1])
        S.op("dve", lambda: nc.vector.tensor_copy(out=OFFP[:], in_=acc[:].rearrange("p s h -> p h s")), r=[T_acc], w=[T_OFFP])

        def load_head(hd):
            fox = hd >= 8
            hh = hd % 8
            dk = 73 if fox else 96
            kt, tk = kT[hd % 2]
            vt, tv_ = vS[hd % 2]
            vst, tvst = vstage[hd % 2]
            qt, tq = qT[hd % 2]
            h0_, ng_, k_ = AGOF[hd]
            for r_ in range(4):
                gb_ = h0_ * 640 + r_ * ng_ * 160 + k_ * 160
                for hf in range(2):
                    ns_ = NSH[hf]
                    s0_ = 0 if hf == 0 else HS
                    kc = (r_ * 16 + s0_) * 128
                    S.op("sp", lambda kc=kc, gb_=gb_, hf=hf, ns_=ns_: nc.sync.dma_start(out=kt[0:dk, kc:kc + ns_ * 128], in_=gathH[hf][gb_:gb_ + dk, :]),
                         r=[T_ag[hf][hd]], w=[tk], dma=True, join=("ld", hd))
                    vsrc = gathH[hf][gb_ + 96:gb_ + 160, :].rearrange("t2 (u c) -> (t2 u) c", u=2)
                    vc = (r_ * 16 + s0_) * 64
                    S.op("sp", lambda vc=vc, vsrc=vsrc, ns_=ns_: nc.sync.dma_start(out=vst[:, vc:vc + ns_ * 64], in_=vsrc), r=[T_ag[hf][hd]], w=[tvst], dma=True, join=("ld", hd))
            msrc = (mf_s if fox else mk_s)[hh, :, :]
            S.op("sp", lambda: nc.sync.dma_start(out=kt[0:dk, 64 * 128:65 * 128], in_=msrc), w=[tk], dma=True, join=("ld", hd))
            S.op("sp", lambda: nc.sync.dma_start(out=vst[:, 64 * 64:65 * 64], in_=mv_s[hd, :, :]), w=[tvst], dma=True, join=("ld", hd))
            vo = 0 if hd % 2 == 0 else 64
            S.op("pool", lambda: nc.gpsimd.tensor_copy(out=vt[:, :, vo:vo + 64], in_=vst[:, :].rearrange("p (k d) -> p k d", d=64)), r=[tvst], w=[tv_])
            qsrc = (qf_s if fox else qm_s)[:, :, hh, :].rearrange("s d t -> d s t")
            if fox:
                S.op("dve", lambda: nc.vector.tensor_tensor(out=qt[64:67, :].rearrange("p (s t) -> p s t", s=NSLOT),
                                                            in0=ones3[64:67, :].rearrange("p (s t) -> p s t", s=NSLOT),
                                                            in1=OFFP[64:67, hh, :].unsqueeze(2).to_broadcast([3, NSLOT, 128]), op=ALU.mult),
                     r=[T_o3, T_OFFP], w=[tq], join=("ld", hd))
                S.op("sp", lambda: nc.sync.dma_start(out=qt[0:64, :].rearrange("d (s t) -> d s t", s=NSLOT), in_=qsrc[0:64, :, :]), w=[tq], dma=True, join=("ld", hd))
                S.op("sp", lambda: nc.sync.dma_start(out=qt[67:73, :].rearrange("d (s t) -> d s t", s=NSLOT), in_=qsrc[67:73, :, :]), w=[tq], dma=True, join=("ld", hd))
            else:
                S.op("sp", lambda: nc.sync.dma_start(out=qt[0:dk, :].rearrange("d (s t) -> d s t", s=NSLOT), in_=qsrc), w=[tq], dma=True)

        load_head(0)
        jobs = []
        for hd in range(16):
            for half in range(2):
                qlo = half * 1024
                kbs = [(None, None)] + [(c_, s_) for s_ in range(8 * (half + 1)) for c_ in range(4)]
                lastc = {}
                for idx, (c_, s_) in enumerate(kbs):
                    q0 = qlo if c_ is None else max(qlo, s_ * 128)
                    for ch in range((q0 - qlo) // 512, 2):
                        lastc[ch] = idx
                for idx, (c_, s_) in enumerate(kbs):
                    jobs.append(dict(hd=hd, half=half, idx=idx, c_=c_, s_=s_, lastc=lastc, first=(idx == 0 and half == 0), last=(idx == len(kbs) - 1)))

        def emit_S(j, n):
            hd, half, c_, s_ = j["hd"], j["half"], j["c_"], j["s_"]
            fox = hd >= 8
            hh = hd % 8
            dk = 73 if fox else 96
            kt, tk = kT[hd % 2]
            qt, tq = qT[hd % 2]
            qlo = half * 1024
            meta = c_ is None
            kb = 64 if meta else c_ * 16 + s_
            q0 = qlo if meta else max(qlo, s_ * 128)
            r0 = q0 - qlo
            pt_, tpt_ = pT[n % len(pT)]
            ps_, tps_ = psS[n % 3]
            chunks = [(ch, max(r0, ch * 512), (ch + 1) * 512) for ch in range(r0 // 512, 2)]
            j.update(kb=kb, r0=r0, pt=pt_, tpt=tpt_, chunks=chunks)
            for ch, a0, a1 in chunks:
                S.op("pe", lambda a0=a0, a1=a1: nc.tensor.matmul(ps_[:, a0:a1], lhsT=kt[0:dk, kb * 128:(kb + 1) * 128], rhs=qt[0:dk, qlo + a0:qlo + a1],
                                                                 start=True, stop=True), r=[tk, tq], w=[tps_])
            if not meta and s_ * 128 >= qlo:
                mi = c_ * 2 + (s_ % 2)
                S.op("dve", lambda: nc.vector.tensor_tensor(out=ps_[:, r0:r0 + 128], in0=ps_[:, r0:r0 + 128], in1=mask[:, mi, :], op=ALU.add),
                     r=[tps_, T_mask], w=[tps_])
            if fox and not meta:
                bias = OFF[:, c_, s_, hh:hh + 1]
                S.op("act", lambda: nc.scalar.activation(out=pt_[:, r0:1024], in_=ps_[:, r0:1024], func=AF.Exp, bias=bias), r=[tps_, T_OFF], w=[tpt_])
            else:
                S.op("act", lambda: nc.scalar.activation(out=pt_[:, r0:1024], in_=ps_[:, r0:1024], func=AF.Exp), r=[tps_], w=[tpt_])

        def emit_PV(j):
            hd, half, idx, lastc = j["hd"], j["half"], j["idx"], j["lastc"]
            odd = hd % 2 == 1
            vt, tv_ = vS[hd % 2]
            lhs = vt[:, j["kb"], :]
            pt_, tpt_ = j["pt"], j["tpt"]
            for ch, a0, a1 in j["chunks"]:
                S.op("pe", lambda ch=ch, a0=a0, a1=a1: nc.tensor.matmul(psOT[:, a0:a1], lhsT=lhs, rhs=pt_[:, a0:a1], start=(idx == 0), stop=(lastc[ch] == idx)),
                     r=[tv_, tpt_], w=[T_pOT[ch]])
            if not j["last"]:
                return
            qlo = half * 1024
            on, ton = otn[(hd // 2) % 2]
            lo, hi_ = (64, 128) if odd else (0, 64)
            so, sh = (0, 64) if odd else (64, 128)
            for ch in range(2):
                S.op("dve", lambda ch=ch: nc.vector.tensor_copy(out=osb[:, qlo + ch * 512:qlo + (ch + 1) * 512], in_=psOT[:, ch * 512:(ch + 1) * 512]),
                     r=[T_pOT[ch]], w=[T_osb[half]])
            S.op("dve", lambda: nc.vector.reciprocal(out=rc[lo:hi_, qlo:qlo + 1024], in_=osb[so:sh, qlo:qlo + 1024]), r=[T_osb[half]], w=[T_rc])
            S.op("dve", lambda: nc.vector.tensor_tensor(out=on[lo:hi_, qlo:qlo + 1024], in0=osb[lo:hi_, qlo:qlo + 1024], in1=rc[lo:hi_, qlo:qlo + 1024], op=ALU.mult),
                 r=[T_osb[half], T_rc], w=[ton])
            if odd and half == 1:
                S.op("pool", lambda: nc.gpsimd.dma_start(out=ot_s[hd // 2, :, :], in_=on[:, :]), r=[ton], w=[Trk()], dma=True)
            if half == 1 and hd + 2 < 16:
                load_head(hd + 2)
            if half == 1 and hd in (0, 2, 4):
                cast_phase_c(hd // 2, None)

        load_head(1)
        LA = 2
        for n in range(len(jobs) + LA):
            if n < len(jobs):
                emit_S(jobs[n], n)
            if n >= LA:
                emit_PV(jobs[n - LA])
        S.barrier()

    with ExitStack() as sc:
        PSC = make_psum(sc, "c")
        BC = make_ffn_bufs(sc, "c")
        wout = sb("wout", [128, 8, D], BF16, sc)
        T_wout = Trk()
        S.op("sp", lambda: nc.sync.dma_start(out=wout[:], in_=wout_s[:, :].rearrange("(k p) c -> p k c", p=128)), r=cast_ops["wout"], w=[T_wout], dma=True)
        load_wd(BC, w2d_s, "w2d")
        hb2 = [sb("hb2_%d" % i, [128, 4, D], F32, sc) for i in range(2)]
        T_h2 = [[Trk() for _ in range(4)] for _ in range(2)]
        otg = [(sb("otg%d" % i, [128, 8, 512], BF16, sc), Trk()) for i in range(2)]
        for g in range(4):
            hbuf2 = hb2[g % 2]
            og, tog = otg[g % 2]
            blocks = []
            S.op("sp", lambda g=g, og=og: nc.sync.dma_start(out=og[:], in_=ot_s[:, :, g * 512:(g + 1) * 512].rearrange("k p t -> p k t")), w=[tog], dma=True)
            for j in range(4):
                slot = g * 4 + j
                S.op("sp", lambda j=j, slot=slot, hbuf2=hbuf2: nc.sync.dma_start(out=hbuf2[:, j, :], in_=h1_s[slot + 1, :, :]), w=[T_h2[g % 2][j]], dma=True)
                blocks.append((hbuf2[:, j, :], T_h2[g % 2][j]))
            for j, (h, th) in enumerate(blocks):
                for half in range(2):
                    po, tpo = PSC["O"][half]
                    for k in range(8):
                        S.op("pe", lambda k=k, j=j, half=half, po=po, og=og: nc.tensor.matmul(po[:, :], lhsT=og[:, k, j * 128:(j + 1) * 128],
                                                                                               rhs=wout[:, k, half * 512:(half + 1) * 512], start=(k == 0), stop=(k == 7)),
                             r=[tog, T_wout], w=[tpo])
                    S.op("dve", lambda h=h, half=half, po=po: nc.vector.tensor_tensor(out=h[:, half * 512:(half + 1) * 512], in0=po[:, :],
                                                                                      in1=h[:, half * 512:(half + 1) * 512], op=ALU.add),
                         r=[tpo, th], w=[th])
            ffn(BC, PSC, blocks, G_FFN2, w2g_s, w2u_s, "w2g", "w2u")
            for j, (h, th) in enumerate(blocks):
                S.op("pool", lambda h=h, slot=g * 4 + j: nc.gpsimd.dma_start(out=y_d[slot, :, :], in_=h), r=[th], w=[Trk()], dma=True)
    S.emit()
    es.close()
    return nc


def _gb(c, s):
    return 4 * s + 1 + c if s % 2 == 0 else 4 * s + 4 - c


_NC = None


def kernel(x, meta_tokens, g_ffn1, w1_gate, w1_up, w1_down, g_mix, w_in, g_cq, w_uq, g_ckv, w_ukv, g_q_mla, g_k_mla,
           b_forget, g_q_fox, g_k_fox, w_out, g_ffn2, w2_gate, w2_up, w2_down):
    global _NC
    f32 = np.float32
    x = np.asarray(x, f32)
    B_, SEQ, _ = x.shape
    nc = build()
    meta_blk = np.zeros((128, D), f32)
    meta_blk[112:] = np.asarray(meta_tokens, f32)
    half = 16
    inv_freq = 1.0 / (10000.0 ** (np.arange(half, dtype=np.float64) / float(half)))
    in_maps = []
    common = {
        "meta": meta_blk,
        "w1g": np.ascontiguousarray(np.asarray(w1_gate, f32)[0]), "w1u": np.ascontiguousarray(np.asarray(w1_up, f32)[0]),
        "w1d": np.ascontiguousarray(np.asarray(w1_down, f32)[0]),
        "w2g": np.ascontiguousarray(np.asarray(w2_gate, f32)[0]), "w2u": np.ascontiguousarray(np.asarray(w2_up, f32)[0]),
        "w2d": np.ascontiguousarray(np.asarray(w2_down, f32)[0]),
        "w_in": np.ascontiguousarray(np.asarray(w_in, f32)[0]), "w_uq": np.ascontiguousarray(np.asarray(w_uq, f32)[0]),
        "w_ukv": np.ascontiguousarray(np.asarray(w_ukv, f32)[0]), "w_out": np.ascontiguousarray(np.asarray(w_out, f32)[0]),
    }
    gparts = [g_ffn1, g_mix, g_ffn2, g_cq, g_ckv, g_q_mla, g_k_mla, g_q_fox, g_k_fox, b_forget]
    grow = np.concatenate([np.asarray(a, f32)[0] for a in gparts])
    tri_m = (np.arange(128)[:, None] <= np.arange(128)[None, :]).astype(f32)
    for core in range(8):
        b, c = core // 4, core % 4
        gbs = [_gb(c, s) for s in range(NSLOT)]
        xl = np.stack([x[b, (g - 1) * 128:g * 128, :] for g in gbs])
        sel = np.zeros(4, f32)
        sel[c] = 1.0
        gvec = np.tile(np.concatenate([grow, sel])[None, :], (128, 1)).astype(f32)
        cs = np.zeros((128, 17, 32), f32)
        for lb in range(17):
            g = 0 if lb == 0 else gbs[lb - 1]
            idx = g * 128 + np.arange(128)
            pos = np.maximum(idx - 112, 0).astype(np.float64)
            ang = pos[:, None] * inv_freq[None, :]
            cs[:, lb, 0:16] = np.cos(ang).astype(f32)
            cs[:, lb, 16:32] = np.sin(ang).astype(f32)
        masks = np.zeros((128, 8, 128), f32)
        for cp in range(4):
            for par in range(2):
                if cp == c:
                    m = (tri_m - 1.0) * NEGM
                elif (par == 0 and cp < c) or (par == 1 and cp > c):
                    m = np.zeros((128, 128), f32)
                else:
                    m = np.full((128, 128), -NEGM, f32)
                masks[:, cp * 2 + par, :] = m
        d = dict(common)
        d.update({"x": np.ascontiguousarray(xl), "gvec": gvec, "cossin": cs, "masks": masks})
        in_maps.append(d)
    res = run_bass_kernel_spmd(nc, in_maps, core_ids=list(range(8)))
    if os.environ.get("KDEBUG"):
        global DBG
        DBG = res.results
    out = np.zeros((B_, SEQ, D), f32)
    for core in range(8):
        b, c = core // 4, core % 4
        y = np.asarray(res.results[core]["y"], f32)
        for s in range(NSLOT):
            g = _gb(c, s)
            out[b, (g - 1) * 128:g * 128, :] = y[s]
    return out
```
